# Optimizing a Trainium2 kernel written in Bass

```python
import math
import jax, jax.numpy as jnp
from jax import lax
import numpy as np

D_MODEL = 1024
BATCH = 8
SEQ = 2048
DEPTH = 1

CHUNK = 64
Q_BLOCK = 128

SSD_HEADS = 8
SSD_HEAD_DIM = 64
D_SSD = SSD_HEADS * SSD_HEAD_DIM
SSD_GROUPS = 2
SSD_STATE = 128
CONV_WIDTH = 4
D_CONV = D_SSD + 2 * SSD_GROUPS * SSD_STATE

SB_HEADS = 8
SB_HEAD_DIM = 64
D_SB = SB_HEADS * SB_HEAD_DIM

D_MIX = D_SSD + D_SB
D_IN_PROJ = D_SSD + D_CONV + SSD_HEADS + 3 * D_SB

D_FF = int(math.ceil(8 * D_MODEL / 3 / 256) * 256)
EPS = 1e-6

kernel_name = "hymba_ssd_stickbreaking_sandwich_block"


def rms_norm(x, g):
    xf = x.astype(jnp.float32)
    y = xf * lax.rsqrt(jnp.mean(xf * xf, axis=-1, keepdims=True) + EPS)
    return (y * g.astype(jnp.float32)).astype(x.dtype)


def causal_depthwise_conv(u, w, b):
    k = w.shape[0]
    out = lax.conv_general_dilated(
        u, w[:, None, :].astype(u.dtype), window_strides=(1,), padding=[(k - 1, 0)],
        dimension_numbers=("NWC", "WIO", "NWC"), feature_group_count=u.shape[-1])
    return out + b.astype(u.dtype)


def segsum(a):
    t = a.shape[-1]
    cs = jnp.cumsum(a, axis=-1)
    diff = cs[..., :, None] - cs[..., None, :]
    mask = jnp.tril(jnp.ones((t, t), dtype=bool))
    return jnp.where(mask, diff, -jnp.inf)


def ssd_scan(xs, dt, a, bm, cm):
    b_, seq, n_heads, p = xs.shape
    g, n = bm.shape[-2:]
    r = n_heads // g
    c = seq // CHUNK
    x = (xs * dt[..., None]).reshape(b_, c, CHUNK, g, r, p)
    adt = (dt * a).reshape(b_, c, CHUNK, g, r).transpose(0, 3, 4, 1, 2)
    bc = bm.reshape(b_, c, CHUNK, g, n)
    cc = cm.reshape(b_, c, CHUNK, g, n)
    a_cs = jnp.cumsum(adt, axis=-1)
    decay_in = jnp.exp(segsum(adt))
    cb = jnp.einsum("bclgn,bcsgn->bgcls", cc, bc)
    y_diag = jnp.einsum("bgcls,bgrcls,bcsgrp->bclgrp", cb, decay_in, x)
    decay_states = jnp.exp(a_cs[..., -1:] - a_cs)
    chunk_states = jnp.einsum("bclgn,bgrcl,bclgrp->bcgrpn", bc, decay_states, x)
    chunk_decay = jnp.exp(a_cs[..., -1])

    def step(state, inp):
        s_c, d_c = inp
        return state * d_c[..., None, None] + s_c, state

    init = jnp.zeros_like(chunk_states[:, 0])
    _, prev = lax.scan(step, init, (jnp.moveaxis(chunk_states, 1, 0), jnp.moveaxis(chunk_decay, -1, 0)))
    prev = jnp.moveaxis(prev, 0, 1)
    y_off = jnp.einsum("bclgn,bcgrpn,bgrcl->bclgrp", cc, prev, jnp.exp(a_cs))
    return (y_diag + y_off).reshape(b_, seq, n_heads, p)


def stick_breaking_attention(q, k, v):
    seq, d = q.shape[2], q.shape[3]
    scale = 1.0 / math.sqrt(d)
    outs = []
    for i in range(seq // Q_BLOCK):
        start = i * Q_BLOCK
        end = start + Q_BLOCK
        kb = k[:, :, :end]
        vb = v[:, :, :end]
        z = jnp.einsum("bhqd,bhkd->bhqk", q[:, :, start:end], kb).astype(jnp.float32) * scale
        t_idx = start + jnp.arange(Q_BLOCK)
        s_idx = jnp.arange(end)
        strict = s_idx[None, :] < t_idx[:, None]
        log_1mb = jnp.where(strict, jax.nn.log_sigmoid(-z), 0.0)
        after = lax.cumsum(log_1mb, axis=3, reverse=True) - log_1mb
        w = jnp.where(strict, jnp.exp(jax.nn.log_sigmoid(z) + after), 0.0)
        outs.append(jnp.einsum("bhqk,bhkd->bhqd", w.astype(vb.dtype), vb))
    return jnp.concatenate(outs, axis=2)


def setup_inputs(seed: int = 0) -> dict:
    key = jax.random.key(seed)
    ks = jax.random.split(key, 20)
    f32 = jnp.float32

    def gain(k, n):
        return 1.0 + 0.02 * jax.random.normal(k, (DEPTH, n), f32)

    x = jax.random.normal(ks[0], (BATCH, SEQ, D_MODEL), f32)
    w_in = jax.random.normal(ks[1], (DEPTH, D_MODEL, D_IN_PROJ), f32) * D_MODEL ** -0.5
    conv_w = jax.random.normal(ks[2], (DEPTH, CONV_WIDTH, D_CONV), f32) * CONV_WIDTH ** -0.5
    conv_b = 0.02 * jax.random.normal(ks[3], (DEPTH, D_CONV), f32)
    dt0 = jnp.exp(jax.random.uniform(ks[4], (DEPTH, SSD_HEADS), f32, math.log(1e-3), math.log(1e-1)))
    dt_bias = dt0 + jnp.log(-jnp.expm1(-dt0))
    a_log = jnp.log(jax.random.uniform(ks[5], (DEPTH, SSD_HEADS), f32, 1.0, 16.0))
    d_skip = 1.0 + 0.1 * jax.random.normal(ks[6], (DEPTH, SSD_HEADS), f32)
    w_out = jax.random.normal(ks[7], (DEPTH, D_MIX, D_MODEL), f32) * D_MIX ** -0.5
    w_gate = jax.random.normal(ks[8], (DEPTH, D_MODEL, D_FF), f32) * D_MODEL ** -0.5
    w_up = jax.random.normal(ks[9], (DEPTH, D_MODEL, D_FF), f32) * D_MODEL ** -0.5
    w_down = jax.random.normal(ks[10], (DEPTH, D_FF, D_MODEL), f32) * D_FF ** -0.5
    return {
        "x": x,
        "pre_mix_gain": gain(ks[11], D_MODEL),
        "w_in": w_in,
        "conv_w": conv_w,
        "conv_b": conv_b,
        "dt_bias": dt_bias,
        "a_log": a_log,
        "d_skip": d_skip,
        "ssd_norm_gain": gain(ks[12], D_SSD),
        "sb_norm_gain": gain(ks[13], D_SB),
        "w_out": w_out,
        "post_mix_gain": gain(ks[14], D_MODEL),
        "pre_ffn_gain": gain(ks[15], D_MODEL),
        "w_gate": w_gate,
        "w_up": w_up,
        "w_down": w_down,
        "post_ffn_gain": gain(ks[16], D_MODEL),
    }


def reference(x, pre_mix_gain, w_in, conv_w, conv_b, dt_bias, a_log, d_skip, ssd_norm_gain,
              sb_norm_gain, w_out, post_mix_gain, pre_ffn_gain, w_gate, w_up, w_down, post_ffn_gain):
    b_, seq, _ = x.shape
    splits = np.cumsum([D_SSD, D_CONV, SSD_HEADS, D_SB, D_SB]).tolist()
    for layer in range(DEPTH):
        h = rms_norm(x, pre_mix_gain[layer])
        proj = h @ w_in[layer]
        z, xbc, dt_raw, q, k, v = jnp.split(proj, splits, axis=-1)

        xbc = jax.nn.silu(causal_depthwise_conv(xbc, conv_w[layer], conv_b[layer]))
        xs, bm, cm = jnp.split(xbc, [D_SSD, D_SSD + SSD_GROUPS * SSD_STATE], axis=-1)
        xs = xs.astype(jnp.float32).reshape(b_, seq, SSD_HEADS, SSD_HEAD_DIM)
        bm = bm.astype(jnp.float32).reshape(b_, seq, SSD_GROUPS, SSD_STATE)
        cm = cm.astype(jnp.float32).reshape(b_, seq, SSD_GROUPS, SSD_STATE)
        dt = jax.nn.softplus(dt_raw.astype(jnp.float32) + dt_bias[layer].astype(jnp.float32))
        a = -jnp.exp(a_log[layer].astype(jnp.float32))
        y = ssd_scan(xs, dt, a, bm, cm) + d_skip[layer].astype(jnp.float32)[:, None] * xs
        y = y.reshape(b_, seq, D_SSD) * jax.nn.silu(z.astype(jnp.float32))
        y = rms_norm(y.reshape(b_, seq, SSD_GROUPS, D_SSD // SSD_GROUPS),
                     ssd_norm_gain[layer].reshape(SSD_GROUPS, D_SSD // SSD_GROUPS))
        y_ssd = y.reshape(b_, seq, D_SSD).astype(x.dtype)

        def heads(t):
            return t.reshape(b_, seq, SB_HEADS, SB_HEAD_DIM).transpose(0, 2, 1, 3)
        o = stick_breaking_attention(heads(q), heads(k), heads(v)).transpose(0, 2, 1, 3)
        o = rms_norm(o, sb_norm_gain[layer].reshape(SB_HEADS, SB_HEAD_DIM))
        y_sb = o.reshape(b_, seq, D_SB).astype(x.dtype)

        mix = jnp.concatenate([y_ssd, y_sb], axis=-1) @ w_out[layer]
        x = x + rms_norm(mix, post_mix_gain[layer])

        h = rms_norm(x, pre_ffn_gain[layer])
        f = (jax.nn.silu(h @ w_gate[layer]) * (h @ w_up[layer])) @ w_down[layer]
        x = x + rms_norm(f, post_ffn_gain[layer])
    return x
```

```python
import numpy as np
import concourse.bass as bass
import concourse.mybir as mybir
from concourse.bass_utils import run_bass_kernel_spmd
from contextlib import ExitStack

F32 = mybir.dt.float32
BF16 = mybir.dt.bfloat16
U8 = mybir.dt.uint8
AF = mybir.ActivationFunctionType
ALU = mybir.AluOpType

L = 2048
D = 1024
NT = 16
DIN = 3080
DFF = 2816
NFC = 22
EPS = 1e-6
NPH = 5
SBUF_BYTES = 212000


class Eng:
    def __init__(self, K, eng, name):
        self.K = K
        self.e = eng
        self.name = name
        self.sem = K.new_sem("prog_" + name)
        self.cnt = 0
        self.seen = {}

    def wait(self, tok):
        if tok is None:
            return
        sem, val = tok
        key = sem.num
        if self.seen.get(key, 0) >= val:
            return
        self.e.wait_ge(sem, val)
        self.seen[key] = val

    def mark(self, ins):
        self.cnt += 1
        ins.then_inc(self.sem, 1)
        return (self.sem, self.cnt)

    def last(self):
        return (self.sem, self.cnt) if self.cnt else None


class Buf:
    __slots__ = ("w", "r", "name")

    def __init__(self, name=""):
        self.w = None
        self.r = {}
        self.name = name


class DmaSlot:
    def __init__(self, K, name):
        self.sem = K.new_sem("dma_" + name)
        self.cnt = 0


class Builder:
    def __init__(self, nc, es):
        self.nc = nc
        self.es = es
        self.nsem = 0
        self.PE = Eng(self, nc.tensor, "pe")
        self.ACT = Eng(self, nc.scalar, "act")
        self.DVE = Eng(self, nc.vector, "dve")
        self.POOL = Eng(self, nc.gpsimd, "pool")
        self.SP = Eng(self, nc.sync, "sp")
        self.engs = [self.PE, self.ACT, self.DVE, self.POOL, self.SP]
        self.allocs = []
        self.dma_toks = []

    def new_sem(self, name):
        self.nsem += 1
        return self.es.enter_context(self.nc.semaphore(name))

    def do(self, E, thunks, reads=(), writes=()):
        for b in reads:
            E.wait(b.w)
        for b in writes:
            E.wait(b.w)
            for t in b.r.values():
                E.wait(t)
        if callable(thunks):
            thunks = [thunks]
        ins = None
        for th in thunks:
            ins = th()
        tok = E.mark(ins)
        for b in reads:
            b.r[tok[0].num] = tok
        for b in writes:
            b.w = tok
            b.r = {}
        return tok

    def dma(self, E, slot, out, in_, reads=(), writes=()):
        for b in reads:
            E.wait(b.w)
        for b in writes:
            E.wait(b.w)
            for t in b.r.values():
                E.wait(t)
        E.e.dma_start(out=out, in_=in_).then_inc(slot.sem, 16)
        slot.cnt += 16
        tok = (slot.sem, slot.cnt)
        for b in reads:
            b.r[tok[0].num] = tok
        for b in writes:
            b.w = tok
            b.r = {}
        return tok

    def barrier(self, skip=()):
        toks = [e.last() for e in self.engs]
        for e in self.engs:
            if e in skip:
                continue
            for t in toks:
                if t is not None and t[0].num != e.sem.num:
                    e.wait(t)

    def alloc(self, name, shape, dt, p0, p1):
        esz = 4 if dt == F32 else 2
        n = int(np.prod(shape[1:])) * esz
        n = (n + 63) // 64 * 64
        self.allocs.append(dict(name=name, shape=shape, dt=dt, p0=p0, p1=p1, n=n, off=None))
        return len(self.allocs) - 1

    def place(self):
        keys = [lambda a: (-a["n"],),
                lambda a: (-(a["p1"] - a["p0"]), -a["n"]),
                lambda a: (-a["n"] * (a["p1"] - a["p0"] + 1),),
                lambda a: (a["p0"], -a["n"]),
                lambda a: (-a["p1"], -a["n"])]
        err = None
        for key in keys:
            try:
                self._place(key)
                return
            except RuntimeError as e:
                err = e
        raise err

    def _place(self, key):
        for a in self.allocs:
            a["off"] = None
        order = sorted(range(len(self.allocs)), key=lambda i: key(self.allocs[i]))
        placed = []
        for i in order:
            a = self.allocs[i]
            cands = [0] + sorted(b["off"] + b["n"] for b in placed)
            for off in cands:
                ok = True
                for b in placed:
                    if a["p0"] <= b["p1"] and b["p0"] <= a["p1"]:
                        if off < b["off"] + b["n"] and b["off"] < off + a["n"]:
                            ok = False
                            break
                if ok:
                    a["off"] = off
                    break
            placed.append(a)
            if a["off"] + a["n"] > SBUF_BYTES:
                raise RuntimeError("SBUF overflow placing %s (%d + %d)" % (a["name"], a["off"], a["n"]))

    def ap(self, big, idx):
        a = self.allocs[idx]
        shape = a["shape"]
        v = big[0:shape[0], a["off"]:a["off"] + int(np.prod(shape[1:])) * (4 if a["dt"] == F32 else 2)]
        if a["dt"] != U8:
            v = v.bitcast(a["dt"])
        if len(shape) == 3:
            v = v.rearrange("p (a b) -> p a b", b=shape[2])
        elif len(shape) == 4:
            v = v.rearrange("p (a b c) -> p a b c", b=shape[2], c=shape[3])
        return v


def build(dbg=None):
    nc = bass.Bass("TRN2", target_bir_lowering=False)

    def din(name, shape, dt=F32):
        return nc.dram_tensor(name, shape, dt, kind="ExternalInput").ap()

    def dout(name, shape, dt=F32):
        return nc.dram_tensor(name, shape, dt, kind="ExternalOutput").ap()

    x_d = din("x", [L, D])
    win_d = din("w_in", [128, 8, DIN])
    wout_d = din("w_out", [128, 8, D])
    wgu_d = din("wgu", [NFC, 128, 2 * 8 * 128])
    wd_d = din("wd", [128, NFC, D])
    gains_d = din("gains", [128, 4, D])
    convp_d = din("convp", [128, 8, 5])
    smallp_d = din("smallp", [128, 3, 128])
    gssd_d = din("gssd", [128, 512])
    sbg_d = din("sbg", [128, 4])
    out_d = dout("out", [L, D])

    dbg_outs = {}
    if dbg == "A1":
        dbg_outs["d_hT"] = dout("d_hT", [128, 8 * L], BF16)
        dbg_outs["d_xs"] = dout("d_xs", [128, NT * 512], F32)
        dbg_outs["d_bmT"] = dout("d_bmT", [128, 2 * L], BF16)
        dbg_outs["d_cmT"] = dout("d_cmT", [128, 2 * L], BF16)
        dbg_outs["d_bmtm"] = dout("d_bmtm", [128, NT * 256], BF16)
    if dbg in ("A2", "A4"):
        dbg_outs["d_ycat"] = dout("d_ycat", [128, 8 * L], BF16)
    if dbg == "B":
        dbg_outs["d_x1"] = dout("d_x1", [128, NT * D], F32)
        dbg_outs["d_h2T"] = dout("d_h2T", [128, 8 * L], BF16)

    es = ExitStack()
    with es:
        big = es.enter_context(nc.sbuf_tensor("big", [128, SBUF_BYTES], U8))
        psum = es.enter_context(nc.psum_tensor("psum", [128, 4096], F32))
        es.enter_context(nc.Block())
        K = Builder(nc, es)
        PE, ACT, DVE, POOL, SP = K.PE, K.ACT, K.DVE, K.POOL, K.SP
        bank = [psum[:, b * 512:(b + 1) * 512] for b in range(8)]
        bankB = [Buf("bank%d" % b) for b in range(8)]

        A = {}

        def al(name, shape, dt, p0, p1):
            A[name] = K.alloc(name, shape, dt, p0, p1)

        al("ident", [128, 128], F32, 0, 4)
        al("onesf", [128, 128], F32, 0, 1)
        al("UT", [128, 128], F32, 0, 1)
        al("LTs", [128, 128], F32, 0, 1)
        al("BD", [128, 128], F32, 0, 2)
        al("triI", [128, 128], BF16, 0, 2)
        al("triC", [128, 128], BF16, 0, 2)
        al("mks", [128, 128], BF16, 0, 2)
        al("zerob", [128, 128], BF16, 0, 2)
        al("identb", [128, 128], BF16, 0, 3)
        al("negm", [128, 128], BF16, 0, 2)
        al("convp", [128, 8, 5], F32, 0, 0)
        al("smallp", [128, 3, 128], F32, 0, 1)
        al("gssd", [128, 512], F32, 0, 1)
        al("sbg", [128, 4], F32, 0, 2)
        al("stat", [128, 256], F32, 0, 4)
        al("hT", [128, 8, L], BF16, 0, 2)
        al("ycS", [128, 4, L], BF16, 1, 3)
        al("ycA", [128, 4, L], BF16, 2, 3)
        al("xs_tm", [128, NT, 512], F32, 0, 1)
        al("bmT", [128, 2, L], BF16, 0, 1)
        al("cmT", [128, 2, L], BF16, 0, 1)
        al("bm_tm", [128, NT, 256], BF16, 0, 1)
        al("g_pre", [128, D], F32, 0, 0)
        al("xin0", [128, D], F32, 0, 0)
        al("xin1", [128, D], F32, 0, 0)
        al("xin2", [128, D], F32, 0, 0)
        al("hn0", [128, D], BF16, 0, 0)
        al("hn1", [128, D], BF16, 0, 0)
        al("cs0", [128, L + 16], F32, 0, 0)
        al("cs1", [128, L + 16], F32, 0, 0)
        al("acc0", [128, L], F32, 0, 0)
        al("acc1", [128, L], F32, 0, 0)
        al("sil0", [128, L], F32, 0, 0)
        al("sil1", [128, L], F32, 0, 0)
        al("Wx0", [128, 8, 512], BF16, 0, 0)
        al("Wx1", [128, 8, 512], BF16, 0, 0)
        al("Wz", [128, 8, 512], BF16, 0, 1)
        al("Wdt", [128, 8, 8], BF16, 0, 1)

        al("dtall", [128, 128], F32, 0, 1)
        al("adtall", [128, 128], F32, 0, 1)
        al("eaT", [128, 128], F32, 0, 1)
        al("scx", [128, 384], F32, 0, 1)
        al("dtds", [128, 128], F32, 0, 1)
        for b_ in range(2):
            al("Ah%d" % b_, [128, 8, 128], F32, 1, 1)
            al("dec%d" % b_, [128, 8, 128], F32, 1, 1)
            al("cbm%d" % b_, [128, 2, 128], F32, 1, 1)
            al("MT%d" % b_, [128, 8, 128], BF16, 1, 1)
            al("xdt%d" % b_, [128, 512], BF16, 1, 1)
            al("xds%d" % b_, [128, 512], BF16, 1, 1)
            al("stbf%d" % b_, [128, 512], BF16, 1, 1)
            al("yn%d" % b_, [128, 512], BF16, 1, 1)
        for nm in ("ez", "t1", "t2", "t3", "zz", "yg", "state"):
            al(nm, [128, 512], F32, 1, 1)

        al("qpad", [128, 4, 2, L], BF16, 2, 2)
        al("kT", [128, 4, L], BF16, 2, 2)
        al("v_tm", [128, NT, 512], BF16, 1, 2)
        al("Wq", [128, 8, 512], BF16, 1, 2)
        al("Wk", [128, 8, 512], BF16, 2, 2)
        al("Wv", [128, 8, 512], BF16, 1, 2)
        al("E2", [128, 2, 512], F32, 2, 2)
        for sl_ in range(2):
            al("E%d" % sl_, [128, 2, 512], F32, 2, 2)
            al("u%d" % sl_, [128, 2, 512], F32, 2, 2)
            al("sp%d" % sl_, [128, 2, 512], BF16, 2, 2)
            al("w%d" % sl_, [128, 2, 512], BF16, 2, 2)
        al("osq", [128, 512], F32, 2, 2)
        al("osave", [128, 512], F32, 2, 2)
        al("rs", [128, 512], F32, 2, 2)
        al("Wout", [128, 8, D], BF16, 2, 3)

        al("x1", [128, NT, D], F32, 3, 4)
        al("h2T", [128, 8, L], BF16, 3, 4)
        al("xb0", [128, D], F32, 3, 3)
        al("xb1", [128, D], F32, 3, 3)
        al("hn20", [128, D], BF16, 3, 3)
        al("hn21", [128, D], BF16, 3, 3)
        al("gpm", [128, D], F32, 3, 3)
        al("gpf", [128, D], F32, 3, 3)
        al("junkb", [128, D], BF16, 3, 3)
        al("WdA", [128, 11, D], BF16, 3, 4)
        al("WdB", [128, 11, D], BF16, 4, 4)
        al("GT", [128, NFC, 1024], BF16, 4, 4)
        al("wgu0", [128, 2, 8, 128], BF16, 3, 4)
        al("wgu1", [128, 2, 8, 128], BF16, 3, 4)
        al("sg0", [128, 512], F32, 4, 4)
        al("sg1", [128, 512], F32, 4, 4)
        al("ft0", [128, 512], F32, 4, 4)
        al("ft1", [128, 512], F32, 4, 4)
        al("gff", [128, D], F32, 4, 4)

        K.place()
        T = {k: K.ap(big, v) for k, v in A.items()}
        used = max(a["off"] + a["n"] for a in K.allocs)

        cB = Buf("consts")
        small_slot = DmaSlot(K, "small")
        gpreB = Buf("gpre")
        pB = Buf("params")
        K.dma(SP, DmaSlot(K, "gpre"), T["g_pre"], gains_d[:, 0, :], writes=[gpreB])

        WxB = [Buf("Wx0"), Buf("Wx1")]
        WzB = Buf("Wz")
        wslots = [DmaSlot(K, "w%d" % i) for i in range(4)]
        WdtB = Buf("Wdt")
        K.dma(POOL, wslots[0], T["Wx0"], win_d[:, :, 512:1024], writes=[WxB[0]])
        K.dma(POOL, wslots[3], T["Wdt"], win_d[:, :, 1536:1544], writes=[WdtB])

        onesB = Buf("ones")
        K.do(POOL, lambda: nc.gpsimd.memset(T["onesf"], 1.0), writes=[onesB])
        K.do(POOL, lambda: nc.gpsimd.memset(T["zerob"], 0.0), writes=[cB])
        K.do(POOL, lambda: nc.gpsimd.memset(T["cs0"][:, 0:3], 0.0), writes=[cB])
        K.do(POOL, lambda: nc.gpsimd.memset(T["cs1"][:, 0:3], 0.0), writes=[cB])

        def asel(name, pattern, cmp, base, cm, src="onesf", fill=0.0):
            K.do(POOL, lambda: nc.gpsimd.affine_select(out=T[name], in_=T[src], pattern=pattern, compare_op=cmp,
                                                        fill=fill, base=base, channel_multiplier=cm),
                 reads=[onesB], writes=[cB])

        asel("ident", [[1, 128]], ALU.is_equal, 0, -1)
        asel("UT", [[1, 128]], ALU.is_ge, 0, -1)
        asel("LTs", [[-1, 128]], ALU.is_gt, 0, 1)
        asel("triI", [[-1, 128]], ALU.is_ge, 0, 1)
        asel("triC", [[1, 128]], ALU.is_gt, 0, -1)
        asel("mks", [[1, 128]], ALU.is_gt, 0, -1)
        asel("identb", [[1, 128]], ALU.is_equal, 0, -1)
        asel("negm", [[1, 128]], ALU.is_gt, 0, -1, src="zerob", fill=-30000.0)
        bdB = Buf("bd")
        K.do(POOL, lambda: nc.gpsimd.memset(T["BD"], 0.0), writes=[bdB])
        K.do(POOL, lambda: nc.gpsimd.memset(T["BD"][0:64, 0:64], 1.0 / 64), writes=[bdB])
        K.do(POOL, lambda: nc.gpsimd.memset(T["BD"][64:128, 64:128], 1.0 / 64), writes=[bdB])


        ident = T["ident"]
        hT = T["hT"]
        xin = [T["xin0"], T["xin1"], T["xin2"]]
        hn = hn_a0 = [T["hn0"], T["hn1"]]
        xinB = [Buf("xin0"), Buf("xin1"), Buf("xin2")]
        hnB = hnB_a0 = [Buf("hn0"), Buf("hn1")]
        xslot = [DmaSlot(K, "x0"), DmaSlot(K, "x1"), DmaSlot(K, "x2")]
        stat = T["stat"]
        hTB = [[Buf("hT%d_%d" % (i, h)) for h in range(2)] for i in range(NT)]
        statB = [Buf("stat%d" % i) for i in range(NT)]

        def rms_tile(i, src_ap, srcB, s, gain_ap, gainB, junk_ap, junkB, n=D, hn=None, hnB=None):
            hn = hn if hn is not None else hn_a0
            hnB = hnB if hnB is not None else hnB_a0
            sb = statB[i % NT]
            c = (i % NT)
            K.do(ACT, lambda: nc.scalar.activation(out=junk_ap, in_=src_ap, func=AF.Square,
                                                   accum_out=stat[:, c:c + 1]),
                 reads=[srcB], writes=[junkB, sb])
            K.do(ACT, lambda: nc.scalar.activation(out=stat[:, 16 + c:17 + c], in_=stat[:, c:c + 1], func=AF.Ln,
                                                   bias=EPS, scale=1.0 / n), writes=[sb])
            K.do(ACT, lambda: nc.scalar.activation(out=stat[:, 32 + c:33 + c], in_=stat[:, 16 + c:17 + c],
                                                   func=AF.Exp, scale=-0.5), writes=[sb])
            K.do(DVE, lambda: nc.vector.scalar_tensor_tensor(out=hn[s], in0=src_ap, scalar=stat[:, 32 + c:33 + c],
                                                             in1=gain_ap, op0=ALU.mult, op1=ALU.mult),
                 reads=[srcB, sb, gainB], writes=[hnB[s]])

        def transpose_to(i, s, dstT, dstB, bk0, hn=None, hnB=None, all_act=False, part="both"):
            hn = hn if hn is not None else hn_a0
            hnB = hnB if hnB is not None else hnB_a0
            for hb in range(2):
                bk = bk0 + hb
                bkb = bank[bk].bitcast(BF16)
                if part in ("both", "pe"):
                    K.do(PE, [(lambda c=c: nc.tensor.transpose(bkb[:, (c % 4) * 128:(c % 4 + 1) * 128],
                                                               hn[s][:, c * 128:(c + 1) * 128], T["identb"]))
                              for c in range(4 * hb, 4 * hb + 4)],
                         reads=[hnB[s], cB], writes=[bankB[bk]])
                if part in ("both", "evac"):
                    src = bkb[:, 0:512].rearrange("p (c t) -> p c t", t=128)
                    dst = dstT[:, 4 * hb:4 * hb + 4, i * 128:(i + 1) * 128]
                    if hb == 0 or all_act:
                        K.do(ACT, lambda: nc.scalar.copy(out=dst, in_=src), writes=[bankB[bk], dstB[i][hb]])
                    else:
                        K.do(DVE, lambda: nc.vector.tensor_copy(out=dst, in_=src), writes=[bankB[bk], dstB[i][hb]])

        def A0_load(i):
            s3 = i % 3
            K.dma(SP, xslot[s3], xin[s3], x_d[i * 128:(i + 1) * 128, :], writes=[xinB[s3]])

        def A0_front(i):
            s = i % 2
            s3 = i % 3
            if i + 2 < NT:
                A0_load(i + 2)
            if i == 1:
                for nm, src in (("convp", convp_d[:, :, :]), ("smallp", smallp_d[:, :, :]), ("gssd", gssd_d[:, :]),
                                ("sbg", sbg_d[:, :])):
                    K.dma(SP, small_slot, T[nm], src)
                pB.w = (small_slot.sem, small_slot.cnt)
            rms_tile(i, xin[s3], xinB[s3], s, T["g_pre"], gpreB, hn[s], hnB[s])

        smallp = T["smallp"]
        dtall, adtall, eaT, scx, dtds = T["dtall"], T["adtall"], T["eaT"], T["scx"], T["dtds"]
        UT, LTs, onesf = T["UT"], T["LTs"], T["onesf"]
        Wz, Wdt = T["Wz"], T["Wdt"]
        preB = Buf("pre")

        def dt_proj(i):
            K.do(PE, [(lambda c=c: nc.tensor.matmul(bank[7][:, i * 8:(i + 1) * 8], lhsT=hT[:, c, i * 128:(i + 1) * 128],
                                                    rhs=Wdt[:, c, :], start=(c == 0), stop=(c == 7))) for c in range(8)],
                 reads=[WdtB, hTB[i][0], hTB[i][1]], writes=[bankB[7]])

        def prepass1():
            K.do(DVE, lambda: nc.vector.tensor_tensor(out=dtall, in0=bank[7][:, 0:128], in1=smallp[:, 0, :], op=ALU.add),
                 reads=[cB, pB], writes=[bankB[7], preB])
            K.do(ACT, lambda: nc.scalar.activation(out=dtall, in_=dtall, func=AF.Exp), writes=[preB])
            K.do(ACT, lambda: nc.scalar.activation(out=dtall, in_=dtall, func=AF.Ln, bias=1.0), writes=[preB])
            K.do(ACT, lambda: nc.scalar.activation(out=eaT, in_=smallp[:, 1, :], func=AF.Exp), reads=[cB, pB], writes=[preB])
            K.do(DVE, lambda: nc.vector.scalar_tensor_tensor(out=adtall, in0=dtall, scalar=-1.0, in1=eaT,
                                                             op0=ALU.mult, op1=ALU.mult), writes=[preB])

        convp = T["convp"]
        cs = [T["cs0"], T["cs1"]]
        acc = [T["acc0"], T["acc1"]]
        csB = [[Buf("cs%d_%d" % (a, t)) for t in range(4)] for a in range(2)]
        accB = [[Buf("acc%d_%d" % (a, t)) for t in range(4)] for a in range(2)]
        sil2 = [T["sil0"], T["sil1"]]
        silB2 = [[Buf("sil%d_%d" % (a_, t)) for t in range(4)] for a_ in range(2)]
        Wx = [T["Wx0"], T["Wx1"]]
        xs_tm, bmT, cmT, bm_tm = T["xs_tm"], T["bmT"], T["cmT"], T["bm_tm"]
        xsB = [Buf("xs%d" % i) for i in range(NT)]
        bmTB = [Buf("bmT%d" % g) for g in range(2)]
        cmTB = [Buf("cmT%d" % g) for g in range(2)]
        bmtmB = [Buf("bmtm%d" % i) for i in range(NT)]
        rot = [0, 0]

        def A1_front_t(j, t):
            blk, co, a = j // 4, (j % 4) * 128, j % 2
            bk = 4 + rot[0] % 3
            rot[0] += 1
            K.do(PE, [(lambda c=c: nc.tensor.matmul(bank[bk], lhsT=Wx[blk][:, c, co:co + 128],
                                                    rhs=hT[:, c, t * 512:(t + 1) * 512],
                                                    start=(c == 0), stop=(c == 7))) for c in range(8)],
                 reads=[WxB[blk]] + [hTB[i][h] for i in range(4 * t, 4 * t + 4) for h in range(2)],
                 writes=[bankB[bk]])
            K.do(ACT, [lambda: nc.scalar.copy(out=cs[a][:, 3 + t * 512:3 + (t + 1) * 512], in_=bank[bk]),
                       lambda: nc.scalar.activation(out=acc[a][:, t * 512:(t + 1) * 512], in_=bank[bk],
                                                    func=AF.Identity, scale=convp[:, j, 3:4],
                                                    bias=convp[:, j, 4:5])],
                 reads=[cB, pB], writes=[bankB[bk], csB[a][t], accB[a][t]])

        def A1_tap(j, t, sh):
            a = j % 2
            K.do(DVE, lambda: nc.vector.scalar_tensor_tensor(
                out=acc[a][:, t * 512:(t + 1) * 512],
                in0=cs[a][:, 3 - sh + t * 512:3 - sh + (t + 1) * 512],
                scalar=convp[:, j, 3 - sh:4 - sh],
                in1=acc[a][:, t * 512:(t + 1) * 512], op0=ALU.mult, op1=ALU.add),
                reads=[csB[a][t]] + ([csB[a][t - 1]] if t > 0 else []), writes=[accB[a][t]])

        def A1_front(j):
            for t in range(4):
                A1_front_t(j, t)
            for sh in (1, 2, 3):
                for t in range(4):
                    A1_tap(j, t, sh)

        def A1_silu(j):
            a = j % 2
            sil, silB = sil2[a], silB2[a]
            for t in range(4):
                if j < 4:
                    dst, dB = sil[:, t * 512:(t + 1) * 512], [silB[t]]
                elif j < 6:
                    dst, dB = bmT[:, j - 4, t * 512:(t + 1) * 512], [bmTB[j - 4]]
                else:
                    dst, dB = cmT[:, j - 6, t * 512:(t + 1) * 512], [cmTB[j - 6]]
                K.do(ACT, lambda: nc.scalar.activation(out=dst, in_=acc[a][:, t * 512:(t + 1) * 512], func=AF.Silu),
                     reads=[accB[a][t]], writes=dB)

        def A1_back(j):
            sil, silB = sil2[j % 2], silB2[j % 2]
            if j < 6:
                for q in range(4):
                    bk = rot[1] % 4
                    rot[1] += 1
                    if j >= 4:
                        bkb = bank[bk].bitcast(BF16)
                        K.do(PE, [(lambda r=r: nc.tensor.transpose(bkb[:, r * 128:(r + 1) * 128],
                                                                   bmT[:, j - 4, (4 * q + r) * 128:(4 * q + r + 1) * 128],
                                                                   T["identb"])) for r in range(4)],
                             reads=[bmTB[j - 4], cB], writes=[bankB[bk]])
                        src = bkb[:, 0:512].rearrange("p (r c) -> p r c", c=128)
                    else:
                        K.do(PE, [(lambda r=r: nc.tensor.transpose(bank[bk][:, r * 128:(r + 1) * 128],
                                                                   sil[:, (4 * q + r) * 128:(4 * q + r + 1) * 128],
                                                                   ident)) for r in range(4)],
                             reads=[silB[q], cB], writes=[bankB[bk]])
                        src = bank[bk].rearrange("p (r c) -> p r c", c=128)
                    if j < 4:
                        dst = xs_tm[:, 4 * q:4 * q + 4, j * 128:(j + 1) * 128]
                        dB = [xsB[i] for i in range(4 * q, 4 * q + 4)]
                    else:
                        dst = bm_tm[:, 4 * q:4 * q + 4, (j - 4) * 128:(j - 3) * 128]
                        dB = [bmtmB[i] for i in range(4 * q, 4 * q + 4)]
                    if q % 2 == 0:
                        K.do(DVE, lambda: nc.vector.tensor_copy(out=dst, in_=src), writes=[bankB[bk]] + dB)
                    else:
                        K.do(ACT, lambda: nc.scalar.copy(out=dst, in_=src), writes=[bankB[bk]] + dB)


        def prepass2():
            K.do(PE, [(lambda q=q, m=m: nc.tensor.matmul(bank[1][:, q * 128:(q + 1) * 128], lhsT=m, rhs=adtall,
                                                         start=True, stop=True)) for q, m in enumerate((LTs, onesf, UT))],
                 reads=[preB, cB, onesB], writes=[bankB[1]])
            K.do(ACT, lambda: nc.scalar.activation(out=scx, in_=bank[1][:, 0:384], func=AF.Exp), writes=[bankB[1], preB])
            K.do(DVE, lambda: nc.vector.tensor_tensor(out=dtds, in0=dtall, in1=scx[:, 0:128], op=ALU.mult), writes=[preB])

        A0_load(0)
        A0_load(1)
        A0_front(0)
        for i in range(NT):
            if i + 1 < NT:
                A0_front(i + 1)
            transpose_to(i, i % 2, hT, hTB, 2 * (i % 2))
            if i == 5:
                K.dma(POOL, wslots[1], T["Wx1"], win_d[:, :, 1024:1536], reads=[hTB[i][0]], writes=[WxB[1]])
            if i == 11:
                K.dma(POOL, wslots[2], T["Wz"], win_d[:, :, 0:512], reads=[hTB[i][0]], writes=[WzB])
            if i >= 2:
                dt_proj(i - 2)
            if i % 4 == 3:
                t = i // 4
                for j in (0, 1):
                    A1_front_t(j, t)
                if t >= 1:
                    for j in (0, 1):
                        for sh in (1, 2, 3):
                            A1_tap(j, t - 1, sh)
        for j in (0, 1):
            for sh in (1, 2, 3):
                A1_tap(j, 3, sh)
        dt_proj(NT - 2)
        dt_proj(NT - 1)
        prepass1()
        A1_silu(0)
        A1_silu(1)
        prepass2()
        for j in range(8):
            if j + 2 < 8:
                A1_front(j + 2)
            A1_back(j)
            if j + 2 < 8:
                A1_silu(j + 2)

        ycS = ycA = None
        if dbg != "A1":
            K.barrier(skip=(PE,))
            ycS, ycA = T["ycS"], T["ycA"]
            ycB = [[Buf("yc%d_%d" % (i, h)) for h in range(2)] for i in range(NT)]
            WqB, WkB, WvB = Buf("Wq"), Buf("Wk"), Buf("Wv")
            K.dma(POOL, wslots[2], T["Wv"], win_d[:, :, 2568:3080], writes=[WvB])
            K.dma(POOL, wslots[0], T["Wq"], win_d[:, :, 1544:2056], writes=[WqB])
            vB = [Buf("v%d" % i) for i in range(NT)]

            def vproj_pe(i, bk=7):
                K.do(PE, [(lambda c=c: nc.tensor.matmul(bank[bk], lhsT=hT[:, c, i * 128:(i + 1) * 128], rhs=T["Wv"][:, c, :],
                                                        start=(c == 0), stop=(c == 7))) for c in range(8)],
                     reads=[WvB], writes=[bankB[bk]])

            def vproj_evac(i, bk=7, on_act=True):
                if on_act:
                    K.do(ACT, lambda: nc.scalar.copy(out=T["v_tm"][:, i, :], in_=bank[bk]), writes=[bankB[bk], vB[i]])
                else:
                    K.do(DVE, lambda: nc.vector.tensor_copy(out=T["v_tm"][:, i, :], in_=bank[bk]), writes=[bankB[bk], vB[i]])
            state = T["state"]
            stbf = [T["stbf0"], T["stbf1"]]
            stateB = Buf("state")
            sttmpB = Buf("sttmp")
            stbfB = [Buf("stbf0"), Buf("stbf1")]
            K.do(POOL, lambda: nc.gpsimd.memset(state, 0.0), writes=[stateB])
            K.do(POOL, lambda: nc.gpsimd.memset(stbf[0], 0.0), writes=[stbfB[0]])
            Ah = [T["Ah0"], T["Ah1"]]; dec = [T["dec0"], T["dec1"]]; cbm = [T["cbm0"], T["cbm1"]]
            MT = [T["MT0"], T["MT1"]]; xdt = [T["xdt0"], T["xdt1"]]; xds = [T["xds0"], T["xds1"]]
            yn = [T["yn0"], T["yn1"]]
            AhB = [Buf(), Buf()]; decB = [[Buf(), Buf()], [Buf(), Buf()]]; cbmB = [Buf(), Buf()]
            MTB = [[Buf(), Buf()], [Buf(), Buf()]]; xdtB = [Buf(), Buf()]; xdsB = [Buf(), Buf()]; ynB = [Buf(), Buf()]
            ez, t1, t2, t3, yg = (T[n] for n in ("ez", "t1", "t2", "t3", "yg"))
            zz = [T["zz"], T["zz"]]
            ezB, t1B, t2B, t3B, ygB = (Buf(n) for n in ("ez", "t1", "t2", "t3", "yg"))
            zzB = [Buf("zz")] * 2
            gssd = T["gssd"]
            BZ, BS0, BS1, BYO, BCS = 0, 1, 2, 5, 6
            BCB, BT = BS0, BS1
            BY = [3, 4]

            def xs3_(i):
                return xs_tm[:, i, :].rearrange("p (h q) -> p h q", q=64)

            def fA(i):
                b = i % 2
                tok = slice(i * 128, (i + 1) * 128)
                K.do(PE, [(lambda c=c: nc.tensor.matmul(bank[BZ], lhsT=hT[:, c, tok], rhs=Wz[:, c, :],
                                                        start=(c == 0), stop=(c == 7))) for c in range(8)],
                     reads=[WzB], writes=[bankB[BZ]])
                K.do(ACT, [(lambda h=h: nc.scalar.activation(out=Ah[b][:, h, :], in_=UT, func=AF.Identity,
                                                             scale=adtall[:, i * 8 + h:i * 8 + h + 1])) for h in range(8)],
                     reads=[preB, cB], writes=[AhB[b]])
                for hh in range(2):
                    bs = BS0 + hh
                    K.do(PE, lambda: nc.tensor.matmul(bank[bs], lhsT=LTs,
                                                      rhs=Ah[b][:, 4 * hh:4 * hh + 4, :].rearrange("p h l -> p (h l)"),
                                                      start=True, stop=True),
                         reads=[AhB[b], cB], writes=[bankB[bs]])
                    K.do(ACT, lambda: nc.scalar.activation(out=dec[b][:, 4 * hh:4 * hh + 4, :],
                                                           in_=bank[bs].rearrange("p (h l) -> p h l", l=128), func=AF.Exp),
                         writes=[bankB[bs], decB[b][hh]])
                K.do(PE, [(lambda g=g: nc.tensor.matmul(bank[BCB][:, g * 128:(g + 1) * 128], lhsT=bmT[:, g, tok],
                                                        rhs=cmT[:, g, tok], start=True, stop=True)) for g in range(2)],
                     reads=bmTB + cmTB, writes=[bankB[BCB]])
                K.do(ACT, lambda: nc.scalar.activation(out=ez, in_=bank[BZ], func=AF.Exp, scale=-1.0),
                     writes=[bankB[BZ], ezB])
                K.do(ACT, lambda: nc.scalar.activation(out=ez, in_=ez, func=AF.Ln, bias=1.0), writes=[ezB])
                K.do(ACT, lambda: nc.scalar.activation(out=ez, in_=ez, func=AF.Exp, scale=-1.0), writes=[ezB])

            def fB(i):
                b = i % 2
                K.do(DVE, lambda: nc.vector.tensor_tensor(out=cbm[b], in0=bank[BCB][:, 0:256].rearrange("p (g l) -> p g l", l=128),
                                                          in1=UT.unsqueeze(1).to_broadcast([128, 2, 128]), op=ALU.mult),
                     reads=[cB], writes=[bankB[BCB], cbmB[b]])
                for g in range(2):
                    K.do(DVE, lambda: nc.vector.tensor_tensor(out=MT[b][:, 4 * g:4 * g + 4, :], in0=dec[b][:, 4 * g:4 * g + 4, :],
                                                              in1=cbm[b][:, g, :].unsqueeze(1).to_broadcast([128, 4, 128]),
                                                              op=ALU.mult),
                         reads=[decB[b][g], cbmB[b]], writes=[MTB[b][g]])
                K.do(DVE, lambda: nc.vector.tensor_tensor(out=xdt[b].rearrange("p (h q) -> p h q", q=64), in0=xs3_(i),
                                                          in1=dtall[:, i * 8:(i + 1) * 8].unsqueeze(2).to_broadcast([128, 8, 64]),
                                                          op=ALU.mult),
                     reads=[xsB[i], preB], writes=[xdtB[b]])
                K.do(DVE, lambda: nc.vector.tensor_tensor(out=xds[b].rearrange("p (h q) -> p h q", q=64), in0=xs3_(i),
                                                          in1=dtds[:, i * 8:(i + 1) * 8].unsqueeze(2).to_broadcast([128, 8, 64]),
                                                          op=ALU.mult),
                     reads=[xsB[i], preB], writes=[xdsB[b]])
                K.do(PE, [(lambda h=h: nc.tensor.matmul(bank[BY[b]][:, h * 64:(h + 1) * 64], lhsT=MT[b][:, h, :],
                                                        rhs=xdt[b][:, h * 64:(h + 1) * 64], start=True, stop=True))
                          for h in range(8)],
                     reads=[MTB[b][0], MTB[b][1], xdtB[b]], writes=[bankB[BY[b]]])

            def fC(i):
                b = i % 2
                K.do(DVE, lambda: nc.vector.tensor_tensor(out=zz[b], in0=bank[BZ], in1=ez, op=ALU.mult),
                     reads=[ezB], writes=[bankB[BZ], zzB[b]])

            def sP(i):
                b = i % 2
                pp = i % 2
                tok = slice(i * 128, (i + 1) * 128)
                K.do(PE, [(lambda g=g: nc.tensor.matmul(bank[BYO][:, g * 256:(g + 1) * 256], lhsT=cmT[:, g, tok],
                                                        rhs=stbf[pp][:, g * 256:(g + 1) * 256], start=True, stop=True))
                          for g in range(2)],
                     reads=cmTB + [stbfB[pp]], writes=[bankB[BYO]])
                K.do(PE, [(lambda g=g: nc.tensor.matmul(bank[BCS][:, g * 256:(g + 1) * 256],
                                                        lhsT=bm_tm[:, i, g * 128:(g + 1) * 128],
                                                        rhs=xds[b][:, g * 256:(g + 1) * 256], start=True, stop=True))
                          for g in range(2)],
                     reads=[bmtmB[i], xdsB[b]], writes=[bankB[BCS]])

            def st(i):
                pp = i % 2
                K.do(DVE, lambda: nc.vector.tensor_tensor(out=state.rearrange("p (h q) -> p h q", q=64),
                                                          in0=state.rearrange("p (h q) -> p h q", q=64),
                                                          in1=scx[:, 128 + i * 8:128 + (i + 1) * 8].unsqueeze(2).to_broadcast([128, 8, 64]),
                                                          op=ALU.mult),
                     reads=[preB], writes=[stateB])
                K.do(DVE, lambda: nc.vector.tensor_tensor(out=state, in0=bank[BCS], in1=state, op=ALU.add),
                     writes=[bankB[BCS], stateB])
                K.do(ACT, lambda: nc.scalar.copy(out=stbf[1 - pp], in_=state), reads=[stateB], writes=[stbfB[1 - pp]])

            def eA(i):
                b = i % 2
                K.do(DVE, lambda: nc.vector.tensor_tensor(out=t1.rearrange("p (h q) -> p h q", q=64),
                                                          in0=bank[BYO].rearrange("p (h q) -> p h q", q=64),
                                                          in1=scx[:, 256 + i * 8:256 + (i + 1) * 8].unsqueeze(2).to_broadcast([128, 8, 64]),
                                                          op=ALU.mult),
                     reads=[preB], writes=[bankB[BYO], t1B])
                K.do(DVE, lambda: nc.vector.tensor_tensor(out=t2, in0=bank[BY[b]], in1=t1, op=ALU.add),
                     reads=[t1B], writes=[bankB[BY[b]], t2B])
                K.do(DVE, lambda: nc.vector.tensor_tensor(out=t3.rearrange("p (h q) -> p h q", q=64), in0=xs3_(i),
                                                          in1=smallp[:, 2, 0:8].unsqueeze(2).to_broadcast([128, 8, 64]),
                                                          op=ALU.mult),
                     reads=[xsB[i], cB, pB], writes=[t3B])
                K.do(DVE, lambda: nc.vector.tensor_tensor(out=t2, in0=t2, in1=t3, op=ALU.add),
                     reads=[t3B], writes=[t2B])
                K.do(DVE, lambda: nc.vector.tensor_tensor(out=yg, in0=t2, in1=zz[b], op=ALU.mult),
                     reads=[t2B, zzB[b]], writes=[ygB])

            def eS(i):
                sb = statB[i]
                c0 = 64 + 4 * i
                K.do(ACT, [(lambda g=g: nc.scalar.activation(out=t1[:, g * 256:(g + 1) * 256], in_=yg[:, g * 256:(g + 1) * 256],
                                                             func=AF.Square, accum_out=stat[:, c0 + g:c0 + g + 1]))
                           for g in range(2)],
                     reads=[ygB], writes=[t1B, sb])
                K.do(ACT, lambda: nc.scalar.activation(out=stat[:, c0:c0 + 2], in_=stat[:, c0:c0 + 2], func=AF.Ln,
                                                       bias=EPS, scale=1.0 / 256), writes=[sb])
                K.do(ACT, lambda: nc.scalar.activation(out=stat[:, c0:c0 + 2], in_=stat[:, c0:c0 + 2], func=AF.Exp,
                                                       scale=-0.5), writes=[sb])

            def eB(i):
                b = i % 2
                sb = statB[i]
                c0 = 64 + 4 * i
                tok = slice(i * 128, (i + 1) * 128)
                K.do(DVE, [(lambda g=g: nc.vector.scalar_tensor_tensor(out=yn[b][:, g * 256:(g + 1) * 256],
                                                                       in0=yg[:, g * 256:(g + 1) * 256],
                                                                       scalar=stat[:, c0 + g:c0 + g + 1],
                                                                       in1=gssd[:, g * 256:(g + 1) * 256],
                                                                       op0=ALU.mult, op1=ALU.mult)) for g in range(2)],
                     reads=[ygB, sb, cB, pB], writes=[ynB[b]])
                btb = bank[BT].bitcast(BF16)
                K.do(PE, [(lambda c=c: nc.tensor.transpose(btb[:, c * 128:(c + 1) * 128], yn[b][:, c * 128:(c + 1) * 128], T["identb"]))
                          for c in range(4)],
                     reads=[ynB[b], cB], writes=[bankB[BT]])
                K.do(ACT, lambda: nc.scalar.copy(out=ycS[:, :, tok], in_=btb[:, 0:512].rearrange("p (c t) -> p c t", t=128)),
                     writes=[bankB[BT], ycB[i][0]])

            fA(0); fB(0); fC(0); sP(0)
            for i in range(NT):
                nx = i + 1 < NT
                if nx:
                    fA(i + 1)
                if i >= 1:
                    vproj_pe(i - 1)
                    vproj_evac(i - 1)
                eA(i)
                eS(i)
                st(i)
                if nx:
                    fB(i + 1)
                eB(i)
                if nx:
                    fC(i + 1)
                    sP(i + 1)

        if dbg not in ("A1", "A2"):
            K.barrier(skip=(PE,))
            qpad, kT, v_tm = T["qpad"], T["kT"], T["v_tm"]
            Wq, Wk, Wv, Wout = T["Wq"], T["Wk"], T["Wv"], T["Wout"]
            WoutB = Buf("Wout")
            K.dma(POOL, wslots[1], T["Wk"], win_d[:, :, 2056:2568], writes=[WkB])
            K.dma(POOL, wslots[3], Wout, wout_d[:, :, :], writes=[WoutB])
            qB = [[Buf("q%d_%d" % (j, t)) for t in range(4)] for j in range(4)]
            kB = [[Buf("k%d_%d" % (j, t)) for t in range(4)] for j in range(4)]
            qz = Buf("qzero")
            for j in range(4):
                K.do(POOL, [lambda: nc.gpsimd.memset(qpad[64:128, j, 0, :], 0.0),
                            lambda: nc.gpsimd.memset(qpad[0:64, j, 1, :], 0.0)], writes=[qz])
                for t in range(4):
                    qB[j][t].w = qz.w
            tb = lambda t: slice(t * 512, (t + 1) * 512)
            rot2 = [0]

            def proj_q(j, t, banks, act_ok):
                bk = banks[rot2[0] % len(banks)]; rot2[0] += 1
                K.do(PE, [(lambda c=c: nc.tensor.matmul(bank[bk], lhsT=Wq[:, c, j * 128:(j + 1) * 128], rhs=hT[:, c, tb(t)],
                                                        start=(c == 0), stop=(c == 7))) for c in range(8)],
                     reads=[WqB], writes=[bankB[bk]])
                if act_ok:
                    K.do(ACT, lambda: nc.scalar.activation(out=qpad[0:64, j, 0, tb(t)], in_=bank[bk][0:64, :], func=AF.Copy, scale=0.125),
                         writes=[bankB[bk], qB[j][t]])
                else:
                    K.do(DVE, lambda: nc.vector.tensor_scalar_mul(out=qpad[0:64, j, 0, tb(t)], in0=bank[bk][0:64, :], scalar1=0.125),
                         writes=[bankB[bk], qB[j][t]])
                K.do(DVE, lambda: nc.vector.tensor_scalar_mul(out=qpad[64:128, j, 1, tb(t)], in0=bank[bk][64:128, :], scalar1=0.125),
                     writes=[bankB[bk], qB[j][t]])

            def proj_k(j, t, banks, act_ok):
                bk = banks[rot2[0] % len(banks)]; rot2[0] += 1
                K.do(PE, [(lambda c=c: nc.tensor.matmul(bank[bk], lhsT=Wk[:, c, j * 128:(j + 1) * 128], rhs=hT[:, c, tb(t)],
                                                        start=(c == 0), stop=(c == 7))) for c in range(8)],
                     reads=[WkB], writes=[bankB[bk]])
                if act_ok:
                    K.do(ACT, lambda: nc.scalar.copy(out=kT[:, j, tb(t)], in_=bank[bk]), writes=[bankB[bk], kB[j][t]])
                else:
                    K.do(DVE, lambda: nc.vector.tensor_copy(out=kT[:, j, tb(t)], in_=bank[bk]), writes=[bankB[bk], kB[j][t]])

            allb = list(range(8))
            for t in range(4):
                proj_q(0, t, allb, True)
            for t in range(4):
                proj_k(0, t, allb, True)
            for i in range(NT - 1, NT):
                bk = allb[rot2[0] % 8]; rot2[0] += 1
                vproj_pe(i, bk)
                vproj_evac(i, bk, on_act=False)
            def proj_piece(kind, j, t, c0_, c1_):
                W_, WB_ = (Wq, WqB) if kind == "q" else (Wk, WkB)
                K.do(PE, [(lambda c=c: nc.tensor.matmul(bank[3], lhsT=W_[:, c, j * 128:(j + 1) * 128], rhs=hT[:, c, tb(t)],
                                                        start=(c == 0), stop=(c == 7))) for c in range(c0_, c1_)],
                     reads=[WB_], writes=[bankB[3]])
                if c1_ == 8:
                    if kind == "q":
                        K.do(DVE, lambda: nc.vector.tensor_scalar_mul(out=qpad[0:64, j, 0, tb(t)], in0=bank[3][0:64, :], scalar1=0.125),
                             writes=[bankB[3], qB[j][t]])
                        K.do(DVE, lambda: nc.vector.tensor_scalar_mul(out=qpad[64:128, j, 1, tb(t)], in0=bank[3][64:128, :], scalar1=0.125),
                             writes=[bankB[3], qB[j][t]])
                    else:
                        K.do(DVE, lambda: nc.vector.tensor_copy(out=kT[:, j, tb(t)], in_=bank[3]), writes=[bankB[3], kB[j][t]])

            pending = {j: [(kind, j, t, c, c + 2) for kind in ("q", "k") for t in range(4) for c in range(0, 8, 2)]
                       for j in (1, 2, 3)}

            triI, triC, mks, zerob, BD, sbg = T["triI"], T["triC"], T["mks"], T["zerob"], T["BD"], T["sbg"]
            Et = [T["E0"], T["E1"], T["E2"]]
            ut = [T["u0"], T["u1"]]
            spt = [T["sp0"], T["sp1"]]
            wt = [T["w0"], T["w1"]]
            EB = [Buf(), Buf(), Buf()]
            uB = [Buf(), Buf()]
            spB = [Buf(), Buf()]
            wB = [Buf(), Buf()]
            osq, rs = T["osq"], T["rs"]
            osqB, rsB = Buf("osq"), Buf("rs")
            ycB2 = [[Buf() for _ in range(4)] for _ in range(4)]
            ps3 = lambda b0, c0: psum[:, b0 * 512:(b0 + 2) * 512].rearrange("p (h c) -> p h c", c=512)[:, :, c0:512]
            Z0, NB, RB0, OB0 = 0, 2, 4, 6
            steps = []
            for j in range(4):
                for Tq in range(4):
                    nb = 4 * Tq + 4
                    for idx in range(nb):
                        b = nb - 1 - idx
                        joff = b - 4 * Tq
                        steps.append(dict(j=j, Tq=Tq, b=b, idx=idx, nb=nb, diag=(joff >= 0),
                                          c0=(128 * joff if joff > 0 else 0), first=(idx == 0), last=(idx == nb - 1)))
            NS = len(steps)

            def e_Z(n):
                st_ = steps[n]; j, Tq, b, c0, diag = st_["j"], st_["Tq"], st_["b"], st_["c0"], st_["diag"]
                th = []
                for h in range(2):
                    th.append(lambda h=h: nc.tensor.matmul(bank[Z0 + h][:, c0:512], lhsT=kT[:, j, b * 128:(b + 1) * 128],
                                                           rhs=qpad[:, j, h, Tq * 512 + c0:(Tq + 1) * 512],
                                                           start=True, stop=not diag))
                    if diag:
                        th.append(lambda h=h: nc.tensor.matmul(bank[Z0 + h][:, c0:c0 + 128], lhsT=T["identb"], rhs=T["negm"],
                                                               start=False, stop=True))
                K.do(PE, th, reads=[kB[j][b // 4], qB[j][Tq], cB], writes=[bankB[Z0], bankB[Z0 + 1]])

            def e_E(n):
                c0 = steps[n]["c0"]
                K.do(ACT, lambda: nc.scalar.activation(out=Et[n % 3][:, :, c0:512], in_=ps3(Z0, c0), func=AF.Exp),
                     writes=[bankB[Z0], bankB[Z0 + 1], EB[n % 3]])

            def e_sp(n):
                c0, diag, sl = steps[n]["c0"], steps[n]["diag"], n % 2
                K.do(ACT, lambda: nc.scalar.activation(out=spt[sl][:, :, c0:512], in_=Et[n % 3][:, :, c0:512], func=AF.Ln, bias=1.0),
                     reads=[EB[n % 3]], writes=[spB[sl]])

            def e_init(n, which):
                st_ = steps[n]; j, Tq = st_["j"], st_["Tq"]
                b0 = RB0 if which == "R" else OB0
                K.do(PE, [(lambda bk=bk: nc.tensor.matmul(bank[bk], lhsT=zerob, rhs=qpad[:, j, 0, Tq * 512:(Tq + 1) * 512],
                                                          start=True, stop=False, skip_group_check=True)) for bk in (b0, b0 + 1)],
                     reads=[cB, qB[j][Tq]], writes=[bankB[b0], bankB[b0 + 1]])

            def e_tri(n, tri):
                c0, sl = steps[n]["c0"], n % 2
                K.do(PE, [(lambda h=h: nc.tensor.matmul(bank[RB0 + h][:, c0:512], lhsT=tri, rhs=spt[sl][:, h, c0:512],
                                                        start=False, stop=False, skip_group_check=True)) for h in range(2)],
                     reads=[spB[sl], cB], writes=[bankB[RB0], bankB[RB0 + 1]])

            def e_u_w(n):
                c0, diag, sl = steps[n]["c0"], steps[n]["diag"], n % 2
                K.do(ACT, lambda: nc.scalar.activation(out=ut[sl][:, :, c0:512], in_=ps3(RB0, c0), func=AF.Exp, scale=-1.0),
                     writes=[bankB[RB0], bankB[RB0 + 1], uB[sl]])
                K.do(DVE, lambda: nc.vector.tensor_tensor(out=wt[sl][:, :, c0:512], in0=Et[n % 3][:, :, c0:512],
                                                          in1=ut[sl][:, :, c0:512], op=ALU.mult),
                     reads=[EB[n % 3], uB[sl]], writes=[wB[sl]])

            def e_WV(n):
                st_ = steps[n]; j, b, c0, sl = st_["j"], st_["b"], st_["c0"], n % 2
                K.do(PE, [(lambda h=h: nc.tensor.matmul(bank[OB0 + h][:, c0:512], lhsT=v_tm[:, b, j * 128:(j + 1) * 128],
                                                        rhs=wt[sl][:, h, c0:512], start=False, stop=st_["last"],
                                                        skip_group_check=True)) for h in range(2)],
                     reads=[wB[sl], vB[b]], writes=[bankB[OB0], bankB[OB0 + 1]])

            osave = T["osave"]
            osaveB = Buf("osave")

            def e_epiA(n):
                K.do(DVE, lambda: nc.vector.tensor_copy(out=osave[0:64, :], in_=bank[OB0][0:64, :]),
                     writes=[bankB[OB0], osaveB])
                K.do(DVE, lambda: nc.vector.tensor_copy(out=osave[64:128, :], in_=bank[OB0 + 1][64:128, :]),
                     writes=[bankB[OB0 + 1], osaveB])
                K.do(DVE, lambda: nc.vector.tensor_tensor(out=osq, in0=osave, in1=osave, op=ALU.mult),
                     reads=[osaveB], writes=[osqB])

            def e_epiA2(n):
                K.do(PE, lambda: nc.tensor.matmul(bank[NB], lhsT=BD, rhs=osq, start=True, stop=True),
                     reads=[osqB, bdB], writes=[bankB[NB]])

            def e_epiB(n):
                st_ = steps[n]; j, Tq = st_["j"], st_["Tq"]
                qall = slice(Tq * 512, (Tq + 1) * 512)
                K.do(ACT, lambda: nc.scalar.activation(out=rs, in_=bank[NB], func=AF.Ln, bias=EPS), writes=[bankB[NB], rsB])
                K.do(ACT, lambda: nc.scalar.activation(out=rs, in_=rs, func=AF.Exp, scale=-0.5), writes=[rsB])
                for h in range(2):
                    pr = slice(64 * h, 64 * h + 64)
                    K.do(DVE, lambda: nc.vector.scalar_tensor_tensor(out=ycA[pr, j, qall], in0=osave[pr, :],
                                                                     scalar=sbg[pr, j:j + 1], in1=rs[pr, :],
                                                                     op0=ALU.mult, op1=ALU.mult),
                         reads=[rsB, cB, pB, osaveB], writes=[ycB2[j][Tq]])

            e_Z(0); e_E(0); e_Z(1); e_sp(0)
            for n in range(NS):
                st_ = steps[n]
                if st_["first"]:
                    e_init(n, "R")
                e_tri(n, triI)
                if n + 1 < NS:
                    e_E(n + 1)
                if n + 2 < NS:
                    e_Z(n + 2)
                e_u_w(n)
                if n + 1 < NS:
                    e_sp(n + 1)
                if not st_["last"]:
                    e_tri(n, triC)
                if st_["idx"] == 1:
                    e_init(n - 1, "O")
                if n >= 1:
                    e_WV(n - 1)
                    if steps[n - 1]["last"]:
                        e_epiA(n - 1)
                if st_["idx"] == 1 and n >= 2:
                    e_epiA2(n - 2)
                if st_["idx"] == 2 and n >= 3:
                    e_epiB(n - 3)
                nj = st_["j"] + 1
                if nj in pending and pending[nj]:
                    proj_piece(*pending[nj].pop(0))
            e_WV(NS - 1)
            e_epiA(NS - 1)
            e_epiA2(NS - 1)
            e_epiB(NS - 1)

        if dbg not in ("A1", "A2", "A4"):
            K.barrier(skip=(PE,))
            x1, h2T = T["x1"], T["h2T"]
            xb = [T["xb0"], T["xb1"]]
            hn2 = [T["hn20"], T["hn21"]]
            gpm, gpf = T["gpm"], T["gpf"]
            xbB = [Buf(), Buf()]
            hn2B = [Buf(), Buf()]
            gB3 = Buf("gains3")
            x1B = [Buf("x1_%d" % i) for i in range(NT)]
            h2TB = [[Buf() for _ in range(2)] for _ in range(NT)]
            K.dma(SP, small_slot, gpm, gains_d[:, 1, :])
            K.dma(SP, small_slot, gpf, gains_d[:, 2, :])
            gB3.w = (small_slot.sem, small_slot.cnt)
            wgu = [T["wgu0"], T["wgu1"]]
            wguB = [Buf(), Buf()]
            for sl_ in range(2):
                K.dma(POOL, wslots[sl_], wgu[sl_].rearrange("p a b c -> p (a b c)"), wgu_d[sl_, :, :], writes=[wguB[sl_]])
            WdAB = [Buf() for _ in range(3)]
            wdaslots = [DmaSlot(K, "wda%d" % q) for q in range(3)]
            for q, (lo, hi) in enumerate(((0, 4), (4, 8), (8, 11))):
                K.dma(POOL, wdaslots[q], T["WdA"][:, lo:hi, :], wd_d[:, lo:hi, :], writes=[WdAB[q]])
            junk = T["junkb"]
            junkB = Buf("junk")
            ps2 = lambda b0_: psum[:, b0_ * 512:(b0_ + 2) * 512]

            def B_P1(i):
                s = i % 2
                m = 2 * (i % 3)
                tok = slice(i * 128, (i + 1) * 128)
                K.dma(SP, xslot[s], xb[s], x_d[tok, :], writes=[xbB[s]])
                for hh in range(2):
                    bk = m + hh
                    K.do(PE, [(lambda c=c: nc.tensor.matmul(bank[bk], lhsT=(ycS if c < 4 else ycA)[:, c % 4, tok], rhs=Wout[:, c, hh * 512:(hh + 1) * 512],
                                                            start=(c == 0), stop=(c == 7))) for c in range(8)],
                         reads=[WoutB, ycB[i][0], ycB2[0][i // 4], ycB2[1][i // 4], ycB2[2][i // 4], ycB2[3][i // 4]],
                         writes=[bankB[bk]])

            def B_A12(i):
                m = 2 * (i % 3)
                c0 = 128 + 4 * i
                sb = statB[i]
                K.do(ACT, lambda: nc.scalar.activation(out=junk, in_=ps2(m), func=AF.Square, accum_out=stat[:, c0:c0 + 1]),
                     writes=[bankB[m], bankB[m + 1], junkB, sb])
                K.do(ACT, lambda: nc.scalar.activation(out=stat[:, c0:c0 + 1], in_=stat[:, c0:c0 + 1], func=AF.Ln,
                                                       bias=EPS, scale=1.0 / D), writes=[sb])
                K.do(ACT, lambda: nc.scalar.activation(out=stat[:, c0:c0 + 1], in_=stat[:, c0:c0 + 1], func=AF.Exp,
                                                       scale=-0.5), writes=[sb])

            def B_D1(i):
                s = i % 2
                m = 2 * (i % 3)
                c0 = 128 + 4 * i
                sb = statB[i]
                K.do(DVE, lambda: nc.vector.scalar_tensor_tensor(out=x1[:, i, :], in0=ps2(m), scalar=stat[:, c0:c0 + 1],
                                                                 in1=gpm, op0=ALU.mult, op1=ALU.mult),
                     reads=[sb, gB3], writes=[bankB[m], bankB[m + 1], x1B[i]])
                K.do(DVE, lambda: nc.vector.tensor_tensor(out=x1[:, i, :], in0=x1[:, i, :], in1=xb[s], op=ALU.add),
                     reads=[xbB[s]], writes=[x1B[i]])

            def B_A34(i):
                c1 = 128 + 4 * i + 1
                sb = statB[i]
                K.do(ACT, lambda: nc.scalar.activation(out=junk, in_=x1[:, i, :], func=AF.Square, accum_out=stat[:, c1:c1 + 1]),
                     reads=[x1B[i]], writes=[junkB, sb])
                K.do(ACT, lambda: nc.scalar.activation(out=stat[:, c1:c1 + 1], in_=stat[:, c1:c1 + 1], func=AF.Ln,
                                                       bias=EPS, scale=1.0 / D), writes=[sb])
                K.do(ACT, lambda: nc.scalar.activation(out=stat[:, c1:c1 + 1], in_=stat[:, c1:c1 + 1], func=AF.Exp,
                                                       scale=-0.5), writes=[sb])

            def B_D2(i):
                s = i % 2
                c1 = 128 + 4 * i + 1
                K.do(DVE, lambda: nc.vector.scalar_tensor_tensor(out=hn2[s], in0=x1[:, i, :], scalar=stat[:, c1:c1 + 1],
                                                                 in1=gpf, op0=ALU.mult, op1=ALU.mult),
                     reads=[x1B[i], statB[i], gB3], writes=[hn2B[s]])

            for k in range(NT + 3):
                if 0 <= k - 3 < NT:
                    transpose_to(k - 3, (k - 3) % 2, h2T, h2TB, 6, hn=hn2, hnB=hn2B, all_act=False, part="pe")
                if 0 <= k - 1 < NT:
                    B_A12(k - 1)
                    B_D1(k - 1)
                if 0 <= k - 2 < NT:
                    B_A34(k - 2)
                    B_D2(k - 2)
                if 0 <= k - 3 < NT:
                    transpose_to(k - 3, (k - 3) % 2, h2T, h2TB, 6, hn=hn2, hnB=hn2B, all_act=False, part="evac")
                if k < NT:
                    B_P1(k)

        if dbg is None or dbg == "F":
            K.barrier(skip=(PE,))
            GT, gff = T["GT"], T["gff"]
            Wd_ = lambda f_: (T["WdA"][:, f_, :] if f_ < 11 else T["WdB"][:, f_ - 11, :])
            sg = [T["sg0"], T["sg1"]]
            ft = [T["ft0"], T["ft1"]]
            sgB = [Buf(), Buf()]
            ftB = [Buf(), Buf()]
            WdB = [Buf() for _ in range(11)]
            GTB = [Buf() for _ in range(NFC)]
            gB4 = Buf("gains4")
            K.dma(SP, small_slot, gff, gains_d[:, 3, :], writes=[gB4])
            wdslots = [DmaSlot(K, "wd%d" % q) for q in range(11)]
            wdparts = [(2 * q, 2 * q + 2) for q in range(11)]
            oslot = [DmaSlot(K, "o0"), DmaSlot(K, "o1")]
            seq = [(hf, fc) for hf in range(2) for fc in range(NFC)]

            def load_wgu(k):
                hf, fc = seq[k]
                sl = k % 2
                K.dma(POOL, wslots[sl], wgu[sl].rearrange("p a b c -> p (a b c)"), wgu_d[fc, :, :], writes=[wguB[sl]])

            rotc = 0
            for k, (hf, fc) in enumerate(seq):
                sl = k % 2
                if hf == 0 and fc < 11:
                    K.dma(POOL, wdslots[fc], T["WdB"][:, fc:fc + 1, :], wd_d[:, 11 + fc:12 + fc, :], writes=[WdB[fc]])
                for t in range(2):
                    tcols = slice(hf * 1024 + t * 512, hf * 1024 + (t + 1) * 512)
                    bg = 2 * (rotc % 2)
                    bu = bg + 1
                    ss_ = rotc % 2
                    rotc += 1
                    rd = [wguB[sl]] + [h2TB[i][h] for i in range(hf * 8 + t * 4, hf * 8 + t * 4 + 4) for h in range(2)]
                    K.do(PE, [(lambda c=c: nc.tensor.matmul(bank[bg], lhsT=wgu[sl][:, 0, c, :], rhs=h2T[:, c, tcols],
                                                            start=(c == 0), stop=(c == 7))) for c in range(8)],
                         reads=rd, writes=[bankB[bg]])
                    K.do(PE, [(lambda c=c: nc.tensor.matmul(bank[bu], lhsT=wgu[sl][:, 1, c, :], rhs=h2T[:, c, tcols],
                                                            start=(c == 0), stop=(c == 7))) for c in range(8)],
                         reads=rd, writes=[bankB[bu]])
                    K.do(ACT, lambda: nc.scalar.activation(out=sg[ss_], in_=bank[bg], func=AF.Silu),
                         writes=[bankB[bg], sgB[ss_]])
                    K.do(DVE, lambda: nc.vector.tensor_tensor(out=GT[:, fc, t * 512:(t + 1) * 512], in0=bank[bu], in1=sg[ss_],
                                                              op=ALU.mult),
                         reads=[sgB[ss_]], writes=[bankB[bu], GTB[fc]])
                if k + 2 < len(seq):
                    load_wgu(k + 2)
                if fc == NFC - 1:
                    for il in range(8):
                        i = hf * 8 + il
                        s = il % 2
                        sb = statB[i]
                        c0 = 192 + 4 * i
                        for hh in range(2):
                            bk = 4 + 2 * s + hh
                            K.do(PE, [(lambda f_=f_: nc.tensor.matmul(bank[bk], lhsT=GT[:, f_, il * 128:(il + 1) * 128],
                                                                      rhs=Wd_(f_)[:, hh * 512:(hh + 1) * 512],
                                                                      start=(f_ == 0), stop=(f_ == NFC - 1))) for f_ in range(NFC)],
                                 reads=GTB + WdB + WdAB, writes=[bankB[bk]])
                            K.do(ACT, lambda: nc.scalar.activation(out=ft[hh], in_=bank[bk], func=AF.Square,
                                                                   accum_out=stat[:, c0 + hh:c0 + hh + 1]),
                                 writes=[bankB[bk], ftB[hh], sb])
                        K.do(DVE, lambda: nc.vector.tensor_tensor(out=stat[:, c0 + 2:c0 + 3], in0=stat[:, c0:c0 + 1],
                                                                  in1=stat[:, c0 + 1:c0 + 2], op=ALU.add), writes=[sb])
                        K.do(ACT, lambda: nc.scalar.activation(out=stat[:, c0 + 2:c0 + 3], in_=stat[:, c0 + 2:c0 + 3], func=AF.Ln,
                                                               bias=EPS, scale=1.0 / D), writes=[sb])
                        K.do(ACT, lambda: nc.scalar.activation(out=stat[:, c0 + 2:c0 + 3], in_=stat[:, c0 + 2:c0 + 3], func=AF.Exp,
                                                               scale=-0.5), writes=[sb])
                        for hh in range(2):
                            bk = 4 + 2 * s + hh
                            K.do(DVE, lambda: nc.vector.scalar_tensor_tensor(out=ft[hh], in0=bank[bk], scalar=stat[:, c0 + 2:c0 + 3],
                                                                             in1=gff[:, hh * 512:(hh + 1) * 512],
                                                                             op0=ALU.mult, op1=ALU.mult),
                                 reads=[sb, gB4], writes=[bankB[bk], ftB[hh]])
                            K.do(POOL, lambda: nc.gpsimd.tensor_tensor(out=x1[:, i, hh * 512:(hh + 1) * 512],
                                                                       in0=x1[:, i, hh * 512:(hh + 1) * 512], in1=ft[hh], op=ALU.add),
                                 reads=[ftB[hh]], writes=[x1B[i]])
                        K.dma(SP, oslot[s], out_d[i * 128:(i + 1) * 128, :], x1[:, i, :], reads=[x1B[i]])
            K.barrier()
            for s in range(2):
                SP.wait((oslot[s].sem, oslot[s].cnt))

        fin_slot = DmaSlot(K, "fin")

        def dump(name, ap2d):
            K.dma(SP, fin_slot, dbg_outs[name][:, :], ap2d)

        K.barrier()
        if dbg == "A1":
            dump("d_hT", hT.rearrange("p a b -> p (a b)"))
            dump("d_xs", xs_tm.rearrange("p a b -> p (a b)"))
            dump("d_bmT", bmT.rearrange("p a b -> p (a b)"))
            dump("d_cmT", cmT.rearrange("p a b -> p (a b)"))
            dump("d_bmtm", bm_tm.rearrange("p a b -> p (a b)"))
        if dbg in ("A2", "A4"):
            K.dma(SP, fin_slot, dbg_outs["d_ycat"][:, 0:4 * L], ycS.rearrange("p a b -> p (a b)"))
            if dbg == "A4":
                K.dma(SP, fin_slot, dbg_outs["d_ycat"][:, 4 * L:8 * L], ycA.rearrange("p a b -> p (a b)"))
        if dbg == "B":
            dump("d_x1", x1.rearrange("p a b -> p (a b)"))
            dump("d_h2T", h2T.rearrange("p a b -> p (a b)"))
        if dbg is not None and dbg != "F":
            K.dma(SP, fin_slot, out_d[0:128, :], T["ident"].bitcast(F32)[:, 0:128].to_broadcast([128, 128]) if False else big[:, 0:4096].bitcast(F32))
        if fin_slot.cnt:
            SP.wait((fin_slot.sem, fin_slot.cnt))
    return nc, used


def _layout(inputs):
    f = np.float32
    w_in = np.asarray(inputs["w_in"], f)[0]
    w_out = np.asarray(inputs["w_out"], f)[0]
    wg = np.asarray(inputs["w_gate"], f)[0]
    wu = np.asarray(inputs["w_up"], f)[0]
    wd = np.asarray(inputs["w_down"], f)[0]
    m = {}
    m["w_in"] = np.ascontiguousarray(w_in.reshape(8, 128, DIN).transpose(1, 0, 2))
    m["w_out"] = np.ascontiguousarray(w_out.reshape(8, 128, D).transpose(1, 0, 2))
    g4 = wg.reshape(8, 128, NFC, 128).transpose(2, 1, 0, 3)
    u4 = wu.reshape(8, 128, NFC, 128).transpose(2, 1, 0, 3)
    m["wgu"] = np.ascontiguousarray(np.stack([g4, u4], axis=2).reshape(NFC, 128, 2 * 8 * 128))
    m["wd"] = np.ascontiguousarray(wd.reshape(NFC, 128, D).transpose(1, 0, 2))
    gains = np.stack([np.asarray(inputs[k], f)[0] for k in
                      ("pre_mix_gain", "post_mix_gain", "pre_ffn_gain", "post_ffn_gain")], axis=0)
    m["gains"] = np.ascontiguousarray(np.broadcast_to(gains[None], (128, 4, D)))
    cw = np.asarray(inputs["conv_w"], f)[0]
    cb = np.asarray(inputs["conv_b"], f)[0]
    cp = np.concatenate([cw, cb[None]], axis=0)
    m["convp"] = np.ascontiguousarray(cp.reshape(5, 8, 128).transpose(2, 1, 0))
    sp = np.stack([np.tile(np.asarray(inputs[k], f)[0], 16) for k in ("dt_bias", "a_log", "d_skip")], axis=0)
    m["smallp"] = np.ascontiguousarray(np.broadcast_to(sp[None], (128, 3, 128)))
    m["gssd"] = np.ascontiguousarray(np.broadcast_to(np.asarray(inputs["ssd_norm_gain"], f)[0][None], (128, 512)))
    m["sbg"] = np.ascontiguousarray(np.asarray(inputs["sb_norm_gain"], f)[0].reshape(4, 128).T)
    return m


_CACHE = {}


def kernel(**inputs):
    x = np.asarray(inputs["x"], np.float32)
    B = x.shape[0]
    if "nc" not in _CACHE:
        _CACHE["nc"] = build()[0]
    nc = _CACHE["nc"]
    shared = _layout(inputs)
    in_maps = []
    for b in range(B):
        m = dict(shared)
        m["x"] = np.ascontiguousarray(x[b])
        in_maps.append(m)
    res = run_bass_kernel_spmd(nc, in_maps, core_ids=list(range(B)))
    return np.stack([np.asarray(r["out"], np.float32) for r in res.results], axis=0)
```

```python
import numpy as np
import concourse.bass as bass
import concourse.mybir as mybir
from concourse.bass_utils import run_bass_kernel_spmd
from contextlib import ExitStack

F32 = mybir.dt.float32
BF16 = mybir.dt.bfloat16
U8 = mybir.dt.uint8
AF = mybir.ActivationFunctionType
ALU = mybir.AluOpType

L = 2048
D = 1024
NT = 16
DIN = 3080
DFF = 2816
NFC = 22
EPS = 1e-6
NPH = 5
SBUF_BYTES = 212000


class Eng:
    def __init__(self, K, eng, name):
        self.K = K
        self.e = eng
        self.name = name
        self.sem = K.new_sem("prog_" + name)
        self.cnt = 0
        self.seen = {}

    def wait(self, tok):
        if tok is None:
            return
        sem, val = tok
        key = sem.num
        if self.seen.get(key, 0) >= val:
            return
        self.e.wait_ge(sem, val)
        self.seen[key] = val

    def mark(self, ins):
        self.cnt += 1
        ins.then_inc(self.sem, 1)
        return (self.sem, self.cnt)

    def last(self):
        return (self.sem, self.cnt) if self.cnt else None


class Buf:
    __slots__ = ("w", "r", "name")

    def __init__(self, name=""):
        self.w = None
        self.r = {}
        self.name = name


class DmaSlot:
    def __init__(self, K, name):
        self.sem = K.new_sem("dma_" + name)
        self.cnt = 0


class Builder:
    def __init__(self, nc, es):
        self.nc = nc
        self.es = es
        self.nsem = 0
        self.PE = Eng(self, nc.tensor, "pe")
        self.ACT = Eng(self, nc.scalar, "act")
        self.DVE = Eng(self, nc.vector, "dve")
        self.POOL = Eng(self, nc.gpsimd, "pool")
        self.SP = Eng(self, nc.sync, "sp")
        self.engs = [self.PE, self.ACT, self.DVE, self.POOL, self.SP]
        self.allocs = []
        self.dma_toks = []

    def new_sem(self, name):
        self.nsem += 1
        return self.es.enter_context(self.nc.semaphore(name))

    def do(self, E, thunks, reads=(), writes=()):
        for b in reads:
            E.wait(b.w)
        for b in writes:
            E.wait(b.w)
            for t in b.r.values():
                E.wait(t)
        if callable(thunks):
            thunks = [thunks]
        ins = None
        for th in thunks:
            ins = th()
        tok = E.mark(ins)
        for b in reads:
            b.r[tok[0].num] = tok
        for b in writes:
            b.w = tok
            b.r = {}
        return tok

    def dma(self, E, slot, out, in_, reads=(), writes=()):
        for b in reads:
            E.wait(b.w)
        for b in writes:
            E.wait(b.w)
            for t in b.r.values():
                E.wait(t)
        E.e.dma_start(out=out, in_=in_).then_inc(slot.sem, 16)
        slot.cnt += 16
        tok = (slot.sem, slot.cnt)
        for b in reads:
            b.r[tok[0].num] = tok
        for b in writes:
            b.w = tok
            b.r = {}
        return tok

    def barrier(self, skip=()):
        toks = [e.last() for e in self.engs]
        for e in self.engs:
            if e in skip:
                continue
            for t in toks:
                if t is not None and t[0].num != e.sem.num:
                    e.wait(t)

    def alloc(self, name, shape, dt, p0, p1):
        esz = 4 if dt == F32 else 2
        n = int(np.prod(shape[1:])) * esz
        n = (n + 63) // 64 * 64
        self.allocs.append(dict(name=name, shape=shape, dt=dt, p0=p0, p1=p1, n=n, off=None))
        return len(self.allocs) - 1

    def place(self):
        keys = [lambda a: (-a["n"],),
                lambda a: (-(a["p1"] - a["p0"]), -a["n"]),
                lambda a: (-a["n"] * (a["p1"] - a["p0"] + 1),),
                lambda a: (a["p0"], -a["n"]),
                lambda a: (-a["p1"], -a["n"])]
        err = None
        for key in keys:
            try:
                self._place(key)
                return
            except RuntimeError as e:
                err = e
        raise err

    def _place(self, key):
        for a in self.allocs:
            a["off"] = None
        order = sorted(range(len(self.allocs)), key=lambda i: key(self.allocs[i]))
        placed = []
        for i in order:
            a = self.allocs[i]
            cands = [0] + sorted(b["off"] + b["n"] for b in placed)
            for off in cands:
                ok = True
                for b in placed:
                    if a["p0"] <= b["p1"] and b["p0"] <= a["p1"]:
                        if off < b["off"] + b["n"] and b["off"] < off + a["n"]:
                            ok = False
                            break
                if ok:
                    a["off"] = off
                    break
            placed.append(a)
            if a["off"] + a["n"] > SBUF_BYTES:
                raise RuntimeError("SBUF overflow placing %s (%d + %d)" % (a["name"], a["off"], a["n"]))

    def ap(self, big, idx):
        a = self.allocs[idx]
        shape = a["shape"]
        v = big[0:shape[0], a["off"]:a["off"] + int(np.prod(shape[1:])) * (4 if a["dt"] == F32 else 2)]
        if a["dt"] != U8:
            v = v.bitcast(a["dt"])
        if len(shape) == 3:
            v = v.rearrange("p (a b) -> p a b", b=shape[2])
        elif len(shape) == 4:
            v = v.rearrange("p (a b c) -> p a b c", b=shape[2], c=shape[3])
        return v


def build(dbg=None):
    nc = bass.Bass("TRN2", target_bir_lowering=False)

    def din(name, shape, dt=F32):
        return nc.dram_tensor(name, shape, dt, kind="ExternalInput").ap()

    def dout(name, shape, dt=F32):
        return nc.dram_tensor(name, shape, dt, kind="ExternalOutput").ap()

    x_d = din("x", [L, D])
    win_d = din("w_in", [128, 8, DIN])
    wout_d = din("w_out", [128, 8, D])
    wgu_d = din("wgu", [NFC, 128, 2 * 8 * 128])
    wd_d = din("wd", [128, NFC, D])
    gains_d = din("gains", [128, 4, D])
    convp_d = din("convp", [128, 8, 5])
    smallp_d = din("smallp", [128, 3, 128])
    gssd_d = din("gssd", [128, 512])
    sbg_d = din("sbg", [128, 4])
    out_d = dout("out", [L, D])

    dbg_outs = {}
    if dbg == "A1":
        dbg_outs["d_hT"] = dout("d_hT", [128, 8 * L], BF16)
        dbg_outs["d_xs"] = dout("d_xs", [128, NT * 512], F32)
        dbg_outs["d_bmT"] = dout("d_bmT", [128, 2 * L], BF16)
        dbg_outs["d_cmT"] = dout("d_cmT", [128, 2 * L], BF16)
        dbg_outs["d_bmtm"] = dout("d_bmtm", [128, NT * 256], BF16)
    if dbg in ("A2", "A4"):
        dbg_outs["d_ycat"] = dout("d_ycat", [128, 8 * L], BF16)
    if dbg == "B":
        dbg_outs["d_x1"] = dout("d_x1", [128, NT * D], F32)
        dbg_outs["d_h2T"] = dout("d_h2T", [128, 8 * L], BF16)

    es = ExitStack()
    with es:
        big = es.enter_context(nc.sbuf_tensor("big", [128, SBUF_BYTES], U8))
        psum = es.enter_context(nc.psum_tensor("psum", [128, 4096], F32))
        es.enter_context(nc.Block())
        K = Builder(nc, es)
        PE, ACT, DVE, POOL, SP = K.PE, K.ACT, K.DVE, K.POOL, K.SP
        bank = [psum[:, b * 512:(b + 1) * 512] for b in range(8)]
        bankB = [Buf("bank%d" % b) for b in range(8)]

        A = {}

        def al(name, shape, dt, p0, p1):
            A[name] = K.alloc(name, shape, dt, p0, p1)

        al("ident", [128, 128], F32, 0, 4)
        al("onesf", [128, 128], F32, 0, 1)
        al("UT", [128, 128], F32, 0, 1)
        al("LTs", [128, 128], F32, 0, 1)
        al("BD", [128, 128], F32, 0, 2)
        al("triI", [128, 128], BF16, 0, 2)
        al("triC", [128, 128], BF16, 0, 2)
        al("mks", [128, 128], BF16, 0, 2)
        al("zerob", [128, 128], BF16, 0, 2)
        al("identb", [128, 128], BF16, 0, 3)
        al("negm", [128, 128], BF16, 0, 2)
        al("convp", [128, 8, 5], F32, 0, 0)
        al("smallp", [128, 3, 128], F32, 0, 1)
        al("gssd", [128, 512], F32, 0, 1)
        al("sbg", [128, 4], F32, 0, 2)
        al("stat", [128, 256], F32, 0, 4)
        al("hT", [128, 8, L], BF16, 0, 2)
        al("ycS", [128, 4, L], BF16, 1, 3)
        al("ycA", [128, 4, L], BF16, 2, 3)
        al("xs_tm", [128, NT, 512], F32, 0, 1)
        al("bmT", [128, 2, L], BF16, 0, 1)
        al("cmT", [128, 2, L], BF16, 0, 1)
        al("bm_tm", [128, NT, 256], BF16, 0, 1)
        al("g_pre", [128, D], F32, 0, 0)
        al("xin0", [128, D], F32, 0, 0)
        al("xin1", [128, D], F32, 0, 0)
        al("xin2", [128, D], F32, 0, 0)
        al("hn0", [128, D], BF16, 0, 0)
        al("hn1", [128, D], BF16, 0, 0)
        al("cs0", [128, L + 16], F32, 0, 0)
        al("cs1", [128, L + 16], F32, 0, 0)
        al("acc0", [128, L], F32, 0, 0)
        al("acc1", [128, L], F32, 0, 0)
        al("sil0", [128, L], F32, 0, 0)
        al("sil1", [128, L], F32, 0, 0)
        al("Wx0", [128, 8, 512], BF16, 0, 0)
        al("Wx1", [128, 8, 512], BF16, 0, 0)
        al("Wz", [128, 8, 512], BF16, 0, 1)
        al("Wdt", [128, 8, 8], BF16, 0, 1)

        al("dtall", [128, 128], F32, 0, 1)
        al("adtall", [128, 128], F32, 0, 1)
        al("eaT", [128, 128], F32, 0, 1)
        al("scx", [128, 384], F32, 0, 1)
        al("dtds", [128, 128], F32, 0, 1)
        for b_ in range(2):
            al("Ah%d" % b_, [128, 8, 128], F32, 1, 1)
            al("dec%d" % b_, [128, 8, 128], F32, 1, 1)
            al("cbm%d" % b_, [128, 2, 128], F32, 1, 1)
            al("MT%d" % b_, [128, 8, 128], BF16, 1, 1)
            al("xdt%d" % b_, [128, 512], BF16, 1, 1)
            al("xds%d" % b_, [128, 512], BF16, 1, 1)
            al("stbf%d" % b_, [128, 512], BF16, 1, 1)
            al("yn%d" % b_, [128, 512], BF16, 1, 1)
        for nm in ("ez", "t1", "t2", "t3", "zz", "yg", "state"):
            al(nm, [128, 512], F32, 1, 1)

        al("qpad", [128, 4, 2, L], BF16, 2, 2)
        al("kT", [128, 4, L], BF16, 2, 2)
        al("v_tm", [128, NT, 512], BF16, 1, 2)
        al("Wq", [128, 8, 512], BF16, 1, 2)
        al("Wk", [128, 8, 512], BF16, 2, 2)
        al("Wv", [128, 8, 512], BF16, 1, 2)
        al("E2", [128, 2, 512], F32, 2, 2)
        for sl_ in range(2):
            al("E%d" % sl_, [128, 2, 512], F32, 2, 2)
            al("u%d" % sl_, [128, 2, 512], F32, 2, 2)
            al("sp%d" % sl_, [128, 2, 512], BF16, 2, 2)
            al("w%d" % sl_, [128, 2, 512], BF16, 2, 2)
        al("osq", [128, 512], F32, 2, 2)
        al("osave", [128, 512], F32, 2, 2)
        al("rs", [128, 512], F32, 2, 2)
        al("Wout", [128, 8, D], BF16, 2, 3)

        al("x1", [128, NT, D], F32, 3, 4)
        al("h2T", [128, 8, L], BF16, 3, 4)
        al("xb0", [128, D], F32, 3, 3)
        al("xb1", [128, D], F32, 3, 3)
        al("hn20", [128, D], BF16, 3, 3)
        al("hn21", [128, D], BF16, 3, 3)
        al("gpm", [128, D], F32, 3, 3)
        al("gpf", [128, D], F32, 3, 3)
        al("junkb", [128, D], BF16, 3, 3)
        al("WdA", [128, 11, D], BF16, 3, 4)
        al("WdB", [128, 11, D], BF16, 4, 4)
        al("GT", [128, NFC, 1024], BF16, 4, 4)
        al("wgu0", [128, 2, 8, 128], BF16, 3, 4)
        al("wgu1", [128, 2, 8, 128], BF16, 3, 4)
        al("sg0", [128, 512], F32, 4, 4)
        al("sg1", [128, 512], F32, 4, 4)
        al("ft0", [128, 512], F32, 4, 4)
        al("ft1", [128, 512], F32, 4, 4)
        al("gff", [128, D], F32, 4, 4)

        K.place()
        T = {k: K.ap(big, v) for k, v in A.items()}
        used = max(a["off"] + a["n"] for a in K.allocs)

        cB = Buf("consts")
        small_slot = DmaSlot(K, "small")
        gpreB = Buf("gpre")
        pB = Buf("params")
        K.dma(SP, DmaSlot(K, "gpre"), T["g_pre"], gains_d[:, 0, :], writes=[gpreB])

        WxB = [Buf("Wx0"), Buf("Wx1")]
        WzB = Buf("Wz")
        wslots = [DmaSlot(K, "w%d" % i) for i in range(4)]
        WdtB = Buf("Wdt")
        K.dma(POOL, wslots[0], T["Wx0"], win_d[:, :, 512:1024], writes=[WxB[0]])
        K.dma(POOL, wslots[3], T["Wdt"], win_d[:, :, 1536:1544], writes=[WdtB])

        onesB = Buf("ones")
        K.do(POOL, lambda: nc.gpsimd.memset(T["onesf"], 1.0), writes=[onesB])
        K.do(POOL, lambda: nc.gpsimd.memset(T["zerob"], 0.0), writes=[cB])
        K.do(POOL, lambda: nc.gpsimd.memset(T["cs0"][:, 0:3], 0.0), writes=[cB])
        K.do(POOL, lambda: nc.gpsimd.memset(T["cs1"][:, 0:3], 0.0), writes=[cB])

        def asel(name, pattern, cmp, base, cm, src="onesf", fill=0.0):
            K.do(POOL, lambda: nc.gpsimd.affine_select(out=T[name], in_=T[src], pattern=pattern, compare_op=cmp,
                                                        fill=fill, base=base, channel_multiplier=cm),
                 reads=[onesB], writes=[cB])

        asel("ident", [[1, 128]], ALU.is_equal, 0, -1)
        asel("UT", [[1, 128]], ALU.is_ge, 0, -1)
        asel("LTs", [[-1, 128]], ALU.is_gt, 0, 1)
        asel("triI", [[-1, 128]], ALU.is_ge, 0, 1)
        asel("triC", [[1, 128]], ALU.is_gt, 0, -1)
        asel("mks", [[1, 128]], ALU.is_gt, 0, -1)
        asel("identb", [[1, 128]], ALU.is_equal, 0, -1)
        asel("negm", [[1, 128]], ALU.is_gt, 0, -1, src="zerob", fill=-30000.0)
        bdB = Buf("bd")
        K.do(POOL, lambda: nc.gpsimd.memset(T["BD"], 0.0), writes=[bdB])
        K.do(POOL, lambda: nc.gpsimd.memset(T["BD"][0:64, 0:64], 1.0 / 64), writes=[bdB])
        K.do(POOL, lambda: nc.gpsimd.memset(T["BD"][64:128, 64:128], 1.0 / 64), writes=[bdB])


        ident = T["ident"]
        hT = T["hT"]
        xin = [T["xin0"], T["xin1"], T["xin2"]]
        hn = hn_a0 = [T["hn0"], T["hn1"]]
        xinB = [Buf("xin0"), Buf("xin1"), Buf("xin2")]
        hnB = hnB_a0 = [Buf("hn0"), Buf("hn1")]
        xslot = [DmaSlot(K, "x0"), DmaSlot(K, "x1"), DmaSlot(K, "x2")]
        stat = T["stat"]
        hTB = [[Buf("hT%d_%d" % (i, h)) for h in range(2)] for i in range(NT)]
        statB = [Buf("stat%d" % i) for i in range(NT)]

        def rms_tile(i, src_ap, srcB, s, gain_ap, gainB, junk_ap, junkB, n=D, hn=None, hnB=None):
            hn = hn if hn is not None else hn_a0
            hnB = hnB if hnB is not None else hnB_a0
            sb = statB[i % NT]
            c = (i % NT)
            K.do(ACT, lambda: nc.scalar.activation(out=junk_ap, in_=src_ap, func=AF.Square,
                                                   accum_out=stat[:, c:c + 1]),
                 reads=[srcB], writes=[junkB, sb])
            K.do(ACT, lambda: nc.scalar.activation(out=stat[:, 16 + c:17 + c], in_=stat[:, c:c + 1], func=AF.Ln,
                                                   bias=EPS, scale=1.0 / n), writes=[sb])
            K.do(ACT, lambda: nc.scalar.activation(out=stat[:, 32 + c:33 + c], in_=stat[:, 16 + c:17 + c],
                                                   func=AF.Exp, scale=-0.5), writes=[sb])
            K.do(DVE, lambda: nc.vector.scalar_tensor_tensor(out=hn[s], in0=src_ap, scalar=stat[:, 32 + c:33 + c],
                                                             in1=gain_ap, op0=ALU.mult, op1=ALU.mult),
                 reads=[srcB, sb, gainB], writes=[hnB[s]])

        def transpose_to(i, s, dstT, dstB, bk0, hn=None, hnB=None, all_act=False, part="both"):
            hn = hn if hn is not None else hn_a0
            hnB = hnB if hnB is not None else hnB_a0
            for hb in range(2):
                bk = bk0 + hb
                bkb = bank[bk].bitcast(BF16)
                if part in ("both", "pe"):
                    K.do(PE, [(lambda c=c: nc.tensor.transpose(bkb[:, (c % 4) * 128:(c % 4 + 1) * 128],
                                                               hn[s][:, c * 128:(c + 1) * 128], T["identb"]))
                              for c in range(4 * hb, 4 * hb + 4)],
                         reads=[hnB[s], cB], writes=[bankB[bk]])
                if part in ("both", "evac"):
                    src = bkb[:, 0:512].rearrange("p (c t) -> p c t", t=128)
                    dst = dstT[:, 4 * hb:4 * hb + 4, i * 128:(i + 1) * 128]
                    if hb == 0 or all_act:
                        K.do(ACT, lambda: nc.scalar.copy(out=dst, in_=src), writes=[bankB[bk], dstB[i][hb]])
                    else:
                        K.do(DVE, lambda: nc.vector.tensor_copy(out=dst, in_=src), writes=[bankB[bk], dstB[i][hb]])

        def A0_load(i):
            s3 = i % 3
            K.dma(SP, xslot[s3], xin[s3], x_d[i * 128:(i + 1) * 128, :], writes=[xinB[s3]])

        def A0_front(i):
            s = i % 2
            s3 = i % 3
            if i + 2 < NT:
                A0_load(i + 2)
            if i == 1:
                for nm, src in (("convp", convp_d[:, :, :]), ("smallp", smallp_d[:, :, :]), ("gssd", gssd_d[:, :]),
                                ("sbg", sbg_d[:, :])):
                    K.dma(SP, small_slot, T[nm], src)
                pB.w = (small_slot.sem, small_slot.cnt)
            rms_tile(i, xin[s3], xinB[s3], s, T["g_pre"], gpreB, hn[s], hnB[s])

        smallp = T["smallp"]
        dtall, adtall, eaT, scx, dtds = T["dtall"], T["adtall"], T["eaT"], T["scx"], T["dtds"]
        UT, LTs, onesf = T["UT"], T["LTs"], T["onesf"]
        Wz, Wdt = T["Wz"], T["Wdt"]
        preB = Buf("pre")

        def dt_proj(i):
            K.do(PE, [(lambda c=c: nc.tensor.matmul(bank[7][:, i * 8:(i + 1) * 8], lhsT=hT[:, c, i * 128:(i + 1) * 128],
                                                    rhs=Wdt[:, c, :], start=(c == 0), stop=(c == 7))) for c in range(8)],
                 reads=[WdtB, hTB[i][0], hTB[i][1]], writes=[bankB[7]])

        def prepass1():
            K.do(DVE, lambda: nc.vector.tensor_tensor(out=dtall, in0=bank[7][:, 0:128], in1=smallp[:, 0, :], op=ALU.add),
                 reads=[cB, pB], writes=[bankB[7], preB])
            K.do(ACT, lambda: nc.scalar.activation(out=dtall, in_=dtall, func=AF.Exp), writes=[preB])
            K.do(ACT, lambda: nc.scalar.activation(out=dtall, in_=dtall, func=AF.Ln, bias=1.0), writes=[preB])
            K.do(ACT, lambda: nc.scalar.activation(out=eaT, in_=smallp[:, 1, :], func=AF.Exp), reads=[cB, pB], writes=[preB])
            K.do(DVE, lambda: nc.vector.scalar_tensor_tensor(out=adtall, in0=dtall, scalar=-1.0, in1=eaT,
                                                             op0=ALU.mult, op1=ALU.mult), writes=[preB])

        convp = T["convp"]
        cs = [T["cs0"], T["cs1"]]
        acc = [T["acc0"], T["acc1"]]
        csB = [[Buf("cs%d_%d" % (a, t)) for t in range(4)] for a in range(2)]
        accB = [[Buf("acc%d_%d" % (a, t)) for t in range(4)] for a in range(2)]
        sil2 = [T["sil0"], T["sil1"]]
        silB2 = [[Buf("sil%d_%d" % (a_, t)) for t in range(4)] for a_ in range(2)]
        Wx = [T["Wx0"], T["Wx1"]]
        xs_tm, bmT, cmT, bm_tm = T["xs_tm"], T["bmT"], T["cmT"], T["bm_tm"]
        xsB = [Buf("xs%d" % i) for i in range(NT)]
        bmTB = [Buf("bmT%d" % g) for g in range(2)]
        cmTB = [Buf("cmT%d" % g) for g in range(2)]
        bmtmB = [Buf("bmtm%d" % i) for i in range(NT)]
        rot = [0, 0]

        def A1_front_t(j, t):
            blk, co, a = j // 4, (j % 4) * 128, j % 2
            bk = 4 + rot[0] % 3
            rot[0] += 1
            K.do(PE, [(lambda c=c: nc.tensor.matmul(bank[bk], lhsT=Wx[blk][:, c, co:co + 128],
                                                    rhs=hT[:, c, t * 512:(t + 1) * 512],
                                                    start=(c == 0), stop=(c == 7))) for c in range(8)],
                 reads=[WxB[blk]] + [hTB[i][h] for i in range(4 * t, 4 * t + 4) for h in range(2)],
                 writes=[bankB[bk]])
            K.do(ACT, [lambda: nc.scalar.copy(out=cs[a][:, 3 + t * 512:3 + (t + 1) * 512], in_=bank[bk]),
                       lambda: nc.scalar.activation(out=acc[a][:, t * 512:(t + 1) * 512], in_=bank[bk],
                                                    func=AF.Identity, scale=convp[:, j, 3:4],
                                                    bias=convp[:, j, 4:5])],
                 reads=[cB, pB], writes=[bankB[bk], csB[a][t], accB[a][t]])

        def A1_tap(j, t, sh):
            a = j % 2
            K.do(DVE, lambda: nc.vector.scalar_tensor_tensor(
                out=acc[a][:, t * 512:(t + 1) * 512],
                in0=cs[a][:, 3 - sh + t * 512:3 - sh + (t + 1) * 512],
                scalar=convp[:, j, 3 - sh:4 - sh],
                in1=acc[a][:, t * 512:(t + 1) * 512], op0=ALU.mult, op1=ALU.add),
                reads=[csB[a][t]] + ([csB[a][t - 1]] if t > 0 else []), writes=[accB[a][t]])

        def A1_front(j):
            for t in range(4):
                A1_front_t(j, t)
            for sh in (1, 2, 3):
                for t in range(4):
                    A1_tap(j, t, sh)

        def A1_silu(j):
            a = j % 2
            sil, silB = sil2[a], silB2[a]
            for t in range(4):
                if j < 4:
                    dst, dB = sil[:, t * 512:(t + 1) * 512], [silB[t]]
                elif j < 6:
                    dst, dB = bmT[:, j - 4, t * 512:(t + 1) * 512], [bmTB[j - 4]]
                else:
                    dst, dB = cmT[:, j - 6, t * 512:(t + 1) * 512], [cmTB[j - 6]]
                K.do(ACT, lambda: nc.scalar.activation(out=dst, in_=acc[a][:, t * 512:(t + 1) * 512], func=AF.Silu),
                     reads=[accB[a][t]], writes=dB)

        def A1_back(j):
            sil, silB = sil2[j % 2], silB2[j % 2]
            if j < 6:
                for q in range(4):
                    bk = rot[1] % 4
                    rot[1] += 1
                    if j >= 4:
                        bkb = bank[bk].bitcast(BF16)
                        K.do(PE, [(lambda r=r: nc.tensor.transpose(bkb[:, r * 128:(r + 1) * 128],
                                                                   bmT[:, j - 4, (4 * q + r) * 128:(4 * q + r + 1) * 128],
                                                                   T["identb"])) for r in range(4)],
                             reads=[bmTB[j - 4], cB], writes=[bankB[bk]])
                        src = bkb[:, 0:512].rearrange("p (r c) -> p r c", c=128)
                    else:
                        K.do(PE, [(lambda r=r: nc.tensor.transpose(bank[bk][:, r * 128:(r + 1) * 128],
                                                                   sil[:, (4 * q + r) * 128:(4 * q + r + 1) * 128],
                                                                   ident)) for r in range(4)],
                             reads=[silB[q], cB], writes=[bankB[bk]])
                        src = bank[bk].rearrange("p (r c) -> p r c", c=128)
                    if j < 4:
                        dst = xs_tm[:, 4 * q:4 * q + 4, j * 128:(j + 1) * 128]
                        dB = [xsB[i] for i in range(4 * q, 4 * q + 4)]
                    else:
                        dst = bm_tm[:, 4 * q:4 * q + 4, (j - 4) * 128:(j - 3) * 128]
                        dB = [bmtmB[i] for i in range(4 * q, 4 * q + 4)]
                    if q % 2 == 0:
                        K.do(DVE, lambda: nc.vector.tensor_copy(out=dst, in_=src), writes=[bankB[bk]] + dB)
                    else:
                        K.do(ACT, lambda: nc.scalar.copy(out=dst, in_=src), writes=[bankB[bk]] + dB)


        def prepass2():
            K.do(PE, [(lambda q=q, m=m: nc.tensor.matmul(bank[1][:, q * 128:(q + 1) * 128], lhsT=m, rhs=adtall,
                                                         start=True, stop=True)) for q, m in enumerate((LTs, onesf, UT))],
                 reads=[preB, cB, onesB], writes=[bankB[1]])
            K.do(ACT, lambda: nc.scalar.activation(out=scx, in_=bank[1][:, 0:384], func=AF.Exp), writes=[bankB[1], preB])
            K.do(DVE, lambda: nc.vector.tensor_tensor(out=dtds, in0=dtall, in1=scx[:, 0:128], op=ALU.mult), writes=[preB])

        A0_load(0)
        A0_load(1)
        A0_front(0)
        for i in range(NT):
            if i + 1 < NT:
                A0_front(i + 1)
            transpose_to(i, i % 2, hT, hTB, 2 * (i % 2))
            if i == 5:
                K.dma(POOL, wslots[1], T["Wx1"], win_d[:, :, 1024:1536], reads=[hTB[i][0]], writes=[WxB[1]])
            if i == 11:
                K.dma(POOL, wslots[2], T["Wz"], win_d[:, :, 0:512], reads=[hTB[i][0]], writes=[WzB])
            if i >= 2:
                dt_proj(i - 2)
            if i % 4 == 3:
                t = i // 4
                for j in (0, 1):
                    A1_front_t(j, t)
                if t >= 1:
                    for j in (0, 1):
                        for sh in (1, 2, 3):
                            A1_tap(j, t - 1, sh)
        for j in (0, 1):
            for sh in (1, 2, 3):
                A1_tap(j, 3, sh)
        dt_proj(NT - 2)
        dt_proj(NT - 1)
        prepass1()
        A1_silu(0)
        A1_silu(1)
        prepass2()
        for j in range(8):
            if j + 2 < 8:
                A1_front(j + 2)
            A1_back(j)
            if j + 2 < 8:
                A1_silu(j + 2)

        ycS = ycA = None
        if dbg != "A1":
            K.barrier(skip=(PE,))
            ycS, ycA = T["ycS"], T["ycA"]
            ycB = [[Buf("yc%d_%d" % (i, h)) for h in range(2)] for i in range(NT)]
            WqB, WkB, WvB = Buf("Wq"), Buf("Wk"), Buf("Wv")
            K.dma(POOL, wslots[2], T["Wv"], win_d[:, :, 2568:3080], writes=[WvB])
            K.dma(POOL, wslots[0], T["Wq"], win_d[:, :, 1544:2056], writes=[WqB])
            vB = [Buf("v%d" % i) for i in range(NT)]

            def vproj_pe(i, bk=7):
                K.do(PE, [(lambda c=c: nc.tensor.matmul(bank[bk], lhsT=hT[:, c, i * 128:(i + 1) * 128], rhs=T["Wv"][:, c, :],
                                                        start=(c == 0), stop=(c == 7))) for c in range(8)],
                     reads=[WvB], writes=[bankB[bk]])

            def vproj_evac(i, bk=7, on_act=True):
                if on_act:
                    K.do(ACT, lambda: nc.scalar.copy(out=T["v_tm"][:, i, :], in_=bank[bk]), writes=[bankB[bk], vB[i]])
                else:
                    K.do(DVE, lambda: nc.vector.tensor_copy(out=T["v_tm"][:, i, :], in_=bank[bk]), writes=[bankB[bk], vB[i]])
            state = T["state"]
            stbf = [T["stbf0"], T["stbf1"]]
            stateB = Buf("state")
            sttmpB = Buf("sttmp")
            stbfB = [Buf("stbf0"), Buf("stbf1")]
            K.do(POOL, lambda: nc.gpsimd.memset(state, 0.0), writes=[stateB])
            K.do(POOL, lambda: nc.gpsimd.memset(stbf[0], 0.0), writes=[stbfB[0]])
            Ah = [T["Ah0"], T["Ah1"]]; dec = [T["dec0"], T["dec1"]]; cbm = [T["cbm0"], T["cbm1"]]
            MT = [T["MT0"], T["MT1"]]; xdt = [T["xdt0"], T["xdt1"]]; xds = [T["xds0"], T["xds1"]]
            yn = [T["yn0"], T["yn1"]]
            AhB = [Buf(), Buf()]; decB = [[Buf(), Buf()], [Buf(), Buf()]]; cbmB = [Buf(), Buf()]
            MTB = [[Buf(), Buf()], [Buf(), Buf()]]; xdtB = [Buf(), Buf()]; xdsB = [Buf(), Buf()]; ynB = [Buf(), Buf()]
            ez, t1, t2, t3, yg = (T[n] for n in ("ez", "t1", "t2", "t3", "yg"))
            zz = [T["zz"], T["zz"]]
            ezB, t1B, t2B, t3B, ygB = (Buf(n) for n in ("ez", "t1", "t2", "t3", "yg"))
            zzB = [Buf("zz")] * 2
            gssd = T["gssd"]
            BZ, BS0, BS1, BYO, BCS = 0, 1, 2, 5, 6
            BCB, BT = BS0, BS1
            BY = [3, 4]

            def xs3_(i):
                return xs_tm[:, i, :].rearrange("p (h q) -> p h q", q=64)

            def fA(i):
                b = i % 2
                tok = slice(i * 128, (i + 1) * 128)
                K.do(PE, [(lambda c=c: nc.tensor.matmul(bank[BZ], lhsT=hT[:, c, tok], rhs=Wz[:, c, :],
                                                        start=(c == 0), stop=(c == 7))) for c in range(8)],
                     reads=[WzB], writes=[bankB[BZ]])
                K.do(ACT, [(lambda h=h: nc.scalar.activation(out=Ah[b][:, h, :], in_=UT, func=AF.Identity,
                                                             scale=adtall[:, i * 8 + h:i * 8 + h + 1])) for h in range(8)],
                     reads=[preB, cB], writes=[AhB[b]])
                for hh in range(2):
                    bs = BS0 + hh
                    K.do(PE, lambda: nc.tensor.matmul(bank[bs], lhsT=LTs,
                                                      rhs=Ah[b][:, 4 * hh:4 * hh + 4, :].rearrange("p h l -> p (h l)"),
                                                      start=True, stop=True),
                         reads=[AhB[b], cB], writes=[bankB[bs]])
                    K.do(ACT, lambda: nc.scalar.activation(out=dec[b][:, 4 * hh:4 * hh + 4, :],
                                                           in_=bank[bs].rearrange("p (h l) -> p h l", l=128), func=AF.Exp),
                         writes=[bankB[bs], decB[b][hh]])
                K.do(PE, [(lambda g=g: nc.tensor.matmul(bank[BCB][:, g * 128:(g + 1) * 128], lhsT=bmT[:, g, tok],
                                                        rhs=cmT[:, g, tok], start=True, stop=True)) for g in range(2)],
                     reads=bmTB + cmTB, writes=[bankB[BCB]])
                K.do(ACT, lambda: nc.scalar.activation(out=ez, in_=bank[BZ], func=AF.Exp, scale=-1.0),
                     writes=[bankB[BZ], ezB])
                K.do(ACT, lambda: nc.scalar.activation(out=ez, in_=ez, func=AF.Ln, bias=1.0), writes=[ezB])
                K.do(ACT, lambda: nc.scalar.activation(out=ez, in_=ez, func=AF.Exp, scale=-1.0), writes=[ezB])

            def fB(i):
                b = i % 2
                K.do(DVE, lambda: nc.vector.tensor_tensor(out=cbm[b], in0=bank[BCB][:, 0:256].rearrange("p (g l) -> p g l", l=128),
                                                          in1=UT.unsqueeze(1).to_broadcast([128, 2, 128]), op=ALU.mult),
                     reads=[cB], writes=[bankB[BCB], cbmB[b]])
                for g in range(2):
                    K.do(DVE, lambda: nc.vector.tensor_tensor(out=MT[b][:, 4 * g:4 * g + 4, :], in0=dec[b][:, 4 * g:4 * g + 4, :],
                                                              in1=cbm[b][:, g, :].unsqueeze(1).to_broadcast([128, 4, 128]),
                                                              op=ALU.mult),
                         reads=[decB[b][g], cbmB[b]], writes=[MTB[b][g]])
                K.do(DVE, lambda: nc.vector.tensor_tensor(out=xdt[b].rearrange("p (h q) -> p h q", q=64), in0=xs3_(i),
                                                          in1=dtall[:, i * 8:(i + 1) * 8].unsqueeze(2).to_broadcast([128, 8, 64]),
                                                          op=ALU.mult),
                     reads=[xsB[i], preB], writes=[xdtB[b]])
                K.do(DVE, lambda: nc.vector.tensor_tensor(out=xds[b].rearrange("p (h q) -> p h q", q=64), in0=xs3_(i),
                                                          in1=dtds[:, i * 8:(i + 1) * 8].unsqueeze(2).to_broadcast([128, 8, 64]),
                                                          op=ALU.mult),
                     reads=[xsB[i], preB], writes=[xdsB[b]])
                K.do(PE, [(lambda h=h: nc.tensor.matmul(bank[BY[b]][:, h * 64:(h + 1) * 64], lhsT=MT[b][:, h, :],
                                                        rhs=xdt[b][:, h * 64:(h + 1) * 64], start=True, stop=True))
                          for h in range(8)],
                     reads=[MTB[b][0], MTB[b][1], xdtB[b]], writes=[bankB[BY[b]]])

            def fC(i):
                b = i % 2
                K.do(DVE, lambda: nc.vector.tensor_tensor(out=zz[b], in0=bank[BZ], in1=ez, op=ALU.mult),
                     reads=[ezB], writes=[bankB[BZ], zzB[b]])

            def sP(i):
                b = i % 2
                pp = i % 2
                tok = slice(i * 128, (i + 1) * 128)
                K.do(PE, [(lambda g=g: nc.tensor.matmul(bank[BYO][:, g * 256:(g + 1) * 256], lhsT=cmT[:, g, tok],
                                                        rhs=stbf[pp][:, g * 256:(g + 1) * 256], start=True, stop=True))
                          for g in range(2)],
                     reads=cmTB + [stbfB[pp]], writes=[bankB[BYO]])
                K.do(PE, [(lambda g=g: nc.tensor.matmul(bank[BCS][:, g * 256:(g + 1) * 256],
                                                        lhsT=bm_tm[:, i, g * 128:(g + 1) * 128],
                                                        rhs=xds[b][:, g * 256:(g + 1) * 256], start=True, stop=True))
                          for g in range(2)],
                     reads=[bmtmB[i], xdsB[b]], writes=[bankB[BCS]])

            def st(i):
                pp = i % 2
                K.do(DVE, lambda: nc.vector.tensor_tensor(out=state.rearrange("p (h q) -> p h q", q=64),
                                                          in0=state.rearrange("p (h q) -> p h q", q=64),
                                                          in1=scx[:, 128 + i * 8:128 + (i + 1) * 8].unsqueeze(2).to_broadcast([128, 8, 64]),
                                                          op=ALU.mult),
                     reads=[preB], writes=[stateB])
                K.do(DVE, lambda: nc.vector.tensor_tensor(out=state, in0=bank[BCS], in1=state, op=ALU.add),
                     writes=[bankB[BCS], stateB])
                K.do(ACT, lambda: nc.scalar.copy(out=stbf[1 - pp], in_=state), reads=[stateB], writes=[stbfB[1 - pp]])

            def eA(i):
                b = i % 2
                K.do(DVE, lambda: nc.vector.tensor_tensor(out=t1.rearrange("p (h q) -> p h q", q=64),
                                                          in0=bank[BYO].rearrange("p (h q) -> p h q", q=64),
                                                          in1=scx[:, 256 + i * 8:256 + (i + 1) * 8].unsqueeze(2).to_broadcast([128, 8, 64]),
                                                          op=ALU.mult),
                     reads=[preB], writes=[bankB[BYO], t1B])
                K.do(DVE, lambda: nc.vector.tensor_tensor(out=t2, in0=bank[BY[b]], in1=t1, op=ALU.add),
                     reads=[t1B], writes=[bankB[BY[b]], t2B])
                K.do(DVE, lambda: nc.vector.tensor_tensor(out=t3.rearrange("p (h q) -> p h q", q=64), in0=xs3_(i),
                                                          in1=smallp[:, 2, 0:8].unsqueeze(2).to_broadcast([128, 8, 64]),
                                                          op=ALU.mult),
                     reads=[xsB[i], cB, pB], writes=[t3B])
                K.do(DVE, lambda: nc.vector.tensor_tensor(out=t2, in0=t2, in1=t3, op=ALU.add),
                     reads=[t3B], writes=[t2B])
                K.do(DVE, lambda: nc.vector.tensor_tensor(out=yg, in0=t2, in1=zz[b], op=ALU.mult),
                     reads=[t2B, zzB[b]], writes=[ygB])

            def eS(i):
                sb = statB[i]
                c0 = 64 + 4 * i
                K.do(ACT, [(lambda g=g: nc.scalar.activation(out=t1[:, g * 256:(g + 1) * 256], in_=yg[:, g * 256:(g + 1) * 256],
                                                             func=AF.Square, accum_out=stat[:, c0 + g:c0 + g + 1]))
                           for g in range(2)],
                     reads=[ygB], writes=[t1B, sb])
                K.do(ACT, lambda: nc.scalar.activation(out=stat[:, c0:c0 + 2], in_=stat[:, c0:c0 + 2], func=AF.Ln,
                                                       bias=EPS, scale=1.0 / 256), writes=[sb])
                K.do(ACT, lambda: nc.scalar.activation(out=stat[:, c0:c0 + 2], in_=stat[:, c0:c0 + 2], func=AF.Exp,
                                                       scale=-0.5), writes=[sb])

            def eB(i):
                b = i % 2
                sb = statB[i]
                c0 = 64 + 4 * i
                tok = slice(i * 128, (i + 1) * 128)
                K.do(DVE, [(lambda g=g: nc.vector.scalar_tensor_tensor(out=yn[b][:, g * 256:(g + 1) * 256],
                                                                       in0=yg[:, g * 256:(g + 1) * 256],
                                                                       scalar=stat[:, c0 + g:c0 + g + 1],
                                                                       in1=gssd[:, g * 256:(g + 1) * 256],
                                                                       op0=ALU.mult, op1=ALU.mult)) for g in range(2)],
                     reads=[ygB, sb, cB, pB], writes=[ynB[b]])
                btb = bank[BT].bitcast(BF16)
                K.do(PE, [(lambda c=c: nc.tensor.transpose(btb[:, c * 128:(c + 1) * 128], yn[b][:, c * 128:(c + 1) * 128], T["identb"]))
                          for c in range(4)],
                     reads=[ynB[b], cB], writes=[bankB[BT]])
                K.do(ACT, lambda: nc.scalar.copy(out=ycS[:, :, tok], in_=btb[:, 0:512].rearrange("p (c t) -> p c t", t=128)),
                     writes=[bankB[BT], ycB[i][0]])

            fA(0); fB(0); fC(0); sP(0)
            for i in range(NT):
                nx = i + 1 < NT
                if nx:
                    fA(i + 1)
                if i >= 1:
                    vproj_pe(i - 1)
                    vproj_evac(i - 1)
                eA(i)
                eS(i)
                st(i)
                if nx:
                    fB(i + 1)
                eB(i)
                if nx:
                    fC(i + 1)
                    sP(i + 1)

        if dbg not in ("A1", "A2"):
            K.barrier(skip=(PE,))
            qpad, kT, v_tm = T["qpad"], T["kT"], T["v_tm"]
            Wq, Wk, Wv, Wout = T["Wq"], T["Wk"], T["Wv"], T["Wout"]
            WoutB = Buf("Wout")
            K.dma(POOL, wslots[1], T["Wk"], win_d[:, :, 2056:2568], writes=[WkB])
            K.dma(POOL, wslots[3], Wout, wout_d[:, :, :], writes=[WoutB])
            qB = [[Buf("q%d_%d" % (j, t)) for t in range(4)] for j in range(4)]
            kB = [[Buf("k%d_%d" % (j, t)) for t in range(4)] for j in range(4)]
            qz = Buf("qzero")
            for j in range(4):
                K.do(POOL, [lambda: nc.gpsimd.memset(qpad[64:128, j, 0, :], 0.0),
                            lambda: nc.gpsimd.memset(qpad[0:64, j, 1, :], 0.0)], writes=[qz])
                for t in range(4):
                    qB[j][t].w = qz.w
            tb = lambda t: slice(t * 512, (t + 1) * 512)
            rot2 = [0]

            def proj_q(j, t, banks, act_ok):
                bk = banks[rot2[0] % len(banks)]; rot2[0] += 1
                K.do(PE, [(lambda c=c: nc.tensor.matmul(bank[bk], lhsT=Wq[:, c, j * 128:(j + 1) * 128], rhs=hT[:, c, tb(t)],
                                                        start=(c == 0), stop=(c == 7))) for c in range(8)],
                     reads=[WqB], writes=[bankB[bk]])
                if act_ok:
                    K.do(ACT, lambda: nc.scalar.activation(out=qpad[0:64, j, 0, tb(t)], in_=bank[bk][0:64, :], func=AF.Copy, scale=0.125),
                         writes=[bankB[bk], qB[j][t]])
                else:
                    K.do(DVE, lambda: nc.vector.tensor_scalar_mul(out=qpad[0:64, j, 0, tb(t)], in0=bank[bk][0:64, :], scalar1=0.125),
                         writes=[bankB[bk], qB[j][t]])
                K.do(DVE, lambda: nc.vector.tensor_scalar_mul(out=qpad[64:128, j, 1, tb(t)], in0=bank[bk][64:128, :], scalar1=0.125),
                     writes=[bankB[bk], qB[j][t]])

            def proj_k(j, t, banks, act_ok):
                bk = banks[rot2[0] % len(banks)]; rot2[0] += 1
                K.do(PE, [(lambda c=c: nc.tensor.matmul(bank[bk], lhsT=Wk[:, c, j * 128:(j + 1) * 128], rhs=hT[:, c, tb(t)],
                                                        start=(c == 0), stop=(c == 7))) for c in range(8)],
                     reads=[WkB], writes=[bankB[bk]])
                if act_ok:
                    K.do(ACT, lambda: nc.scalar.copy(out=kT[:, j, tb(t)], in_=bank[bk]), writes=[bankB[bk], kB[j][t]])
                else:
                    K.do(DVE, lambda: nc.vector.tensor_copy(out=kT[:, j, tb(t)], in_=bank[bk]), writes=[bankB[bk], kB[j][t]])

            allb = list(range(8))
            for t in range(4):
                proj_q(0, t, allb, True)
            for t in range(4):
                proj_k(0, t, allb, True)
            for i in range(NT - 1, NT):
                bk = allb[rot2[0] % 8]; rot2[0] += 1
                vproj_pe(i, bk)
                vproj_evac(i, bk, on_act=False)
            def proj_piece(kind, j, t, c0_, c1_):
                W_, WB_ = (Wq, WqB) if kind == "q" else (Wk, WkB)
                K.do(PE, [(lambda c=c: nc.tensor.matmul(bank[3], lhsT=W_[:, c, j * 128:(j + 1) * 128], rhs=hT[:, c, tb(t)],
                                                        start=(c == 0), stop=(c == 7))) for c in range(c0_, c1_)],
                     reads=[WB_], writes=[bankB[3]])
                if c1_ == 8:
                    if kind == "q":
                        K.do(DVE, lambda: nc.vector.tensor_scalar_mul(out=qpad[0:64, j, 0, tb(t)], in0=bank[3][0:64, :], scalar1=0.125),
                             writes=[bankB[3], qB[j][t]])
                        K.do(DVE, lambda: nc.vector.tensor_scalar_mul(out=qpad[64:128, j, 1, tb(t)], in0=bank[3][64:128, :], scalar1=0.125),
                             writes=[bankB[3], qB[j][t]])
                    else:
                        K.do(DVE, lambda: nc.vector.tensor_copy(out=kT[:, j, tb(t)], in_=bank[3]), writes=[bankB[3], kB[j][t]])

            pending = {j: [(kind, j, t, c, c + 2) for kind in ("q", "k") for t in range(4) for c in range(0, 8, 2)]
                       for j in (1, 2, 3)}

            triI, triC, mks, zerob, BD, sbg = T["triI"], T["triC"], T["mks"], T["zerob"], T["BD"], T["sbg"]
            Et = [T["E0"], T["E1"], T["E2"]]
            ut = [T["u0"], T["u1"]]
            spt = [T["sp0"], T["sp1"]]
            wt = [T["w0"], T["w1"]]
            EB = [Buf(), Buf(), Buf()]
            uB = [Buf(), Buf()]
            spB = [Buf(), Buf()]
            wB = [Buf(), Buf()]
            osq, rs = T["osq"], T["rs"]
            osqB, rsB = Buf("osq"), Buf("rs")
            ycB2 = [[Buf() for _ in range(4)] for _ in range(4)]
            ps3 = lambda b0, c0: psum[:, b0 * 512:(b0 + 2) * 512].rearrange("p (h c) -> p h c", c=512)[:, :, c0:512]
            Z0, NB, RB0, OB0 = 0, 2, 4, 6
            steps = []
            for j in range(4):
                for Tq in range(4):
                    nb = 4 * Tq + 4
                    for idx in range(nb):
                        b = nb - 1 - idx
                        joff = b - 4 * Tq
                        steps.append(dict(j=j, Tq=Tq, b=b, idx=idx, nb=nb, diag=(joff >= 0),
                                          c0=(128 * joff if joff > 0 else 0), first=(idx == 0), last=(idx == nb - 1)))
            NS = len(steps)

            def e_Z(n):
                st_ = steps[n]; j, Tq, b, c0, diag = st_["j"], st_["Tq"], st_["b"], st_["c0"], st_["diag"]
                th = []
                for h in range(2):
                    th.append(lambda h=h: nc.tensor.matmul(bank[Z0 + h][:, c0:512], lhsT=kT[:, j, b * 128:(b + 1) * 128],
                                                           rhs=qpad[:, j, h, Tq * 512 + c0:(Tq + 1) * 512],
                                                           start=True, stop=not diag))
                    if diag:
                        th.append(lambda h=h: nc.tensor.matmul(bank[Z0 + h][:, c0:c0 + 128], lhsT=T["identb"], rhs=T["negm"],
                                                               start=False, stop=True))
                K.do(PE, th, reads=[kB[j][b // 4], qB[j][Tq], cB], writes=[bankB[Z0], bankB[Z0 + 1]])

            def e_E(n):
                c0 = steps[n]["c0"]
                K.do(ACT, lambda: nc.scalar.activation(out=Et[n % 3][:, :, c0:512], in_=ps3(Z0, c0), func=AF.Exp),
                     writes=[bankB[Z0], bankB[Z0 + 1], EB[n % 3]])

            def e_sp(n):
                c0, diag, sl = steps[n]["c0"], steps[n]["diag"], n % 2
                K.do(ACT, lambda: nc.scalar.activation(out=spt[sl][:, :, c0:512], in_=Et[n % 3][:, :, c0:512], func=AF.Ln, bias=1.0),
                     reads=[EB[n % 3]], writes=[spB[sl]])

            def e_init(n, which):
                st_ = steps[n]; j, Tq = st_["j"], st_["Tq"]
                b0 = RB0 if which == "R" else OB0
                K.do(PE, [(lambda bk=bk: nc.tensor.matmul(bank[bk], lhsT=zerob, rhs=qpad[:, j, 0, Tq * 512:(Tq + 1) * 512],
                                                          start=True, stop=False, skip_group_check=True)) for bk in (b0, b0 + 1)],
                     reads=[cB, qB[j][Tq]], writes=[bankB[b0], bankB[b0 + 1]])

            def e_tri(n, tri):
                c0, sl = steps[n]["c0"], n % 2
                K.do(PE, [(lambda h=h: nc.tensor.matmul(bank[RB0 + h][:, c0:512], lhsT=tri, rhs=spt[sl][:, h, c0:512],
                                                        start=False, stop=False, skip_group_check=True)) for h in range(2)],
                     reads=[spB[sl], cB], writes=[bankB[RB0], bankB[RB0 + 1]])

            def e_u_w(n):
                c0, diag, sl = steps[n]["c0"], steps[n]["diag"], n % 2
                K.do(ACT, lambda: nc.scalar.activation(out=ut[sl][:, :, c0:512], in_=ps3(RB0, c0), func=AF.Exp, scale=-1.0),
                     writes=[bankB[RB0], bankB[RB0 + 1], uB[sl]])
                K.do(DVE, lambda: nc.vector.tensor_tensor(out=wt[sl][:, :, c0:512], in0=Et[n % 3][:, :, c0:512],
                                                          in1=ut[sl][:, :, c0:512], op=ALU.mult),
                     reads=[EB[n % 3], uB[sl]], writes=[wB[sl]])

            def e_WV(n):
                st_ = steps[n]; j, b, c0, sl = st_["j"], st_["b"], st_["c0"], n % 2
                K.do(PE, [(lambda h=h: nc.tensor.matmul(bank[OB0 + h][:, c0:512], lhsT=v_tm[:, b, j * 128:(j + 1) * 128],
                                                        rhs=wt[sl][:, h, c0:512], start=False, stop=st_["last"],
                                                        skip_group_check=True)) for h in range(2)],
                     reads=[wB[sl], vB[b]], writes=[bankB[OB0], bankB[OB0 + 1]])

            osave = T["osave"]
            osaveB = Buf("osave")

            def e_epiA(n):
                K.do(DVE, lambda: nc.vector.tensor_copy(out=osave[0:64, :], in_=bank[OB0][0:64, :]),
                     writes=[bankB[OB0], osaveB])
                K.do(DVE, lambda: nc.vector.tensor_copy(out=osave[64:128, :], in_=bank[OB0 + 1][64:128, :]),
                     writes=[bankB[OB0 + 1], osaveB])
                K.do(DVE, lambda: nc.vector.tensor_tensor(out=osq, in0=osave, in1=osave, op=ALU.mult),
                     reads=[osaveB], writes=[osqB])

            def e_epiA2(n):
                K.do(PE, lambda: nc.tensor.matmul(bank[NB], lhsT=BD, rhs=osq, start=True, stop=True),
                     reads=[osqB, bdB], writes=[bankB[NB]])

            def e_epiB(n):
                st_ = steps[n]; j, Tq = st_["j"], st_["Tq"]
                qall = slice(Tq * 512, (Tq + 1) * 512)
                K.do(ACT, lambda: nc.scalar.activation(out=rs, in_=bank[NB], func=AF.Ln, bias=EPS), writes=[bankB[NB], rsB])
                K.do(ACT, lambda: nc.scalar.activation(out=rs, in_=rs, func=AF.Exp, scale=-0.5), writes=[rsB])
                for h in range(2):
                    pr = slice(64 * h, 64 * h + 64)
                    K.do(DVE, lambda: nc.vector.scalar_tensor_tensor(out=ycA[pr, j, qall], in0=osave[pr, :],
                                                                     scalar=sbg[pr, j:j + 1], in1=rs[pr, :],
                                                                     op0=ALU.mult, op1=ALU.mult),
                         reads=[rsB, cB, pB, osaveB], writes=[ycB2[j][Tq]])

            e_Z(0); e_E(0); e_Z(1); e_sp(0)
            for n in range(NS):
                st_ = steps[n]
                if st_["first"]:
                    e_init(n, "R")
                e_tri(n, triI)
                if n + 1 < NS:
                    e_E(n + 1)
                if n + 2 < NS:
                    e_Z(n + 2)
                e_u_w(n)
                if n + 1 < NS:
                    e_sp(n + 1)
                if not st_["last"]:
                    e_tri(n, triC)
                if st_["idx"] == 1:
                    e_init(n - 1, "O")
                if n >= 1:
                    e_WV(n - 1)
                    if steps[n - 1]["last"]:
                        e_epiA(n - 1)
                if st_["idx"] == 1 and n >= 2:
                    e_epiA2(n - 2)
                if st_["idx"] == 2 and n >= 3:
                    e_epiB(n - 3)
                nj = st_["j"] + 1
                if nj in pending and pending[nj]:
                    proj_piece(*pending[nj].pop(0))
            e_WV(NS - 1)
            e_epiA(NS - 1)
            e_epiA2(NS - 1)
            e_epiB(NS - 1)

        if dbg not in ("A1", "A2", "A4"):
            K.barrier(skip=(PE,))
            x1, h2T = T["x1"], T["h2T"]
            xb = [T["xb0"], T["xb1"]]
            hn2 = [T["hn20"], T["hn21"]]
            gpm, gpf = T["gpm"], T["gpf"]
            xbB = [Buf(), Buf()]
            hn2B = [Buf(), Buf()]
            gB3 = Buf("gains3")
            x1B = [Buf("x1_%d" % i) for i in range(NT)]
            h2TB = [[Buf() for _ in range(2)] for _ in range(NT)]
            K.dma(SP, small_slot, gpm, gains_d[:, 1, :])
            K.dma(SP, small_slot, gpf, gains_d[:, 2, :])
            gB3.w = (small_slot.sem, small_slot.cnt)
            wgu = [T["wgu0"], T["wgu1"]]
            wguB = [Buf(), Buf()]
            for sl_ in range(2):
                K.dma(POOL, wslots[sl_], wgu[sl_].rearrange("p a b c -> p (a b c)"), wgu_d[sl_, :, :], writes=[wguB[sl_]])
            WdAB = [Buf() for _ in range(3)]
            wdaslots = [DmaSlot(K, "wda%d" % q) for q in range(3)]
            for q, (lo, hi) in enumerate(((0, 4), (4, 8), (8, 11))):
                K.dma(POOL, wdaslots[q], T["WdA"][:, lo:hi, :], wd_d[:, lo:hi, :], writes=[WdAB[q]])
            junk = T["junkb"]
            junkB = Buf("junk")
            ps2 = lambda b0_: psum[:, b0_ * 512:(b0_ + 2) * 512]

            def B_P1(i):
                s = i % 2
                m = 2 * (i % 3)
                tok = slice(i * 128, (i + 1) * 128)
                K.dma(SP, xslot[s], xb[s], x_d[tok, :], writes=[xbB[s]])
                for hh in range(2):
                    bk = m + hh
                    K.do(PE, [(lambda c=c: nc.tensor.matmul(bank[bk], lhsT=(ycS if c < 4 else ycA)[:, c % 4, tok], rhs=Wout[:, c, hh * 512:(hh + 1) * 512],
                                                            start=(c == 0), stop=(c == 7))) for c in range(8)],
                         reads=[WoutB, ycB[i][0], ycB2[0][i // 4], ycB2[1][i // 4], ycB2[2][i // 4], ycB2[3][i // 4]],
                         writes=[bankB[bk]])

            def B_A12(i):
                m = 2 * (i % 3)
                c0 = 128 + 4 * i
                sb = statB[i]
                K.do(ACT, lambda: nc.scalar.activation(out=junk, in_=ps2(m), func=AF.Square, accum_out=stat[:, c0:c0 + 1]),
                     writes=[bankB[m], bankB[m + 1], junkB, sb])
                K.do(ACT, lambda: nc.scalar.activation(out=stat[:, c0:c0 + 1], in_=stat[:, c0:c0 + 1], func=AF.Ln,
                                                       bias=EPS, scale=1.0 / D), writes=[sb])
                K.do(ACT, lambda: nc.scalar.activation(out=stat[:, c0:c0 + 1], in_=stat[:, c0:c0 + 1], func=AF.Exp,
                                                       scale=-0.5), writes=[sb])

            def B_D1(i):
                s = i % 2
                m = 2 * (i % 3)
                c0 = 128 + 4 * i
                sb = statB[i]
                K.do(DVE, lambda: nc.vector.scalar_tensor_tensor(out=x1[:, i, :], in0=ps2(m), scalar=stat[:, c0:c0 + 1],
                                                                 in1=gpm, op0=ALU.mult, op1=ALU.mult),
                     reads=[sb, gB3], writes=[bankB[m], bankB[m + 1], x1B[i]])
                K.do(DVE, lambda: nc.vector.tensor_tensor(out=x1[:, i, :], in0=x1[:, i, :], in1=xb[s], op=ALU.add),
                     reads=[xbB[s]], writes=[x1B[i]])

            def B_A34(i):
                c1 = 128 + 4 * i + 1
                sb = statB[i]
                K.do(ACT, lambda: nc.scalar.activation(out=junk, in_=x1[:, i, :], func=AF.Square, accum_out=stat[:, c1:c1 + 1]),
                     reads=[x1B[i]], writes=[junkB, sb])
                K.do(ACT, lambda: nc.scalar.activation(out=stat[:, c1:c1 + 1], in_=stat[:, c1:c1 + 1], func=AF.Ln,
                                                       bias=EPS, scale=1.0 / D), writes=[sb])
                K.do(ACT, lambda: nc.scalar.activation(out=stat[:, c1:c1 + 1], in_=stat[:, c1:c1 + 1], func=AF.Exp,
                                                       scale=-0.5), writes=[sb])

            def B_D2(i):
                s = i % 2
                c1 = 128 + 4 * i + 1
                K.do(DVE, lambda: nc.vector.scalar_tensor_tensor(out=hn2[s], in0=x1[:, i, :], scalar=stat[:, c1:c1 + 1],
                                                                 in1=gpf, op0=ALU.mult, op1=ALU.mult),
                     reads=[x1B[i], statB[i], gB3], writes=[hn2B[s]])

            for k in range(NT + 3):
                if 0 <= k - 1 < NT:
                    B_A12(k - 1)
                    B_D1(k - 1)
                if 0 <= k - 2 < NT:
                    B_A34(k - 2)
                    B_D2(k - 2)
                if k < NT:
                    B_P1(k)
                if 0 <= k - 3 < NT:
                    transpose_to(k - 3, (k - 3) % 2, h2T, h2TB, 6, hn=hn2, hnB=hn2B, all_act=False, part="pe")
                    transpose_to(k - 3, (k - 3) % 2, h2T, h2TB, 6, hn=hn2, hnB=hn2B, all_act=False, part="evac")

        if dbg is None or dbg == "F":
            K.barrier(skip=(PE,))
            GT, gff = T["GT"], T["gff"]
            Wd_ = lambda f_: (T["WdA"][:, f_, :] if f_ < 11 else T["WdB"][:, f_ - 11, :])
            sg = [T["sg0"], T["sg1"]]
            ft = [T["ft0"], T["ft1"]]
            sgB = [Buf(), Buf()]
            ftB = [Buf(), Buf()]
            WdB = [Buf() for _ in range(11)]
            GTB = [Buf() for _ in range(NFC)]
            gB4 = Buf("gains4")
            K.dma(SP, small_slot, gff, gains_d[:, 3, :], writes=[gB4])
            wdslots = [DmaSlot(K, "wd%d" % q) for q in range(11)]
            wdparts = [(2 * q, 2 * q + 2) for q in range(11)]
            oslot = [DmaSlot(K, "o0"), DmaSlot(K, "o1")]
            seq = [(hf, fc) for hf in range(2) for fc in range(NFC)]

            def load_wgu(k):
                hf, fc = seq[k]
                sl = k % 2
                K.dma(POOL, wslots[sl], wgu[sl].rearrange("p a b c -> p (a b c)"), wgu_d[fc, :, :], writes=[wguB[sl]])

            rotc = 0
            for k, (hf, fc) in enumerate(seq):
                sl = k % 2
                if hf == 0 and fc < 11:
                    K.dma(POOL, wdslots[fc], T["WdB"][:, fc:fc + 1, :], wd_d[:, 11 + fc:12 + fc, :], writes=[WdB[fc]])
                for t in range(2):
                    tcols = slice(hf * 1024 + t * 512, hf * 1024 + (t + 1) * 512)
                    bg = 2 * (rotc % 2)
                    bu = bg + 1
                    ss_ = rotc % 2
                    rotc += 1
                    rd = [wguB[sl]] + [h2TB[i][h] for i in range(hf * 8 + t * 4, hf * 8 + t * 4 + 4) for h in range(2)]
                    K.do(PE, [(lambda c=c: nc.tensor.matmul(bank[bg], lhsT=wgu[sl][:, 0, c, :], rhs=h2T[:, c, tcols],
                                                            start=(c == 0), stop=(c == 7))) for c in range(8)],
                         reads=rd, writes=[bankB[bg]])
                    K.do(PE, [(lambda c=c: nc.tensor.matmul(bank[bu], lhsT=wgu[sl][:, 1, c, :], rhs=h2T[:, c, tcols],
                                                            start=(c == 0), stop=(c == 7))) for c in range(8)],
                         reads=rd, writes=[bankB[bu]])
                    K.do(ACT, lambda: nc.scalar.activation(out=sg[ss_], in_=bank[bg], func=AF.Silu),
                         writes=[bankB[bg], sgB[ss_]])
                    K.do(DVE, lambda: nc.vector.tensor_tensor(out=GT[:, fc, t * 512:(t + 1) * 512], in0=bank[bu], in1=sg[ss_],
                                                              op=ALU.mult),
                         reads=[sgB[ss_]], writes=[bankB[bu], GTB[fc]])
                if k + 2 < len(seq):
                    load_wgu(k + 2)
                if fc == NFC - 1:
                    for il in range(8):
                        i = hf * 8 + il
                        s = il % 2
                        sb = statB[i]
                        c0 = 192 + 4 * i
                        for hh in range(2):
                            bk = 4 + 2 * s + hh
                            K.do(PE, [(lambda f_=f_: nc.tensor.matmul(bank[bk], lhsT=GT[:, f_, il * 128:(il + 1) * 128],
                                                                      rhs=Wd_(f_)[:, hh * 512:(hh + 1) * 512],
                                                                      start=(f_ == 0), stop=(f_ == NFC - 1))) for f_ in range(NFC)],
                                 reads=GTB + WdB + WdAB, writes=[bankB[bk]])
                            K.do(ACT, lambda: nc.scalar.activation(out=ft[hh], in_=bank[bk], func=AF.Square,
                                                                   accum_out=stat[:, c0 + hh:c0 + hh + 1]),
                                 writes=[bankB[bk], ftB[hh], sb])
                        K.do(DVE, lambda: nc.vector.tensor_tensor(out=stat[:, c0 + 2:c0 + 3], in0=stat[:, c0:c0 + 1],
                                                                  in1=stat[:, c0 + 1:c0 + 2], op=ALU.add), writes=[sb])
                        K.do(ACT, lambda: nc.scalar.activation(out=stat[:, c0 + 2:c0 + 3], in_=stat[:, c0 + 2:c0 + 3], func=AF.Ln,
                                                               bias=EPS, scale=1.0 / D), writes=[sb])
                        K.do(ACT, lambda: nc.scalar.activation(out=stat[:, c0 + 2:c0 + 3], in_=stat[:, c0 + 2:c0 + 3], func=AF.Exp,
                                                               scale=-0.5), writes=[sb])
                        for hh in range(2):
                            bk = 4 + 2 * s + hh
                            K.do(DVE, lambda: nc.vector.scalar_tensor_tensor(out=ft[hh], in0=bank[bk], scalar=stat[:, c0 + 2:c0 + 3],
                                                                             in1=gff[:, hh * 512:(hh + 1) * 512],
                                                                             op0=ALU.mult, op1=ALU.mult),
                                 reads=[sb, gB4], writes=[bankB[bk], ftB[hh]])
                            K.do(POOL, lambda: nc.gpsimd.tensor_tensor(out=x1[:, i, hh * 512:(hh + 1) * 512],
                                                                       in0=x1[:, i, hh * 512:(hh + 1) * 512], in1=ft[hh], op=ALU.add),
                                 reads=[ftB[hh]], writes=[x1B[i]])
                        K.dma(SP, oslot[s], out_d[i * 128:(i + 1) * 128, :], x1[:, i, :], reads=[x1B[i]])
            K.barrier()
            for s in range(2):
                SP.wait((oslot[s].sem, oslot[s].cnt))

        fin_slot = DmaSlot(K, "fin")

        def dump(name, ap2d):
            K.dma(SP, fin_slot, dbg_outs[name][:, :], ap2d)

        K.barrier()
        if dbg == "A1":
            dump("d_hT", hT.rearrange("p a b -> p (a b)"))
            dump("d_xs", xs_tm.rearrange("p a b -> p (a b)"))
            dump("d_bmT", bmT.rearrange("p a b -> p (a b)"))
            dump("d_cmT", cmT.rearrange("p a b -> p (a b)"))
            dump("d_bmtm", bm_tm.rearrange("p a b -> p (a b)"))
        if dbg in ("A2", "A4"):
            K.dma(SP, fin_slot, dbg_outs["d_ycat"][:, 0:4 * L], ycS.rearrange("p a b -> p (a b)"))
            if dbg == "A4":
                K.dma(SP, fin_slot, dbg_outs["d_ycat"][:, 4 * L:8 * L], ycA.rearrange("p a b -> p (a b)"))
        if dbg == "B":
            dump("d_x1", x1.rearrange("p a b -> p (a b)"))
            dump("d_h2T", h2T.rearrange("p a b -> p (a b)"))
        if dbg is not None and dbg != "F":
            K.dma(SP, fin_slot, out_d[0:128, :], T["ident"].bitcast(F32)[:, 0:128].to_broadcast([128, 128]) if False else big[:, 0:4096].bitcast(F32))
        if fin_slot.cnt:
            SP.wait((fin_slot.sem, fin_slot.cnt))
    return nc, used


def _layout(inputs):
    f = np.float32
    w_in = np.asarray(inputs["w_in"], f)[0]
    w_out = np.asarray(inputs["w_out"], f)[0]
    wg = np.asarray(inputs["w_gate"], f)[0]
    wu = np.asarray(inputs["w_up"], f)[0]
    wd = np.asarray(inputs["w_down"], f)[0]
    m = {}
    m["w_in"] = np.ascontiguousarray(w_in.reshape(8, 128, DIN).transpose(1, 0, 2))
    m["w_out"] = np.ascontiguousarray(w_out.reshape(8, 128, D).transpose(1, 0, 2))
    g4 = wg.reshape(8, 128, NFC, 128).transpose(2, 1, 0, 3)
    u4 = wu.reshape(8, 128, NFC, 128).transpose(2, 1, 0, 3)
    m["wgu"] = np.ascontiguousarray(np.stack([g4, u4], axis=2).reshape(NFC, 128, 2 * 8 * 128))
    m["wd"] = np.ascontiguousarray(wd.reshape(NFC, 128, D).transpose(1, 0, 2))
    gains = np.stack([np.asarray(inputs[k], f)[0] for k in
                      ("pre_mix_gain", "post_mix_gain", "pre_ffn_gain", "post_ffn_gain")], axis=0)
    m["gains"] = np.ascontiguousarray(np.broadcast_to(gains[None], (128, 4, D)))
    cw = np.asarray(inputs["conv_w"], f)[0]
    cb = np.asarray(inputs["conv_b"], f)[0]
    cp = np.concatenate([cw, cb[None]], axis=0)
    m["convp"] = np.ascontiguousarray(cp.reshape(5, 8, 128).transpose(2, 1, 0))
    sp = np.stack([np.tile(np.asarray(inputs[k], f)[0], 16) for k in ("dt_bias", "a_log", "d_skip")], axis=0)
    m["smallp"] = np.ascontiguousarray(np.broadcast_to(sp[None], (128, 3, 128)))
    m["gssd"] = np.ascontiguousarray(np.broadcast_to(np.asarray(inputs["ssd_norm_gain"], f)[0][None], (128, 512)))
    m["sbg"] = np.ascontiguousarray(np.asarray(inputs["sb_norm_gain"], f)[0].reshape(4, 128).T)
    return m


_CACHE = {}


def kernel(**inputs):
    x = np.asarray(inputs["x"], np.float32)
    B = x.shape[0]
    if "nc" not in _CACHE:
        _CACHE["nc"] = build()[0]
    nc = _CACHE["nc"]
    shared = _layout(inputs)
    in_maps = []
    for b in range(B):
        m = dict(shared)
        m["x"] = np.ascontiguousarray(x[b])
        in_maps.append(m)
    res = run_bass_kernel_spmd(nc, in_maps, core_ids=list(range(B)))
    return np.stack([np.asarray(r["out"], np.float32) for r in res.results], axis=0)
```

```python
import numpy as np
import concourse.bass as bass
import concourse.mybir as mybir
from concourse.bass_utils import run_bass_kernel_spmd
from contextlib import ExitStack

F32 = mybir.dt.float32
BF16 = mybir.dt.bfloat16
U8 = mybir.dt.uint8
AF = mybir.ActivationFunctionType
ALU = mybir.AluOpType

L = 2048
D = 1024
NT = 16
DIN = 3080
DFF = 2816
NFC = 22
EPS = 1e-6
NPH = 5
SBUF_BYTES = 212000


class Eng:
    def __init__(self, K, eng, name):
        self.K = K
        self.e = eng
        self.name = name
        self.sem = K.new_sem("prog_" + name)
        self.cnt = 0
        self.seen = {}

    def wait(self, tok):
        if tok is None:
            return
        sem, val = tok
        key = sem.num
        if self.seen.get(key, 0) >= val:
            return
        self.e.wait_ge(sem, val)
        self.seen[key] = val

    def mark(self, ins):
        self.cnt += 1
        ins.then_inc(self.sem, 1)
        return (self.sem, self.cnt)

    def last(self):
        return (self.sem, self.cnt) if self.cnt else None


class Buf:
    __slots__ = ("w", "r", "name")

    def __init__(self, name=""):
        self.w = None
        self.r = {}
        self.name = name


class DmaSlot:
    def __init__(self, K, name):
        self.sem = K.new_sem("dma_" + name)
        self.cnt = 0


class Builder:
    def __init__(self, nc, es):
        self.nc = nc
        self.es = es
        self.nsem = 0
        self.PE = Eng(self, nc.tensor, "pe")
        self.ACT = Eng(self, nc.scalar, "act")
        self.DVE = Eng(self, nc.vector, "dve")
        self.POOL = Eng(self, nc.gpsimd, "pool")
        self.SP = Eng(self, nc.sync, "sp")
        self.engs = [self.PE, self.ACT, self.DVE, self.POOL, self.SP]
        self.allocs = []
        self.dma_toks = []

    def new_sem(self, name):
        self.nsem += 1
        return self.es.enter_context(self.nc.semaphore(name))

    def do(self, E, thunks, reads=(), writes=()):
        for b in reads:
            E.wait(b.w)
        for b in writes:
            E.wait(b.w)
            for t in b.r.values():
                E.wait(t)
        if callable(thunks):
            thunks = [thunks]
        ins = None
        for th in thunks:
            ins = th()
        tok = E.mark(ins)
        for b in reads:
            b.r[tok[0].num] = tok
        for b in writes:
            b.w = tok
            b.r = {}
        return tok

    def dma(self, E, slot, out, in_, reads=(), writes=()):
        for b in reads:
            E.wait(b.w)
        for b in writes:
            E.wait(b.w)
            for t in b.r.values():
                E.wait(t)
        E.e.dma_start(out=out, in_=in_).then_inc(slot.sem, 16)
        slot.cnt += 16
        tok = (slot.sem, slot.cnt)
        for b in reads:
            b.r[tok[0].num] = tok
        for b in writes:
            b.w = tok
            b.r = {}
        return tok

    def barrier(self, skip=()):
        toks = [e.last() for e in self.engs]
        for e in self.engs:
            if e in skip:
                continue
            for t in toks:
                if t is not None and t[0].num != e.sem.num:
                    e.wait(t)

    def alloc(self, name, shape, dt, p0, p1):
        esz = 4 if dt == F32 else 2
        n = int(np.prod(shape[1:])) * esz
        n = (n + 63) // 64 * 64
        self.allocs.append(dict(name=name, shape=shape, dt=dt, p0=p0, p1=p1, n=n, off=None))
        return len(self.allocs) - 1

    def place(self):
        keys = [lambda a: (-a["n"],),
                lambda a: (-(a["p1"] - a["p0"]), -a["n"]),
                lambda a: (-a["n"] * (a["p1"] - a["p0"] + 1),),
                lambda a: (a["p0"], -a["n"]),
                lambda a: (-a["p1"], -a["n"])]
        err = None
        for key in keys:
            try:
                self._place(key)
                return
            except RuntimeError as e:
                err = e
        raise err

    def _place(self, key):
        for a in self.allocs:
            a["off"] = None
        order = sorted(range(len(self.allocs)), key=lambda i: key(self.allocs[i]))
        placed = []
        for i in order:
            a = self.allocs[i]
            cands = [0] + sorted(b["off"] + b["n"] for b in placed)
            for off in cands:
                ok = True
                for b in placed:
                    if a["p0"] <= b["p1"] and b["p0"] <= a["p1"]:
                        if off < b["off"] + b["n"] and b["off"] < off + a["n"]:
                            ok = False
                            break
                if ok:
                    a["off"] = off
                    break
            placed.append(a)
            if a["off"] + a["n"] > SBUF_BYTES:
                raise RuntimeError("SBUF overflow placing %s (%d + %d)" % (a["name"], a["off"], a["n"]))

    def ap(self, big, idx):
        a = self.allocs[idx]
        shape = a["shape"]
        v = big[0:shape[0], a["off"]:a["off"] + int(np.prod(shape[1:])) * (4 if a["dt"] == F32 else 2)]
        if a["dt"] != U8:
            v = v.bitcast(a["dt"])
        if len(shape) == 3:
            v = v.rearrange("p (a b) -> p a b", b=shape[2])
        elif len(shape) == 4:
            v = v.rearrange("p (a b c) -> p a b c", b=shape[2], c=shape[3])
        return v


def build(dbg=None):
    nc = bass.Bass("TRN2", target_bir_lowering=False)

    def din(name, shape, dt=F32):
        return nc.dram_tensor(name, shape, dt, kind="ExternalInput").ap()

    def dout(name, shape, dt=F32):
        return nc.dram_tensor(name, shape, dt, kind="ExternalOutput").ap()

    x_d = din("x", [L, D])
    win_d = din("w_in", [128, 8, DIN])
    wout_d = din("w_out", [128, 8, D])
    wgu_d = din("wgu", [NFC, 128, 2 * 8 * 128])
    wd_d = din("wd", [128, NFC, D])
    gains_d = din("gains", [128, 4, D])
    convp_d = din("convp", [128, 8, 5])
    smallp_d = din("smallp", [128, 3, 128])
    gssd_d = din("gssd", [128, 512])
    sbg_d = din("sbg", [128, 4])
    out_d = dout("out", [L, D])

    dbg_outs = {}
    if dbg == "A1":
        dbg_outs["d_hT"] = dout("d_hT", [128, 8 * L], BF16)
        dbg_outs["d_xs"] = dout("d_xs", [128, NT * 512], F32)
        dbg_outs["d_bmT"] = dout("d_bmT", [128, 2 * L], BF16)
        dbg_outs["d_cmT"] = dout("d_cmT", [128, 2 * L], BF16)
        dbg_outs["d_bmtm"] = dout("d_bmtm", [128, NT * 256], BF16)
    if dbg in ("A2", "A4"):
        dbg_outs["d_ycat"] = dout("d_ycat", [128, 8 * L], BF16)
    if dbg == "B":
        dbg_outs["d_x1"] = dout("d_x1", [128, NT * D], F32)
        dbg_outs["d_h2T"] = dout("d_h2T", [128, 8 * L], BF16)

    es = ExitStack()
    with es:
        big = es.enter_context(nc.sbuf_tensor("big", [128, SBUF_BYTES], U8))
        psum = es.enter_context(nc.psum_tensor("psum", [128, 4096], F32))
        es.enter_context(nc.Block())
        K = Builder(nc, es)
        PE, ACT, DVE, POOL, SP = K.PE, K.ACT, K.DVE, K.POOL, K.SP
        bank = [psum[:, b * 512:(b + 1) * 512] for b in range(8)]
        bankB = [Buf("bank%d" % b) for b in range(8)]

        A = {}

        def al(name, shape, dt, p0, p1):
            A[name] = K.alloc(name, shape, dt, p0, p1)

        al("ident", [128, 128], F32, 0, 4)
        al("onesf", [128, 128], F32, 0, 1)
        al("UT", [128, 128], F32, 0, 1)
        al("LTs", [128, 128], F32, 0, 1)
        al("BD", [128, 128], F32, 0, 2)
        al("triI", [128, 128], BF16, 0, 2)
        al("triC", [128, 128], BF16, 0, 2)
        al("mks", [128, 128], BF16, 0, 2)
        al("zerob", [128, 128], BF16, 0, 2)
        al("identb", [128, 128], BF16, 0, 3)
        al("negm", [128, 128], BF16, 0, 2)
        al("convp", [128, 8, 5], F32, 0, 0)
        al("smallp", [128, 3, 128], F32, 0, 1)
        al("gssd", [128, 512], F32, 0, 1)
        al("sbg", [128, 4], F32, 0, 2)
        al("stat", [128, 256], F32, 0, 4)
        al("hT", [128, 8, L], BF16, 0, 2)
        al("ycS", [128, 4, L], BF16, 1, 3)
        al("ycA", [128, 4, L], BF16, 2, 3)
        al("xs_tm", [128, NT, 512], F32, 0, 1)
        al("bmT", [128, 2, L], BF16, 0, 1)
        al("cmT", [128, 2, L], BF16, 0, 1)
        al("bm_tm", [128, NT, 256], BF16, 0, 1)
        al("g_pre", [128, D], F32, 0, 0)
        al("xin0", [128, D], F32, 0, 0)
        al("xin1", [128, D], F32, 0, 0)
        al("xin2", [128, D], F32, 0, 0)
        al("hn0", [128, D], BF16, 0, 0)
        al("hn1", [128, D], BF16, 0, 0)
        al("cs0", [128, L + 16], F32, 0, 0)
        al("cs1", [128, L + 16], F32, 0, 0)
        al("acc0", [128, L], F32, 0, 0)
        al("acc1", [128, L], F32, 0, 0)
        al("sil0", [128, L], F32, 0, 0)
        al("sil1", [128, L], F32, 0, 0)
        al("Wx0", [128, 8, 512], BF16, 0, 0)
        al("Wx1", [128, 8, 512], BF16, 0, 0)
        al("Wz", [128, 8, 512], BF16, 0, 1)
        al("Wdt", [128, 8, 8], BF16, 0, 1)

        al("dtall", [128, 128], F32, 0, 1)
        al("adtall", [128, 128], F32, 0, 1)
        al("eaT", [128, 128], F32, 0, 1)
        al("scx", [128, 384], F32, 0, 1)
        al("dtds", [128, 128], F32, 0, 1)
        for b_ in range(2):
            al("Ah%d" % b_, [128, 8, 128], F32, 1, 1)
            al("dec%d" % b_, [128, 8, 128], F32, 1, 1)
            al("cbm%d" % b_, [128, 2, 128], F32, 1, 1)
            al("MT%d" % b_, [128, 8, 128], BF16, 1, 1)
            al("xdt%d" % b_, [128, 512], BF16, 1, 1)
            al("xds%d" % b_, [128, 512], BF16, 1, 1)
            al("stbf%d" % b_, [128, 512], BF16, 1, 1)
            al("yn%d" % b_, [128, 512], BF16, 1, 1)
        for nm in ("ez", "t1", "t2", "t3", "zz", "yg", "state"):
            al(nm, [128, 512], F32, 1, 1)

        al("qpad", [128, 4, 2, L], BF16, 2, 2)
        al("kT", [128, 4, L], BF16, 2, 2)
        al("v_tm", [128, NT, 512], BF16, 1, 2)
        al("Wq", [128, 8, 512], BF16, 1, 2)
        al("Wk", [128, 8, 512], BF16, 2, 2)
        al("Wv", [128, 8, 512], BF16, 1, 2)
        al("E2", [128, 2, 512], F32, 2, 2)
        for sl_ in range(2):
            al("E%d" % sl_, [128, 2, 512], F32, 2, 2)
            al("u%d" % sl_, [128, 2, 512], F32, 2, 2)
            al("sp%d" % sl_, [128, 2, 512], BF16, 2, 2)
            al("w%d" % sl_, [128, 2, 512], BF16, 2, 2)
        al("osq", [128, 512], F32, 2, 2)
        al("osave", [128, 512], F32, 2, 2)
        al("rs", [128, 512], F32, 2, 2)
        al("Wout", [128, 8, D], BF16, 2, 3)

        al("x1", [128, NT, D], F32, 3, 4)
        al("h2T", [128, 8, L], BF16, 3, 4)
        al("xb0", [128, D], F32, 3, 3)
        al("xb1", [128, D], F32, 3, 3)
        al("hn20", [128, D], BF16, 3, 3)
        al("hn21", [128, D], BF16, 3, 3)
        al("gpm", [128, D], F32, 3, 3)
        al("gpf", [128, D], F32, 3, 3)
        al("junkb", [128, D], BF16, 3, 3)
        al("WdA", [128, 11, D], BF16, 3, 4)
        al("WdB", [128, 11, D], BF16, 4, 4)
        al("GT", [128, NFC, 1024], BF16, 4, 4)
        al("wgu0", [128, 2, 8, 128], BF16, 3, 4)
        al("wgu1", [128, 2, 8, 128], BF16, 3, 4)
        al("sg0", [128, 512], F32, 4, 4)
        al("sg1", [128, 512], F32, 4, 4)
        al("ft0", [128, 512], F32, 4, 4)
        al("ft1", [128, 512], F32, 4, 4)
        al("gff", [128, D], F32, 4, 4)

        K.place()
        T = {k: K.ap(big, v) for k, v in A.items()}
        used = max(a["off"] + a["n"] for a in K.allocs)

        cB = Buf("consts")
        small_slot = DmaSlot(K, "small")
        gpreB = Buf("gpre")
        pB = Buf("params")
        K.dma(SP, DmaSlot(K, "gpre"), T["g_pre"], gains_d[:, 0, :], writes=[gpreB])

        WxB = [Buf("Wx0"), Buf("Wx1")]
        WzB = Buf("Wz")
        wslots = [DmaSlot(K, "w%d" % i) for i in range(4)]
        WdtB = Buf("Wdt")
        K.dma(POOL, wslots[0], T["Wx0"], win_d[:, :, 512:1024], writes=[WxB[0]])
        K.dma(POOL, wslots[3], T["Wdt"], win_d[:, :, 1536:1544], writes=[WdtB])

        onesB = Buf("ones")
        K.do(POOL, lambda: nc.gpsimd.memset(T["onesf"], 1.0), writes=[onesB])
        K.do(POOL, lambda: nc.gpsimd.memset(T["zerob"], 0.0), writes=[cB])
        K.do(POOL, lambda: nc.gpsimd.memset(T["cs0"][:, 0:3], 0.0), writes=[cB])
        K.do(POOL, lambda: nc.gpsimd.memset(T["cs1"][:, 0:3], 0.0), writes=[cB])

        def asel(name, pattern, cmp, base, cm, src="onesf", fill=0.0):
            K.do(POOL, lambda: nc.gpsimd.affine_select(out=T[name], in_=T[src], pattern=pattern, compare_op=cmp,
                                                        fill=fill, base=base, channel_multiplier=cm),
                 reads=[onesB], writes=[cB])

        asel("ident", [[1, 128]], ALU.is_equal, 0, -1)
        asel("UT", [[1, 128]], ALU.is_ge, 0, -1)
        asel("LTs", [[-1, 128]], ALU.is_gt, 0, 1)
        asel("triI", [[-1, 128]], ALU.is_ge, 0, 1)
        asel("triC", [[1, 128]], ALU.is_gt, 0, -1)
        asel("mks", [[1, 128]], ALU.is_gt, 0, -1)
        asel("identb", [[1, 128]], ALU.is_equal, 0, -1)
        asel("negm", [[1, 128]], ALU.is_gt, 0, -1, src="zerob", fill=-30000.0)
        bdB = Buf("bd")
        K.do(POOL, lambda: nc.gpsimd.memset(T["BD"], 0.0), writes=[bdB])
        K.do(POOL, lambda: nc.gpsimd.memset(T["BD"][0:64, 0:64], 1.0 / 64), writes=[bdB])
        K.do(POOL, lambda: nc.gpsimd.memset(T["BD"][64:128, 64:128], 1.0 / 64), writes=[bdB])


        ident = T["ident"]
        hT = T["hT"]
        xin = [T["xin0"], T["xin1"], T["xin2"]]
        hn = hn_a0 = [T["hn0"], T["hn1"]]
        xinB = [Buf("xin0"), Buf("xin1"), Buf("xin2")]
        hnB = hnB_a0 = [Buf("hn0"), Buf("hn1")]
        xslot = [DmaSlot(K, "x0"), DmaSlot(K, "x1"), DmaSlot(K, "x2")]
        stat = T["stat"]
        hTB = [[Buf("hT%d_%d" % (i, h)) for h in range(2)] for i in range(NT)]
        statB = [Buf("stat%d" % i) for i in range(NT)]

        def rms_tile(i, src_ap, srcB, s, gain_ap, gainB, junk_ap, junkB, n=D, hn=None, hnB=None):
            hn = hn if hn is not None else hn_a0
            hnB = hnB if hnB is not None else hnB_a0
            sb = statB[i % NT]
            c = (i % NT)
            K.do(ACT, lambda: nc.scalar.activation(out=junk_ap, in_=src_ap, func=AF.Square,
                                                   accum_out=stat[:, c:c + 1]),
                 reads=[srcB], writes=[junkB, sb])
            K.do(ACT, lambda: nc.scalar.activation(out=stat[:, 16 + c:17 + c], in_=stat[:, c:c + 1], func=AF.Ln,
                                                   bias=EPS, scale=1.0 / n), writes=[sb])
            K.do(ACT, lambda: nc.scalar.activation(out=stat[:, 32 + c:33 + c], in_=stat[:, 16 + c:17 + c],
                                                   func=AF.Exp, scale=-0.5), writes=[sb])
            K.do(DVE, lambda: nc.vector.scalar_tensor_tensor(out=hn[s], in0=src_ap, scalar=stat[:, 32 + c:33 + c],
                                                             in1=gain_ap, op0=ALU.mult, op1=ALU.mult),
                 reads=[srcB, sb, gainB], writes=[hnB[s]])

        def transpose_to(i, s, dstT, dstB, bk0, hn=None, hnB=None, all_act=False, part="both"):
            hn = hn if hn is not None else hn_a0
            hnB = hnB if hnB is not None else hnB_a0
            for hb in range(2):
                bk = bk0 + hb
                bkb = bank[bk].bitcast(BF16)
                if part in ("both", "pe"):
                    K.do(PE, [(lambda c=c: nc.tensor.transpose(bkb[:, (c % 4) * 128:(c % 4 + 1) * 128],
                                                               hn[s][:, c * 128:(c + 1) * 128], T["identb"]))
                              for c in range(4 * hb, 4 * hb + 4)],
                         reads=[hnB[s], cB], writes=[bankB[bk]])
                if part in ("both", "evac"):
                    src = bkb[:, 0:512].rearrange("p (c t) -> p c t", t=128)
                    dst = dstT[:, 4 * hb:4 * hb + 4, i * 128:(i + 1) * 128]
                    if hb == 0 or all_act:
                        K.do(ACT, lambda: nc.scalar.copy(out=dst, in_=src), writes=[bankB[bk], dstB[i][hb]])
                    else:
                        K.do(DVE, lambda: nc.vector.tensor_copy(out=dst, in_=src), writes=[bankB[bk], dstB[i][hb]])

        def A0_load(i):
            s3 = i % 3
            K.dma(SP, xslot[s3], xin[s3], x_d[i * 128:(i + 1) * 128, :], writes=[xinB[s3]])

        def A0_front(i):
            s = i % 2
            s3 = i % 3
            if i + 2 < NT:
                A0_load(i + 2)
            if i == 1:
                for nm, src in (("convp", convp_d[:, :, :]), ("smallp", smallp_d[:, :, :]), ("gssd", gssd_d[:, :]),
                                ("sbg", sbg_d[:, :])):
                    K.dma(SP, small_slot, T[nm], src)
                pB.w = (small_slot.sem, small_slot.cnt)
            rms_tile(i, xin[s3], xinB[s3], s, T["g_pre"], gpreB, hn[s], hnB[s])

        smallp = T["smallp"]
        dtall, adtall, eaT, scx, dtds = T["dtall"], T["adtall"], T["eaT"], T["scx"], T["dtds"]
        UT, LTs, onesf = T["UT"], T["LTs"], T["onesf"]
        Wz, Wdt = T["Wz"], T["Wdt"]
        preB = Buf("pre")

        def dt_proj(i):
            K.do(PE, [(lambda c=c: nc.tensor.matmul(bank[7][:, i * 8:(i + 1) * 8], lhsT=hT[:, c, i * 128:(i + 1) * 128],
                                                    rhs=Wdt[:, c, :], start=(c == 0), stop=(c == 7))) for c in range(8)],
                 reads=[WdtB, hTB[i][0], hTB[i][1]], writes=[bankB[7]])

        def prepass1():
            K.do(DVE, lambda: nc.vector.tensor_tensor(out=dtall, in0=bank[7][:, 0:128], in1=smallp[:, 0, :], op=ALU.add),
                 reads=[cB, pB], writes=[bankB[7], preB])
            K.do(ACT, lambda: nc.scalar.activation(out=dtall, in_=dtall, func=AF.Exp), writes=[preB])
            K.do(ACT, lambda: nc.scalar.activation(out=dtall, in_=dtall, func=AF.Ln, bias=1.0), writes=[preB])
            K.do(ACT, lambda: nc.scalar.activation(out=eaT, in_=smallp[:, 1, :], func=AF.Exp), reads=[cB, pB], writes=[preB])
            K.do(DVE, lambda: nc.vector.scalar_tensor_tensor(out=adtall, in0=dtall, scalar=-1.0, in1=eaT,
                                                             op0=ALU.mult, op1=ALU.mult), writes=[preB])

        convp = T["convp"]
        cs = [T["cs0"], T["cs1"]]
        acc = [T["acc0"], T["acc1"]]
        csB = [[Buf("cs%d_%d" % (a, t)) for t in range(4)] for a in range(2)]
        accB = [[Buf("acc%d_%d" % (a, t)) for t in range(4)] for a in range(2)]
        sil2 = [T["sil0"], T["sil1"]]
        silB2 = [[Buf("sil%d_%d" % (a_, t)) for t in range(4)] for a_ in range(2)]
        Wx = [T["Wx0"], T["Wx1"]]
        xs_tm, bmT, cmT, bm_tm = T["xs_tm"], T["bmT"], T["cmT"], T["bm_tm"]
        xsB = [Buf("xs%d" % i) for i in range(NT)]
        bmTB = [Buf("bmT%d" % g) for g in range(2)]
        cmTB = [Buf("cmT%d" % g) for g in range(2)]
        bmtmB = [Buf("bmtm%d" % i) for i in range(NT)]
        rot = [0, 0]

        def A1_front_t(j, t):
            blk, co, a = j // 4, (j % 4) * 128, j % 2
            bk = 4 + rot[0] % 3
            rot[0] += 1
            K.do(PE, [(lambda c=c: nc.tensor.matmul(bank[bk], lhsT=Wx[blk][:, c, co:co + 128],
                                                    rhs=hT[:, c, t * 512:(t + 1) * 512],
                                                    start=(c == 0), stop=(c == 7))) for c in range(8)],
                 reads=[WxB[blk]] + [hTB[i][h] for i in range(4 * t, 4 * t + 4) for h in range(2)],
                 writes=[bankB[bk]])
            K.do(ACT, [lambda: nc.scalar.copy(out=cs[a][:, 3 + t * 512:3 + (t + 1) * 512], in_=bank[bk]),
                       lambda: nc.scalar.activation(out=acc[a][:, t * 512:(t + 1) * 512], in_=bank[bk],
                                                    func=AF.Identity, scale=convp[:, j, 3:4],
                                                    bias=convp[:, j, 4:5])],
                 reads=[cB, pB], writes=[bankB[bk], csB[a][t], accB[a][t]])

        def A1_tap(j, t, sh):
            a = j % 2
            K.do(DVE, lambda: nc.vector.scalar_tensor_tensor(
                out=acc[a][:, t * 512:(t + 1) * 512],
                in0=cs[a][:, 3 - sh + t * 512:3 - sh + (t + 1) * 512],
                scalar=convp[:, j, 3 - sh:4 - sh],
                in1=acc[a][:, t * 512:(t + 1) * 512], op0=ALU.mult, op1=ALU.add),
                reads=[csB[a][t]] + ([csB[a][t - 1]] if t > 0 else []), writes=[accB[a][t]])

        def A1_front(j):
            for t in range(4):
                A1_front_t(j, t)
            for sh in (1, 2, 3):
                for t in range(4):
                    A1_tap(j, t, sh)

        def A1_silu(j):
            a = j % 2
            sil, silB = sil2[a], silB2[a]
            for t in range(4):
                if j < 4:
                    dst, dB = sil[:, t * 512:(t + 1) * 512], [silB[t]]
                elif j < 6:
                    dst, dB = bmT[:, j - 4, t * 512:(t + 1) * 512], [bmTB[j - 4]]
                else:
                    dst, dB = cmT[:, j - 6, t * 512:(t + 1) * 512], [cmTB[j - 6]]
                K.do(ACT, lambda: nc.scalar.activation(out=dst, in_=acc[a][:, t * 512:(t + 1) * 512], func=AF.Silu),
                     reads=[accB[a][t]], writes=dB)

        def A1_back(j):
            sil, silB = sil2[j % 2], silB2[j % 2]
            if j < 6:
                for q in range(4):
                    bk = rot[1] % 4
                    rot[1] += 1
                    if j >= 4:
                        bkb = bank[bk].bitcast(BF16)
                        K.do(PE, [(lambda r=r: nc.tensor.transpose(bkb[:, r * 128:(r + 1) * 128],
                                                                   bmT[:, j - 4, (4 * q + r) * 128:(4 * q + r + 1) * 128],
                                                                   T["identb"])) for r in range(4)],
                             reads=[bmTB[j - 4], cB], writes=[bankB[bk]])
                        src = bkb[:, 0:512].rearrange("p (r c) -> p r c", c=128)
                    else:
                        K.do(PE, [(lambda r=r: nc.tensor.transpose(bank[bk][:, r * 128:(r + 1) * 128],
                                                                   sil[:, (4 * q + r) * 128:(4 * q + r + 1) * 128],
                                                                   ident)) for r in range(4)],
                             reads=[silB[q], cB], writes=[bankB[bk]])
                        src = bank[bk].rearrange("p (r c) -> p r c", c=128)
                    if j < 4:
                        dst = xs_tm[:, 4 * q:4 * q + 4, j * 128:(j + 1) * 128]
                        dB = [xsB[i] for i in range(4 * q, 4 * q + 4)]
                    else:
                        dst = bm_tm[:, 4 * q:4 * q + 4, (j - 4) * 128:(j - 3) * 128]
                        dB = [bmtmB[i] for i in range(4 * q, 4 * q + 4)]
                    if q % 2 == 0:
                        K.do(DVE, lambda: nc.vector.tensor_copy(out=dst, in_=src), writes=[bankB[bk]] + dB)
                    else:
                        K.do(ACT, lambda: nc.scalar.copy(out=dst, in_=src), writes=[bankB[bk]] + dB)


        def prepass2():
            K.do(PE, [(lambda q=q, m=m: nc.tensor.matmul(bank[1][:, q * 128:(q + 1) * 128], lhsT=m, rhs=adtall,
                                                         start=True, stop=True)) for q, m in enumerate((LTs, onesf, UT))],
                 reads=[preB, cB, onesB], writes=[bankB[1]])
            K.do(ACT, lambda: nc.scalar.activation(out=scx, in_=bank[1][:, 0:384], func=AF.Exp), writes=[bankB[1], preB])
            K.do(DVE, lambda: nc.vector.tensor_tensor(out=dtds, in0=dtall, in1=scx[:, 0:128], op=ALU.mult), writes=[preB])

        A0_load(0)
        A0_load(1)
        A0_front(0)
        for i in range(NT):
            if i + 1 < NT:
                A0_front(i + 1)
            transpose_to(i, i % 2, hT, hTB, 2 * (i % 2))
            if i == 5:
                K.dma(POOL, wslots[1], T["Wx1"], win_d[:, :, 1024:1536], reads=[hTB[i][0]], writes=[WxB[1]])
            if i == 11:
                K.dma(POOL, wslots[2], T["Wz"], win_d[:, :, 0:512], reads=[hTB[i][0]], writes=[WzB])
            if i >= 2:
                dt_proj(i - 2)
            if i % 4 == 3:
                t = i // 4
                for j in (0, 1):
                    A1_front_t(j, t)
                if t >= 1:
                    for j in (0, 1):
                        for sh in (1, 2, 3):
                            A1_tap(j, t - 1, sh)
        for j in (0, 1):
            for sh in (1, 2, 3):
                A1_tap(j, 3, sh)
        dt_proj(NT - 2)
        dt_proj(NT - 1)
        prepass1()
        A1_silu(0)
        A1_silu(1)
        prepass2()
        for j in range(8):
            if j + 2 < 8:
                A1_front(j + 2)
            A1_back(j)
            if j + 2 < 8:
                A1_silu(j + 2)

        ycS = ycA = None
        if dbg != "A1":
            K.barrier(skip=(PE,))
            ycS, ycA = T["ycS"], T["ycA"]
            ycB = [[Buf("yc%d_%d" % (i, h)) for h in range(2)] for i in range(NT)]
            WqB, WkB, WvB = Buf("Wq"), Buf("Wk"), Buf("Wv")
            K.dma(POOL, wslots[2], T["Wv"], win_d[:, :, 2568:3080], writes=[WvB])
            K.dma(POOL, wslots[0], T["Wq"], win_d[:, :, 1544:2056], writes=[WqB])
            vB = [Buf("v%d" % i) for i in range(NT)]

            def vproj_pe(i, bk=7):
                K.do(PE, [(lambda c=c: nc.tensor.matmul(bank[bk], lhsT=hT[:, c, i * 128:(i + 1) * 128], rhs=T["Wv"][:, c, :],
                                                        start=(c == 0), stop=(c == 7))) for c in range(8)],
                     reads=[WvB], writes=[bankB[bk]])

            def vproj_evac(i, bk=7, on_act=True):
                if on_act:
                    K.do(ACT, lambda: nc.scalar.copy(out=T["v_tm"][:, i, :], in_=bank[bk]), writes=[bankB[bk], vB[i]])
                else:
                    K.do(DVE, lambda: nc.vector.tensor_copy(out=T["v_tm"][:, i, :], in_=bank[bk]), writes=[bankB[bk], vB[i]])
            state = T["state"]
            stbf = [T["stbf0"], T["stbf1"]]
            stateB = Buf("state")
            sttmpB = Buf("sttmp")
            stbfB = [Buf("stbf0"), Buf("stbf1")]
            K.do(POOL, lambda: nc.gpsimd.memset(state, 0.0), writes=[stateB])
            K.do(POOL, lambda: nc.gpsimd.memset(stbf[0], 0.0), writes=[stbfB[0]])
            Ah = [T["Ah0"], T["Ah1"]]; dec = [T["dec0"], T["dec1"]]; cbm = [T["cbm0"], T["cbm1"]]
            MT = [T["MT0"], T["MT1"]]; xdt = [T["xdt0"], T["xdt1"]]; xds = [T["xds0"], T["xds1"]]
            yn = [T["yn0"], T["yn1"]]
            AhB = [Buf(), Buf()]; decB = [[Buf(), Buf()], [Buf(), Buf()]]; cbmB = [Buf(), Buf()]
            MTB = [[Buf(), Buf()], [Buf(), Buf()]]; xdtB = [Buf(), Buf()]; xdsB = [Buf(), Buf()]; ynB = [Buf(), Buf()]
            ez, t1, t2, t3, yg = (T[n] for n in ("ez", "t1", "t2", "t3", "yg"))
            zz = [T["zz"], T["zz"]]
            ezB, t1B, t2B, t3B, ygB = (Buf(n) for n in ("ez", "t1", "t2", "t3", "yg"))
            zzB = [Buf("zz")] * 2
            gssd = T["gssd"]
            BZ, BS0, BS1, BYO, BCS = 0, 1, 2, 5, 6
            BCB, BT = BS0, BS1
            BY = [3, 4]

            def xs3_(i):
                return xs_tm[:, i, :].rearrange("p (h q) -> p h q", q=64)

            def fA(i):
                b = i % 2
                tok = slice(i * 128, (i + 1) * 128)
                K.do(PE, [(lambda c=c: nc.tensor.matmul(bank[BZ], lhsT=hT[:, c, tok], rhs=Wz[:, c, :],
                                                        start=(c == 0), stop=(c == 7))) for c in range(8)],
                     reads=[WzB], writes=[bankB[BZ]])
                K.do(ACT, [(lambda h=h: nc.scalar.activation(out=Ah[b][:, h, :], in_=UT, func=AF.Identity,
                                                             scale=adtall[:, i * 8 + h:i * 8 + h + 1])) for h in range(8)],
                     reads=[preB, cB], writes=[AhB[b]])
                for hh in range(2):
                    bs = BS0 + hh
                    K.do(PE, lambda: nc.tensor.matmul(bank[bs], lhsT=LTs,
                                                      rhs=Ah[b][:, 4 * hh:4 * hh + 4, :].rearrange("p h l -> p (h l)"),
                                                      start=True, stop=True),
                         reads=[AhB[b], cB], writes=[bankB[bs]])
                    K.do(ACT, lambda: nc.scalar.activation(out=dec[b][:, 4 * hh:4 * hh + 4, :],
                                                           in_=bank[bs].rearrange("p (h l) -> p h l", l=128), func=AF.Exp),
                         writes=[bankB[bs], decB[b][hh]])
                K.do(PE, [(lambda g=g: nc.tensor.matmul(bank[BCB][:, g * 128:(g + 1) * 128], lhsT=bmT[:, g, tok],
                                                        rhs=cmT[:, g, tok], start=True, stop=True)) for g in range(2)],
                     reads=bmTB + cmTB, writes=[bankB[BCB]])

            def fA_ez(i):
                K.do(ACT, lambda: nc.scalar.activation(out=ez, in_=bank[BZ], func=AF.Exp, scale=-1.0),
                     writes=[bankB[BZ], ezB])
                K.do(ACT, lambda: nc.scalar.activation(out=ez, in_=ez, func=AF.Ln, bias=1.0), writes=[ezB])
                K.do(ACT, lambda: nc.scalar.activation(out=ez, in_=ez, func=AF.Exp, scale=-1.0), writes=[ezB])

            def fB(i):
                b = i % 2
                K.do(DVE, lambda: nc.vector.tensor_tensor(out=cbm[b], in0=bank[BCB][:, 0:256].rearrange("p (g l) -> p g l", l=128),
                                                          in1=UT.unsqueeze(1).to_broadcast([128, 2, 128]), op=ALU.mult),
                     reads=[cB], writes=[bankB[BCB], cbmB[b]])
                for g in range(2):
                    K.do(DVE, lambda: nc.vector.tensor_tensor(out=MT[b][:, 4 * g:4 * g + 4, :], in0=dec[b][:, 4 * g:4 * g + 4, :],
                                                              in1=cbm[b][:, g, :].unsqueeze(1).to_broadcast([128, 4, 128]),
                                                              op=ALU.mult),
                         reads=[decB[b][g], cbmB[b]], writes=[MTB[b][g]])
                K.do(DVE, lambda: nc.vector.tensor_tensor(out=xdt[b].rearrange("p (h q) -> p h q", q=64), in0=xs3_(i),
                                                          in1=dtall[:, i * 8:(i + 1) * 8].unsqueeze(2).to_broadcast([128, 8, 64]),
                                                          op=ALU.mult),
                     reads=[xsB[i], preB], writes=[xdtB[b]])
                K.do(DVE, lambda: nc.vector.tensor_tensor(out=xds[b].rearrange("p (h q) -> p h q", q=64), in0=xs3_(i),
                                                          in1=dtds[:, i * 8:(i + 1) * 8].unsqueeze(2).to_broadcast([128, 8, 64]),
                                                          op=ALU.mult),
                     reads=[xsB[i], preB], writes=[xdsB[b]])
                K.do(PE, [(lambda h=h: nc.tensor.matmul(bank[BY[b]][:, h * 64:(h + 1) * 64], lhsT=MT[b][:, h, :],
                                                        rhs=xdt[b][:, h * 64:(h + 1) * 64], start=True, stop=True))
                          for h in range(8)],
                     reads=[MTB[b][0], MTB[b][1], xdtB[b]], writes=[bankB[BY[b]]])

            def fC(i):
                b = i % 2
                K.do(DVE, lambda: nc.vector.tensor_tensor(out=zz[b], in0=bank[BZ], in1=ez, op=ALU.mult),
                     reads=[ezB], writes=[bankB[BZ], zzB[b]])

            def sP(i):
                b = i % 2
                pp = i % 2
                tok = slice(i * 128, (i + 1) * 128)
                K.do(PE, [(lambda g=g: nc.tensor.matmul(bank[BYO][:, g * 256:(g + 1) * 256], lhsT=cmT[:, g, tok],
                                                        rhs=stbf[pp][:, g * 256:(g + 1) * 256], start=True, stop=True))
                          for g in range(2)],
                     reads=cmTB + [stbfB[pp]], writes=[bankB[BYO]])
                K.do(PE, [(lambda g=g: nc.tensor.matmul(bank[BCS][:, g * 256:(g + 1) * 256],
                                                        lhsT=bm_tm[:, i, g * 128:(g + 1) * 128],
                                                        rhs=xds[b][:, g * 256:(g + 1) * 256], start=True, stop=True))
                          for g in range(2)],
                     reads=[bmtmB[i], xdsB[b]], writes=[bankB[BCS]])

            def st(i):
                pp = i % 2
                K.do(DVE, lambda: nc.vector.tensor_tensor(out=state.rearrange("p (h q) -> p h q", q=64),
                                                          in0=state.rearrange("p (h q) -> p h q", q=64),
                                                          in1=scx[:, 128 + i * 8:128 + (i + 1) * 8].unsqueeze(2).to_broadcast([128, 8, 64]),
                                                          op=ALU.mult),
                     reads=[preB], writes=[stateB])
                K.do(DVE, lambda: nc.vector.tensor_tensor(out=state, in0=bank[BCS], in1=state, op=ALU.add),
                     writes=[bankB[BCS], stateB])
                K.do(ACT, lambda: nc.scalar.copy(out=stbf[1 - pp], in_=state), reads=[stateB], writes=[stbfB[1 - pp]])

            def eA(i):
                b = i % 2
                K.do(DVE, lambda: nc.vector.tensor_tensor(out=t1.rearrange("p (h q) -> p h q", q=64),
                                                          in0=bank[BYO].rearrange("p (h q) -> p h q", q=64),
                                                          in1=scx[:, 256 + i * 8:256 + (i + 1) * 8].unsqueeze(2).to_broadcast([128, 8, 64]),
                                                          op=ALU.mult),
                     reads=[preB], writes=[bankB[BYO], t1B])
                K.do(DVE, lambda: nc.vector.tensor_tensor(out=t2, in0=bank[BY[b]], in1=t1, op=ALU.add),
                     reads=[t1B], writes=[bankB[BY[b]], t2B])
                K.do(DVE, lambda: nc.vector.tensor_tensor(out=t3.rearrange("p (h q) -> p h q", q=64), in0=xs3_(i),
                                                          in1=smallp[:, 2, 0:8].unsqueeze(2).to_broadcast([128, 8, 64]),
                                                          op=ALU.mult),
                     reads=[xsB[i], cB, pB], writes=[t3B])
                K.do(DVE, lambda: nc.vector.tensor_tensor(out=t2, in0=t2, in1=t3, op=ALU.add),
                     reads=[t3B], writes=[t2B])
                K.do(DVE, lambda: nc.vector.tensor_tensor(out=yg, in0=t2, in1=zz[b], op=ALU.mult),
                     reads=[t2B, zzB[b]], writes=[ygB])

            def eS(i):
                sb = statB[i]
                c0 = 64 + 4 * i
                K.do(ACT, [(lambda g=g: nc.scalar.activation(out=t1[:, g * 256:(g + 1) * 256], in_=yg[:, g * 256:(g + 1) * 256],
                                                             func=AF.Square, accum_out=stat[:, c0 + g:c0 + g + 1]))
                           for g in range(2)],
                     reads=[ygB], writes=[t1B, sb])
                K.do(ACT, lambda: nc.scalar.activation(out=stat[:, c0:c0 + 2], in_=stat[:, c0:c0 + 2], func=AF.Ln,
                                                       bias=EPS, scale=1.0 / 256), writes=[sb])
                K.do(ACT, lambda: nc.scalar.activation(out=stat[:, c0:c0 + 2], in_=stat[:, c0:c0 + 2], func=AF.Exp,
                                                       scale=-0.5), writes=[sb])

            def eB(i):
                b = i % 2
                sb = statB[i]
                c0 = 64 + 4 * i
                tok = slice(i * 128, (i + 1) * 128)
                K.do(DVE, [(lambda g=g: nc.vector.scalar_tensor_tensor(out=yn[b][:, g * 256:(g + 1) * 256],
                                                                       in0=yg[:, g * 256:(g + 1) * 256],
                                                                       scalar=stat[:, c0 + g:c0 + g + 1],
                                                                       in1=gssd[:, g * 256:(g + 1) * 256],
                                                                       op0=ALU.mult, op1=ALU.mult)) for g in range(2)],
                     reads=[ygB, sb, cB, pB], writes=[ynB[b]])
                btb = bank[BT].bitcast(BF16)
                K.do(PE, [(lambda c=c: nc.tensor.transpose(btb[:, c * 128:(c + 1) * 128], yn[b][:, c * 128:(c + 1) * 128], T["identb"]))
                          for c in range(4)],
                     reads=[ynB[b], cB], writes=[bankB[BT]])
                K.do(ACT, lambda: nc.scalar.copy(out=ycS[:, :, tok], in_=btb[:, 0:512].rearrange("p (c t) -> p c t", t=128)),
                     writes=[bankB[BT], ycB[i][0]])

            fA(0); fA_ez(0); fB(0); fC(0); sP(0)
            for i in range(NT):
                nx = i + 1 < NT
                if nx:
                    fA(i + 1)
                if i >= 1:
                    vproj_pe(i - 1)
                eA(i)
                eS(i)
                st(i)
                if nx:
                    fA_ez(i + 1)
                if i >= 1:
                    vproj_evac(i - 1)
                if nx:
                    fB(i + 1)
                eB(i)
                if nx:
                    fC(i + 1)
                    sP(i + 1)

        if dbg not in ("A1", "A2"):
            K.barrier(skip=(PE,))
            qpad, kT, v_tm = T["qpad"], T["kT"], T["v_tm"]
            Wq, Wk, Wv, Wout = T["Wq"], T["Wk"], T["Wv"], T["Wout"]
            WoutB = Buf("Wout")
            K.dma(POOL, wslots[1], T["Wk"], win_d[:, :, 2056:2568], writes=[WkB])
            K.dma(POOL, wslots[3], Wout, wout_d[:, :, :], writes=[WoutB])
            qB = [[Buf("q%d_%d" % (j, t)) for t in range(4)] for j in range(4)]
            kB = [[Buf("k%d_%d" % (j, t)) for t in range(4)] for j in range(4)]
            qz = Buf("qzero")
            for j in range(4):
                K.do(POOL, [lambda: nc.gpsimd.memset(qpad[64:128, j, 0, :], 0.0),
                            lambda: nc.gpsimd.memset(qpad[0:64, j, 1, :], 0.0)], writes=[qz])
                for t in range(4):
                    qB[j][t].w = qz.w
            tb = lambda t: slice(t * 512, (t + 1) * 512)
            rot2 = [0]

            def proj_q(j, t, banks, act_ok):
                bk = banks[rot2[0] % len(banks)]; rot2[0] += 1
                K.do(PE, [(lambda c=c: nc.tensor.matmul(bank[bk], lhsT=Wq[:, c, j * 128:(j + 1) * 128], rhs=hT[:, c, tb(t)],
                                                        start=(c == 0), stop=(c == 7))) for c in range(8)],
                     reads=[WqB], writes=[bankB[bk]])
                if act_ok:
                    K.do(ACT, lambda: nc.scalar.activation(out=qpad[0:64, j, 0, tb(t)], in_=bank[bk][0:64, :], func=AF.Copy, scale=0.125),
                         writes=[bankB[bk], qB[j][t]])
                else:
                    K.do(DVE, lambda: nc.vector.tensor_scalar_mul(out=qpad[0:64, j, 0, tb(t)], in0=bank[bk][0:64, :], scalar1=0.125),
                         writes=[bankB[bk], qB[j][t]])
                K.do(DVE, lambda: nc.vector.tensor_scalar_mul(out=qpad[64:128, j, 1, tb(t)], in0=bank[bk][64:128, :], scalar1=0.125),
                     writes=[bankB[bk], qB[j][t]])

            def proj_k(j, t, banks, act_ok):
                bk = banks[rot2[0] % len(banks)]; rot2[0] += 1
                K.do(PE, [(lambda c=c: nc.tensor.matmul(bank[bk], lhsT=Wk[:, c, j * 128:(j + 1) * 128], rhs=hT[:, c, tb(t)],
                                                        start=(c == 0), stop=(c == 7))) for c in range(8)],
                     reads=[WkB], writes=[bankB[bk]])
                if act_ok:
                    K.do(ACT, lambda: nc.scalar.copy(out=kT[:, j, tb(t)], in_=bank[bk]), writes=[bankB[bk], kB[j][t]])
                else:
                    K.do(DVE, lambda: nc.vector.tensor_copy(out=kT[:, j, tb(t)], in_=bank[bk]), writes=[bankB[bk], kB[j][t]])

            allb = list(range(8))
            for t in range(4):
                proj_q(0, t, allb, True)
            for t in range(4):
                proj_k(0, t, allb, True)
            for i in range(NT - 1, NT):
                bk = allb[rot2[0] % 8]; rot2[0] += 1
                vproj_pe(i, bk)
                vproj_evac(i, bk, on_act=False)
            def proj_piece(kind, j, t, c0_, c1_):
                W_, WB_ = (Wq, WqB) if kind == "q" else (Wk, WkB)
                K.do(PE, [(lambda c=c: nc.tensor.matmul(bank[3], lhsT=W_[:, c, j * 128:(j + 1) * 128], rhs=hT[:, c, tb(t)],
                                                        start=(c == 0), stop=(c == 7))) for c in range(c0_, c1_)],
                     reads=[WB_], writes=[bankB[3]])
                if c1_ == 8:
                    if kind == "q":
                        K.do(DVE, lambda: nc.vector.tensor_scalar_mul(out=qpad[0:64, j, 0, tb(t)], in0=bank[3][0:64, :], scalar1=0.125),
                             writes=[bankB[3], qB[j][t]])
                        K.do(DVE, lambda: nc.vector.tensor_scalar_mul(out=qpad[64:128, j, 1, tb(t)], in0=bank[3][64:128, :], scalar1=0.125),
                             writes=[bankB[3], qB[j][t]])
                    else:
                        K.do(DVE, lambda: nc.vector.tensor_copy(out=kT[:, j, tb(t)], in_=bank[3]), writes=[bankB[3], kB[j][t]])

            pending = {j: [(kind, j, t, c, c + 2) for kind in ("q", "k") for t in range(4) for c in range(0, 8, 2)]
                       for j in (1, 2, 3)}

            triI, triC, mks, zerob, BD, sbg = T["triI"], T["triC"], T["mks"], T["zerob"], T["BD"], T["sbg"]
            Et = [T["E0"], T["E1"], T["E2"]]
            ut = [T["u0"], T["u1"]]
            spt = [T["sp0"], T["sp1"]]
            wt = [T["w0"], T["w1"]]
            EB = [Buf(), Buf(), Buf()]
            uB = [Buf(), Buf()]
            spB = [Buf(), Buf()]
            wB = [Buf(), Buf()]
            osq, rs = T["osq"], T["rs"]
            osqB, rsB = Buf("osq"), Buf("rs")
            ycB2 = [[Buf() for _ in range(4)] for _ in range(4)]
            ps3 = lambda b0, c0: psum[:, b0 * 512:(b0 + 2) * 512].rearrange("p (h c) -> p h c", c=512)[:, :, c0:512]
            Z0, NB, RB0, OB0 = 0, 2, 4, 6
            steps = []
            for j in range(4):
                for Tq in range(4):
                    nb = 4 * Tq + 4
                    for idx in range(nb):
                        b = nb - 1 - idx
                        joff = b - 4 * Tq
                        steps.append(dict(j=j, Tq=Tq, b=b, idx=idx, nb=nb, diag=(joff >= 0),
                                          c0=(128 * joff if joff > 0 else 0), first=(idx == 0), last=(idx == nb - 1)))
            NS = len(steps)

            def e_Z(n):
                st_ = steps[n]; j, Tq, b, c0, diag = st_["j"], st_["Tq"], st_["b"], st_["c0"], st_["diag"]
                th = []
                for h in range(2):
                    th.append(lambda h=h: nc.tensor.matmul(bank[Z0 + h][:, c0:512], lhsT=kT[:, j, b * 128:(b + 1) * 128],
                                                           rhs=qpad[:, j, h, Tq * 512 + c0:(Tq + 1) * 512],
                                                           start=True, stop=not diag))
                    if diag:
                        th.append(lambda h=h: nc.tensor.matmul(bank[Z0 + h][:, c0:c0 + 128], lhsT=T["identb"], rhs=T["negm"],
                                                               start=False, stop=True))
                K.do(PE, th, reads=[kB[j][b // 4], qB[j][Tq], cB], writes=[bankB[Z0], bankB[Z0 + 1]])

            def e_E(n):
                c0 = steps[n]["c0"]
                K.do(ACT, lambda: nc.scalar.activation(out=Et[n % 3][:, :, c0:512], in_=ps3(Z0, c0), func=AF.Exp),
                     writes=[bankB[Z0], bankB[Z0 + 1], EB[n % 3]])

            def e_sp(n):
                c0, diag, sl = steps[n]["c0"], steps[n]["diag"], n % 2
                K.do(ACT, lambda: nc.scalar.activation(out=spt[sl][:, :, c0:512], in_=Et[n % 3][:, :, c0:512], func=AF.Ln, bias=1.0),
                     reads=[EB[n % 3]], writes=[spB[sl]])

            def e_init(n, which):
                st_ = steps[n]; j, Tq = st_["j"], st_["Tq"]
                b0 = RB0 if which == "R" else OB0
                K.do(PE, [(lambda bk=bk: nc.tensor.matmul(bank[bk], lhsT=zerob, rhs=qpad[:, j, 0, Tq * 512:(Tq + 1) * 512],
                                                          start=True, stop=False, skip_group_check=True)) for bk in (b0, b0 + 1)],
                     reads=[cB, qB[j][Tq]], writes=[bankB[b0], bankB[b0 + 1]])

            def e_tri(n, tri):
                c0, sl = steps[n]["c0"], n % 2
                K.do(PE, [(lambda h=h: nc.tensor.matmul(bank[RB0 + h][:, c0:512], lhsT=tri, rhs=spt[sl][:, h, c0:512],
                                                        start=False, stop=False, skip_group_check=True)) for h in range(2)],
                     reads=[spB[sl], cB], writes=[bankB[RB0], bankB[RB0 + 1]])

            def e_u_w(n):
                c0, diag, sl = steps[n]["c0"], steps[n]["diag"], n % 2
                K.do(ACT, lambda: nc.scalar.activation(out=ut[sl][:, :, c0:512], in_=ps3(RB0, c0), func=AF.Exp, scale=-1.0),
                     writes=[bankB[RB0], bankB[RB0 + 1], uB[sl]])
                K.do(DVE, lambda: nc.vector.tensor_tensor(out=wt[sl][:, :, c0:512], in0=Et[n % 3][:, :, c0:512],
                                                          in1=ut[sl][:, :, c0:512], op=ALU.mult),
                     reads=[EB[n % 3], uB[sl]], writes=[wB[sl]])

            def e_WV(n):
                st_ = steps[n]; j, b, c0, sl = st_["j"], st_["b"], st_["c0"], n % 2
                K.do(PE, [(lambda h=h: nc.tensor.matmul(bank[OB0 + h][:, c0:512], lhsT=v_tm[:, b, j * 128:(j + 1) * 128],
                                                        rhs=wt[sl][:, h, c0:512], start=False, stop=st_["last"],
                                                        skip_group_check=True)) for h in range(2)],
                     reads=[wB[sl], vB[b]], writes=[bankB[OB0], bankB[OB0 + 1]])

            osave = T["osave"]
            osaveB = Buf("osave")

            def e_epiA(n):
                K.do(DVE, lambda: nc.vector.tensor_copy(out=osave[0:64, :], in_=bank[OB0][0:64, :]),
                     writes=[bankB[OB0], osaveB])
                K.do(DVE, lambda: nc.vector.tensor_copy(out=osave[64:128, :], in_=bank[OB0 + 1][64:128, :]),
                     writes=[bankB[OB0 + 1], osaveB])
                K.do(DVE, lambda: nc.vector.tensor_tensor(out=osq, in0=osave, in1=osave, op=ALU.mult),
                     reads=[osaveB], writes=[osqB])

            def e_epiA2(n):
                K.do(PE, lambda: nc.tensor.matmul(bank[NB], lhsT=BD, rhs=osq, start=True, stop=True),
                     reads=[osqB, bdB], writes=[bankB[NB]])

            def e_epiB(n):
                st_ = steps[n]; j, Tq = st_["j"], st_["Tq"]
                qall = slice(Tq * 512, (Tq + 1) * 512)
                K.do(ACT, lambda: nc.scalar.activation(out=rs, in_=bank[NB], func=AF.Ln, bias=EPS), writes=[bankB[NB], rsB])
                K.do(ACT, lambda: nc.scalar.activation(out=rs, in_=rs, func=AF.Exp, scale=-0.5), writes=[rsB])
                for h in range(2):
                    pr = slice(64 * h, 64 * h + 64)
                    K.do(DVE, lambda: nc.vector.scalar_tensor_tensor(out=ycA[pr, j, qall], in0=osave[pr, :],
                                                                     scalar=sbg[pr, j:j + 1], in1=rs[pr, :],
                                                                     op0=ALU.mult, op1=ALU.mult),
                         reads=[rsB, cB, pB, osaveB], writes=[ycB2[j][Tq]])

            e_Z(0); e_E(0); e_Z(1); e_sp(0)
            for n in range(NS):
                st_ = steps[n]
                if st_["first"]:
                    e_init(n, "R")
                e_tri(n, triI)
                if n + 1 < NS:
                    e_E(n + 1)
                if n + 2 < NS:
                    e_Z(n + 2)
                e_u_w(n)
                if n + 1 < NS:
                    e_sp(n + 1)
                if not st_["last"]:
                    e_tri(n, triC)
                if st_["idx"] == 1:
                    e_init(n - 1, "O")
                if n >= 1:
                    e_WV(n - 1)
                    if steps[n - 1]["last"]:
                        e_epiA(n - 1)
                if st_["idx"] == 1 and n >= 2:
                    e_epiA2(n - 2)
                if st_["idx"] == 2 and n >= 3:
                    e_epiB(n - 3)
                nj = st_["j"] + 1
                if nj in pending and pending[nj]:
                    proj_piece(*pending[nj].pop(0))
            e_WV(NS - 1)
            e_epiA(NS - 1)
            e_epiA2(NS - 1)
            e_epiB(NS - 1)

        if dbg not in ("A1", "A2", "A4"):
            K.barrier(skip=(PE,))
            x1, h2T = T["x1"], T["h2T"]
            xb = [T["xb0"], T["xb1"]]
            hn2 = [T["hn20"], T["hn21"]]
            gpm, gpf = T["gpm"], T["gpf"]
            xbB = [Buf(), Buf()]
            hn2B = [Buf(), Buf()]
            gB3 = Buf("gains3")
            x1B = [Buf("x1_%d" % i) for i in range(NT)]
            h2TB = [[Buf() for _ in range(2)] for _ in range(NT)]
            K.dma(SP, small_slot, gpm, gains_d[:, 1, :])
            K.dma(SP, small_slot, gpf, gains_d[:, 2, :])
            gB3.w = (small_slot.sem, small_slot.cnt)
            wgu = [T["wgu0"], T["wgu1"]]
            wguB = [Buf(), Buf()]
            for sl_ in range(2):
                K.dma(POOL, wslots[sl_], wgu[sl_].rearrange("p a b c -> p (a b c)"), wgu_d[sl_, :, :], writes=[wguB[sl_]])
            WdAB = [Buf() for _ in range(3)]
            wdaslots = [DmaSlot(K, "wda%d" % q) for q in range(3)]
            for q, (lo, hi) in enumerate(((0, 4), (4, 8), (8, 11))):
                K.dma(POOL, wdaslots[q], T["WdA"][:, lo:hi, :], wd_d[:, lo:hi, :], writes=[WdAB[q]])
            junk = T["junkb"]
            junkB = Buf("junk")
            ps2 = lambda b0_: psum[:, b0_ * 512:(b0_ + 2) * 512]

            def B_P1(i):
                s = i % 2
                m = 2 * (i % 3)
                tok = slice(i * 128, (i + 1) * 128)
                K.dma(SP, xslot[s], xb[s], x_d[tok, :], writes=[xbB[s]])
                for hh in range(2):
                    bk = m + hh
                    K.do(PE, [(lambda c=c: nc.tensor.matmul(bank[bk], lhsT=(ycS if c < 4 else ycA)[:, c % 4, tok], rhs=Wout[:, c, hh * 512:(hh + 1) * 512],
                                                            start=(c == 0), stop=(c == 7))) for c in range(8)],
                         reads=[WoutB, ycB[i][0], ycB2[0][i // 4], ycB2[1][i // 4], ycB2[2][i // 4], ycB2[3][i // 4]],
                         writes=[bankB[bk]])

            def B_A12(i):
                m = 2 * (i % 3)
                c0 = 128 + 4 * i
                sb = statB[i]
                K.do(ACT, lambda: nc.scalar.activation(out=junk, in_=ps2(m), func=AF.Square, accum_out=stat[:, c0:c0 + 1]),
                     writes=[bankB[m], bankB[m + 1], junkB, sb])
                K.do(ACT, lambda: nc.scalar.activation(out=stat[:, c0:c0 + 1], in_=stat[:, c0:c0 + 1], func=AF.Ln,
                                                       bias=EPS, scale=1.0 / D), writes=[sb])
                K.do(ACT, lambda: nc.scalar.activation(out=stat[:, c0:c0 + 1], in_=stat[:, c0:c0 + 1], func=AF.Exp,
                                                       scale=-0.5), writes=[sb])

            def B_D1(i):
                s = i % 2
                m = 2 * (i % 3)
                c0 = 128 + 4 * i
                sb = statB[i]
                K.do(DVE, lambda: nc.vector.scalar_tensor_tensor(out=x1[:, i, :], in0=ps2(m), scalar=stat[:, c0:c0 + 1],
                                                                 in1=gpm, op0=ALU.mult, op1=ALU.mult),
                     reads=[sb, gB3], writes=[bankB[m], bankB[m + 1], x1B[i]])
                K.do(DVE, lambda: nc.vector.tensor_tensor(out=x1[:, i, :], in0=x1[:, i, :], in1=xb[s], op=ALU.add),
                     reads=[xbB[s]], writes=[x1B[i]])

            def B_A34(i):
                c1 = 128 + 4 * i + 1
                sb = statB[i]
                K.do(ACT, lambda: nc.scalar.activation(out=junk, in_=x1[:, i, :], func=AF.Square, accum_out=stat[:, c1:c1 + 1]),
                     reads=[x1B[i]], writes=[junkB, sb])
                K.do(ACT, lambda: nc.scalar.activation(out=stat[:, c1:c1 + 1], in_=stat[:, c1:c1 + 1], func=AF.Ln,
                                                       bias=EPS, scale=1.0 / D), writes=[sb])
                K.do(ACT, lambda: nc.scalar.activation(out=stat[:, c1:c1 + 1], in_=stat[:, c1:c1 + 1], func=AF.Exp,
                                                       scale=-0.5), writes=[sb])

            def B_D2(i):
                s = i % 2
                c1 = 128 + 4 * i + 1
                K.do(DVE, lambda: nc.vector.scalar_tensor_tensor(out=hn2[s], in0=x1[:, i, :], scalar=stat[:, c1:c1 + 1],
                                                                 in1=gpf, op0=ALU.mult, op1=ALU.mult),
                     reads=[x1B[i], statB[i], gB3], writes=[hn2B[s]])

            for k in range(NT + 3):
                if 0 <= k - 1 < NT:
                    B_A12(k - 1)
                    B_D1(k - 1)
                if 0 <= k - 2 < NT:
                    B_A34(k - 2)
                    B_D2(k - 2)
                if k < NT:
                    B_P1(k)
                if 0 <= k - 3 < NT:
                    transpose_to(k - 3, (k - 3) % 2, h2T, h2TB, 6, hn=hn2, hnB=hn2B, all_act=False, part="pe")
                    transpose_to(k - 3, (k - 3) % 2, h2T, h2TB, 6, hn=hn2, hnB=hn2B, all_act=False, part="evac")

        if dbg is None or dbg == "F":
            K.barrier(skip=(PE,))
            GT, gff = T["GT"], T["gff"]
            Wd_ = lambda f_: (T["WdA"][:, f_, :] if f_ < 11 else T["WdB"][:, f_ - 11, :])
            sg = [T["sg0"], T["sg1"]]
            ft = [T["ft0"], T["ft1"]]
            sgB = [Buf(), Buf()]
            ftB = [Buf(), Buf()]
            WdB = [Buf() for _ in range(11)]
            GTB = [Buf() for _ in range(NFC)]
            gB4 = Buf("gains4")
            K.dma(SP, small_slot, gff, gains_d[:, 3, :], writes=[gB4])
            wdslots = [DmaSlot(K, "wd%d" % q) for q in range(11)]
            wdparts = [(2 * q, 2 * q + 2) for q in range(11)]
            oslot = [DmaSlot(K, "o0"), DmaSlot(K, "o1")]
            seq = [(hf, fc) for hf in range(2) for fc in range(NFC)]

            def load_wgu(k):
                hf, fc = seq[k]
                sl = k % 2
                K.dma(POOL, wslots[sl], wgu[sl].rearrange("p a b c -> p (a b c)"), wgu_d[fc, :, :], writes=[wguB[sl]])

            rotc = 0
            for k, (hf, fc) in enumerate(seq):
                sl = k % 2
                if hf == 0 and fc < 11:
                    K.dma(POOL, wdslots[fc], T["WdB"][:, fc:fc + 1, :], wd_d[:, 11 + fc:12 + fc, :], writes=[WdB[fc]])
                for t in range(2):
                    tcols = slice(hf * 1024 + t * 512, hf * 1024 + (t + 1) * 512)
                    bg = 2 * (rotc % 2)
                    bu = bg + 1
                    ss_ = rotc % 2
                    rotc += 1
                    rd = [wguB[sl]] + [h2TB[i][h] for i in range(hf * 8 + t * 4, hf * 8 + t * 4 + 4) for h in range(2)]
                    K.do(PE, [(lambda c=c: nc.tensor.matmul(bank[bg], lhsT=wgu[sl][:, 0, c, :], rhs=h2T[:, c, tcols],
                                                            start=(c == 0), stop=(c == 7))) for c in range(8)],
                         reads=rd, writes=[bankB[bg]])
                    K.do(PE, [(lambda c=c: nc.tensor.matmul(bank[bu], lhsT=wgu[sl][:, 1, c, :], rhs=h2T[:, c, tcols],
                                                            start=(c == 0), stop=(c == 7))) for c in range(8)],
                         reads=rd, writes=[bankB[bu]])
                    K.do(ACT, lambda: nc.scalar.activation(out=sg[ss_], in_=bank[bg], func=AF.Silu),
                         writes=[bankB[bg], sgB[ss_]])
                    K.do(DVE, lambda: nc.vector.tensor_tensor(out=GT[:, fc, t * 512:(t + 1) * 512], in0=bank[bu], in1=sg[ss_],
                                                              op=ALU.mult),
                         reads=[sgB[ss_]], writes=[bankB[bu], GTB[fc]])
                if k + 2 < len(seq):
                    load_wgu(k + 2)
                if fc == NFC - 1:
                    for il in range(8):
                        i = hf * 8 + il
                        s = il % 2
                        sb = statB[i]
                        c0 = 192 + 4 * i
                        for hh in range(2):
                            bk = 4 + 2 * s + hh
                            K.do(PE, [(lambda f_=f_: nc.tensor.matmul(bank[bk], lhsT=GT[:, f_, il * 128:(il + 1) * 128],
                                                                      rhs=Wd_(f_)[:, hh * 512:(hh + 1) * 512],
                                                                      start=(f_ == 0), stop=(f_ == NFC - 1))) for f_ in range(NFC)],
                                 reads=GTB + WdB + WdAB, writes=[bankB[bk]])
                            K.do(ACT, lambda: nc.scalar.activation(out=ft[hh], in_=bank[bk], func=AF.Square,
                                                                   accum_out=stat[:, c0 + hh:c0 + hh + 1]),
                                 writes=[bankB[bk], ftB[hh], sb])
                        K.do(DVE, lambda: nc.vector.tensor_tensor(out=stat[:, c0 + 2:c0 + 3], in0=stat[:, c0:c0 + 1],
                                                                  in1=stat[:, c0 + 1:c0 + 2], op=ALU.add), writes=[sb])
                        K.do(ACT, lambda: nc.scalar.activation(out=stat[:, c0 + 2:c0 + 3], in_=stat[:, c0 + 2:c0 + 3], func=AF.Ln,
                                                               bias=EPS, scale=1.0 / D), writes=[sb])
                        K.do(ACT, lambda: nc.scalar.activation(out=stat[:, c0 + 2:c0 + 3], in_=stat[:, c0 + 2:c0 + 3], func=AF.Exp,
                                                               scale=-0.5), writes=[sb])
                        for hh in range(2):
                            bk = 4 + 2 * s + hh
                            K.do(DVE, lambda: nc.vector.scalar_tensor_tensor(out=ft[hh], in0=bank[bk], scalar=stat[:, c0 + 2:c0 + 3],
                                                                             in1=gff[:, hh * 512:(hh + 1) * 512],
                                                                             op0=ALU.mult, op1=ALU.mult),
                                 reads=[sb, gB4], writes=[bankB[bk], ftB[hh]])
                            K.do(POOL, lambda: nc.gpsimd.tensor_tensor(out=x1[:, i, hh * 512:(hh + 1) * 512],
                                                                       in0=x1[:, i, hh * 512:(hh + 1) * 512], in1=ft[hh], op=ALU.add),
                                 reads=[ftB[hh]], writes=[x1B[i]])
                        K.dma(SP, oslot[s], out_d[i * 128:(i + 1) * 128, :], x1[:, i, :], reads=[x1B[i]])
            K.barrier()
            for s in range(2):
                SP.wait((oslot[s].sem, oslot[s].cnt))

        fin_slot = DmaSlot(K, "fin")

        def dump(name, ap2d):
            K.dma(SP, fin_slot, dbg_outs[name][:, :], ap2d)

        K.barrier()
        if dbg == "A1":
            dump("d_hT", hT.rearrange("p a b -> p (a b)"))
            dump("d_xs", xs_tm.rearrange("p a b -> p (a b)"))
            dump("d_bmT", bmT.rearrange("p a b -> p (a b)"))
            dump("d_cmT", cmT.rearrange("p a b -> p (a b)"))
            dump("d_bmtm", bm_tm.rearrange("p a b -> p (a b)"))
        if dbg in ("A2", "A4"):
            K.dma(SP, fin_slot, dbg_outs["d_ycat"][:, 0:4 * L], ycS.rearrange("p a b -> p (a b)"))
            if dbg == "A4":
                K.dma(SP, fin_slot, dbg_outs["d_ycat"][:, 4 * L:8 * L], ycA.rearrange("p a b -> p (a b)"))
        if dbg == "B":
            dump("d_x1", x1.rearrange("p a b -> p (a b)"))
            dump("d_h2T", h2T.rearrange("p a b -> p (a b)"))
        if dbg is not None and dbg != "F":
            K.dma(SP, fin_slot, out_d[0:128, :], T["ident"].bitcast(F32)[:, 0:128].to_broadcast([128, 128]) if False else big[:, 0:4096].bitcast(F32))
        if fin_slot.cnt:
            SP.wait((fin_slot.sem, fin_slot.cnt))
    return nc, used


def _layout(inputs):
    f = np.float32
    w_in = np.asarray(inputs["w_in"], f)[0]
    w_out = np.asarray(inputs["w_out"], f)[0]
    wg = np.asarray(inputs["w_gate"], f)[0]
    wu = np.asarray(inputs["w_up"], f)[0]
    wd = np.asarray(inputs["w_down"], f)[0]
    m = {}
    m["w_in"] = np.ascontiguousarray(w_in.reshape(8, 128, DIN).transpose(1, 0, 2))
    m["w_out"] = np.ascontiguousarray(w_out.reshape(8, 128, D).transpose(1, 0, 2))
    g4 = wg.reshape(8, 128, NFC, 128).transpose(2, 1, 0, 3)
    u4 = wu.reshape(8, 128, NFC, 128).transpose(2, 1, 0, 3)
    m["wgu"] = np.ascontiguousarray(np.stack([g4, u4], axis=2).reshape(NFC, 128, 2 * 8 * 128))
    m["wd"] = np.ascontiguousarray(wd.reshape(NFC, 128, D).transpose(1, 0, 2))
    gains = np.stack([np.asarray(inputs[k], f)[0] for k in
                      ("pre_mix_gain", "post_mix_gain", "pre_ffn_gain", "post_ffn_gain")], axis=0)
    m["gains"] = np.ascontiguousarray(np.broadcast_to(gains[None], (128, 4, D)))
    cw = np.asarray(inputs["conv_w"], f)[0]
    cb = np.asarray(inputs["conv_b"], f)[0]
    cp = np.concatenate([cw, cb[None]], axis=0)
    m["convp"] = np.ascontiguousarray(cp.reshape(5, 8, 128).transpose(2, 1, 0))
    sp = np.stack([np.tile(np.asarray(inputs[k], f)[0], 16) for k in ("dt_bias", "a_log", "d_skip")], axis=0)
    m["smallp"] = np.ascontiguousarray(np.broadcast_to(sp[None], (128, 3, 128)))
    m["gssd"] = np.ascontiguousarray(np.broadcast_to(np.asarray(inputs["ssd_norm_gain"], f)[0][None], (128, 512)))
    m["sbg"] = np.ascontiguousarray(np.asarray(inputs["sb_norm_gain"], f)[0].reshape(4, 128).T)
    return m


_CACHE = {}


def kernel(**inputs):
    x = np.asarray(inputs["x"], np.float32)
    B = x.shape[0]
    if "nc" not in _CACHE:
        _CACHE["nc"] = build()[0]
    nc = _CACHE["nc"]
    shared = _layout(inputs)
    in_maps = []
    for b in range(B):
        m = dict(shared)
        m["x"] = np.ascontiguousarray(x[b])
        in_maps.append(m)
    res = run_bass_kernel_spmd(nc, in_maps, core_ids=list(range(B)))
    return np.stack([np.asarray(r["out"], np.float32) for r in res.results], axis=0)
```
